# Optimizing a Trainium2 kernel written in Bass

```python
import jax, jax.numpy as jnp
from jax import lax
import numpy as np


D_MODEL = 2048
BATCH = 4
SEQ = 4096
DEPTH = 4

D_MIX = D_MODEL
CHUNK = 128
SGU_WIDTH = D_MIX // 4
SGU_GROUPS = 4
SGU_GROUP_DIM = SGU_WIDTH // SGU_GROUPS
HGRN_WIDTH = D_MIX // 4
HGRN_HEADS = 4
HGRN_HEAD_DIM = HGRN_WIDTH // HGRN_HEADS
MLA_HEADS = 8
MLA_V_DIM = (D_MIX - SGU_WIDTH - HGRN_WIDTH) // MLA_HEADS
MLA_NOPE_DIM = 128
MLA_ROPE_DIM = 64
MLA_QK_DIM = MLA_NOPE_DIM + MLA_ROPE_DIM
Q_LORA_RANK = D_MODEL // 4
KV_LORA_RANK = D_MODEL // 4
ROPE_THETA = 10000.0
ATTN_BLOCK = 128
N_GROUPS = 4
EXPERTS_PER_GROUP = 8
N_EXPERTS = N_GROUPS * EXPERTS_PER_GROUP
TOP_K = 2
D_EXPERT = D_MODEL // 4
MOE_BLOCK = 128
NORM_EPS = 1e-5
DEEPNORM_ALPHA = (2 * DEPTH) ** 0.25
DEEPNORM_BETA = (8 * DEPTH) ** -0.25
IN_SPLITS = (SGU_WIDTH, SGU_WIDTH, HGRN_WIDTH, HGRN_WIDTH, HGRN_WIDTH, HGRN_WIDTH, Q_LORA_RANK, KV_LORA_RANK, MLA_ROPE_DIM)
D_IN = SGU_WIDTH * 2 + HGRN_WIDTH * 4 + Q_LORA_RANK + KV_LORA_RANK + MLA_ROPE_DIM

kernel_name = 'hybrid_sgu_hgrn2_mla_hmoe_deepnorm'


def _split_points():
    pts, acc = [], 0
    for w in IN_SPLITS[:-1]:
        acc += w
        pts.append(acc)
    return pts


def layer_norm(x, g, b):
    xf = x.astype(jnp.float32)
    mu = jnp.mean(xf, axis=-1, keepdims=True)
    var = jnp.mean(jnp.square(xf - mu), axis=-1, keepdims=True)
    return ((xf - mu) * lax.rsqrt(var + NORM_EPS) * g.astype(jnp.float32) + b.astype(jnp.float32)).astype(x.dtype)


def rms_norm(x, g):
    xf = x.astype(jnp.float32)
    ms = jnp.mean(jnp.square(xf), axis=-1, keepdims=True)
    return (xf * lax.rsqrt(ms + NORM_EPS) * g.astype(jnp.float32)).astype(x.dtype)


def rope_tables(positions):
    inv = 1.0 / (ROPE_THETA ** (jnp.arange(0, MLA_ROPE_DIM, 2, dtype=jnp.float32) / MLA_ROPE_DIM))
    ang = positions.astype(jnp.float32)[..., None] * inv
    return jnp.cos(ang), jnp.sin(ang)


def apply_rope(x, cos, sin):
    x1, x2 = jnp.split(x.astype(jnp.float32), 2, axis=-1)
    return jnp.concatenate([x1 * cos - x2 * sin, x2 * cos + x1 * sin], axis=-1).astype(x.dtype)


def chunked_spatial_gating(u, v, ln_g, ln_b, ws, bs):
    B, S, _ = u.shape
    nc = S // CHUNK
    v = layer_norm(v, ln_g, ln_b).reshape(B, nc, CHUNK, SGU_GROUPS, SGU_GROUP_DIM)
    causal = jnp.tril(jnp.ones((CHUNK, CHUNK), dtype=bool))
    ws = jnp.where(causal[None], ws, jnp.zeros_like(ws))
    mixed = jnp.einsum('gts,bcsgd->bctgd', ws, v) + bs.T[:, :, None]
    return u * mixed.reshape(B, S, SGU_WIDTH).astype(u.dtype)


def _to_chunks(t):
    B, S, H, D = t.shape
    return t.reshape(B, S // CHUNK, CHUNK, H, D).transpose(1, 0, 3, 2, 4)


def gated_chunk_recurrence(q, k, v, log_f):
    B, S, H, K = q.shape
    V = v.shape[-1]
    causal = jnp.tril(jnp.ones((CHUNK, CHUNK), dtype=bool))

    def step(state, inp):
        qc, kc, vc, gc = inp
        G = jnp.cumsum(gc, axis=2)
        diff = G[:, :, :, None, :] - G[:, :, None, :, :]
        decay = jnp.exp(jnp.where(causal[None, None, :, :, None], diff, -jnp.inf))
        A = jnp.einsum('bhtk,bhtsk,bhsk->bhts', qc, decay, kc)
        o = jnp.einsum('bhts,bhsv->bhtv', A, vc) + jnp.einsum('bhtk,bhkv->bhtv', qc * jnp.exp(G), state)
        G_last = G[:, :, -1:, :]
        new_state = jnp.exp(G_last[:, :, 0, :])[..., None] * state + jnp.einsum('bhsk,bhsv->bhkv', kc * jnp.exp(G_last - G), vc)
        return new_state, o

    init = jnp.zeros((B, H, K, V), jnp.float32)
    _, o = lax.scan(step, init, (_to_chunks(q), _to_chunks(k), _to_chunks(v), _to_chunks(log_f)))
    return o.transpose(1, 0, 3, 2, 4).reshape(B, S, H, V)


def hgrn2_mixer(q_in, f_in, i_in, g_in, lb, norm_g):
    B, S, _ = q_in.shape
    H, K = HGRN_HEADS, HGRN_HEAD_DIM
    q = jax.nn.silu(q_in.astype(jnp.float32))
    fx = f_in.astype(jnp.float32)
    lb = lb.astype(jnp.float32)
    log_f = jnp.logaddexp(jnp.log(lb), jnp.log1p(-lb) + jax.nn.log_sigmoid(fx))
    k = (1.0 - lb) * jax.nn.sigmoid(-fx)
    v = i_in.astype(jnp.float32)
    heads = lambda t: t.reshape(B, S, H, K)
    o = gated_chunk_recurrence(heads(q), heads(k), heads(v), heads(log_f))
    o = rms_norm(o, norm_g.reshape(H, K)).reshape(B, S, HGRN_WIDTH)
    return (o * jax.nn.silu(g_in.astype(jnp.float32))).astype(q_in.dtype)


def causal_block_attention(q, k, v):
    B, S, H, DQK = q.shape
    nq = S // ATTN_BLOCK
    scale = DQK ** -0.5
    qb = q.reshape(B, nq, ATTN_BLOCK, H, DQK).transpose(1, 0, 2, 3, 4)
    key_pos = jnp.arange(S)

    def one_block(args):
        q_blk, blk = args
        s = jnp.einsum('bqhd,bkhd->bhqk', q_blk, k).astype(jnp.float32) * scale
        q_pos = blk * ATTN_BLOCK + jnp.arange(ATTN_BLOCK)
        s = jnp.where(key_pos[None, :] <= q_pos[:, None], s, -jnp.inf)
        p = jax.nn.softmax(s, axis=-1).astype(v.dtype)
        return jnp.einsum('bhqk,bkhd->bqhd', p, v)

    o = lax.map(one_block, (qb, jnp.arange(nq)))
    return o.transpose(1, 0, 2, 3, 4).reshape(B, S, H, v.shape[-1])


def mla_mixer(c_q, c_kv, k_rope, qn_g, w_uq, kvn_g, w_ukv, cos, sin):
    B, S, _ = c_q.shape
    q = (rms_norm(c_q, qn_g) @ w_uq).reshape(B, S, MLA_HEADS, MLA_QK_DIM)
    q_nope, q_rope = q[..., :MLA_NOPE_DIM], q[..., MLA_NOPE_DIM:]
    q_rope = apply_rope(q_rope, cos[:, :, None, :], sin[:, :, None, :])
    kv = (rms_norm(c_kv, kvn_g) @ w_ukv).reshape(B, S, MLA_HEADS, MLA_NOPE_DIM + MLA_V_DIM)
    k_nope, v = kv[..., :MLA_NOPE_DIM], kv[..., MLA_NOPE_DIM:]
    k_rope = apply_rope(k_rope, cos, sin)
    q = jnp.concatenate([q_nope, q_rope], axis=-1)
    k = jnp.concatenate([k_nope, jnp.broadcast_to(k_rope[:, :, None, :], (B, S, MLA_HEADS, MLA_ROPE_DIM))], axis=-1)
    o = causal_block_attention(q, k, v)
    return o.reshape(B, S, MLA_HEADS * MLA_V_DIM)


def hierarchical_moe(x, wg_r, bg_r, we_r, be_r, w_gate, w_up, w_down):
    B, S, D = x.shape
    N = B * S
    xt = x.reshape(N, D)
    group_prob = jax.nn.softmax((xt @ wg_r + bg_r).astype(jnp.float32), axis=-1)
    g_val, g_idx = lax.top_k(group_prob, 1)
    exp_logits = (xt @ we_r + be_r).astype(jnp.float32).reshape(N, N_GROUPS, EXPERTS_PER_GROUP)
    in_group = exp_logits[jnp.arange(N), g_idx[:, 0]]
    e_val, e_idx = lax.top_k(in_group, TOP_K)
    gates = g_val * jax.nn.softmax(e_val, axis=-1)
    experts = g_idx * EXPERTS_PER_GROUP + e_idx
    NK = N * TOP_K
    flat_e = experts.reshape(NK)
    flat_tok = jnp.repeat(jnp.arange(N, dtype=jnp.int32), TOP_K)
    flat_g = gates.reshape(NK)
    order = jnp.argsort(flat_e)
    se = flat_e[order]
    counts = jnp.bincount(flat_e, length=N_EXPERTS)
    padded = ((counts + MOE_BLOCK - 1) // MOE_BLOCK) * MOE_BLOCK
    pad_end = jnp.cumsum(padded)
    pad_start = pad_end - padded
    raw_start = jnp.cumsum(counts) - counts
    dest = pad_start[se] + jnp.arange(NK) - raw_start[se]
    P = ((NK + MOE_BLOCK - 1) // MOE_BLOCK) * MOE_BLOCK + N_EXPERTS * MOE_BLOCK
    nb = P // MOE_BLOCK
    slot_tok = jnp.full((P,), N, dtype=jnp.int32).at[dest].set(flat_tok[order])
    slot_gate = jnp.zeros((P,), jnp.float32).at[dest].set(flat_g[order])
    blk_exp = jnp.minimum(jnp.searchsorted(pad_end, jnp.arange(nb) * MOE_BLOCK, side='right'), N_EXPERTS - 1)
    x_pad = jnp.concatenate([xt, jnp.zeros((1, D), xt.dtype)], axis=0)
    xs = x_pad[slot_tok].reshape(nb, MOE_BLOCK, D)

    def run_block(args):
        xb, e = args
        h = jax.nn.silu(xb @ w_gate[e]) * (xb @ w_up[e])
        return h @ w_down[e]

    ys = lax.map(run_block, (xs, blk_exp)).reshape(P, D)
    out = jnp.zeros((N + 1, D), x.dtype).at[slot_tok].add(ys * slot_gate[:, None].astype(ys.dtype))
    return out[:N].reshape(B, S, D)


def setup_inputs(seed: int = 0) -> dict:
    key = jax.random.key(seed)
    ks = jax.random.split(key, 26)
    f32 = jnp.float32
    L = DEPTH

    def nrm(k, shape, scale):
        return jax.random.normal(k, shape, f32) * scale

    x = nrm(ks[0], (BATCH, SEQ, D_MODEL), 1.0)
    offset = jax.random.randint(ks[1], (BATCH, 1), 0, 1024, dtype=jnp.int32)
    positions = offset + jnp.arange(SEQ, dtype=jnp.int32)[None, :]
    return {
        'x': x,
        'positions': positions,
        'w_in': nrm(ks[2], (L, D_MODEL, D_IN), D_MODEL ** -0.5),
        'sgu_ln_g': 1.0 + nrm(ks[3], (L, SGU_WIDTH), 0.02),
        'sgu_ln_b': nrm(ks[4], (L, SGU_WIDTH), 0.02),
        'sgu_ws': nrm(ks[5], (L, SGU_GROUPS, CHUNK, CHUNK), CHUNK ** -0.5),
        'sgu_b': 1.0 + nrm(ks[6], (L, SGU_GROUPS, CHUNK), 0.1),
        'hgrn_lb_logits': nrm(ks[7], (L, HGRN_WIDTH), 1.0),
        'hgrn_norm_g': 1.0 + nrm(ks[8], (L, HGRN_WIDTH), 0.02),
        'mla_qn_g': 1.0 + nrm(ks[9], (L, Q_LORA_RANK), 0.02),
        'mla_w_uq': nrm(ks[10], (L, Q_LORA_RANK, MLA_HEADS * MLA_QK_DIM), Q_LORA_RANK ** -0.5),
        'mla_kvn_g': 1.0 + nrm(ks[11], (L, KV_LORA_RANK), 0.02),
        'mla_w_ukv': nrm(ks[12], (L, KV_LORA_RANK, MLA_HEADS * (MLA_NOPE_DIM + MLA_V_DIM)), KV_LORA_RANK ** -0.5),
        'w_out': nrm(ks[13], (L, D_MIX, D_MODEL), DEEPNORM_BETA * D_MIX ** -0.5),
        'ln1_g': 1.0 + nrm(ks[14], (L, D_MODEL), 0.02),
        'ln1_b': nrm(ks[15], (L, D_MODEL), 0.02),
        'router_group_w': nrm(ks[16], (L, D_MODEL, N_GROUPS), D_MODEL ** -0.5),
        'router_group_b': nrm(ks[17], (L, N_GROUPS), 0.01),
        'router_expert_w': nrm(ks[18], (L, D_MODEL, N_EXPERTS), D_MODEL ** -0.5),
        'router_expert_b': nrm(ks[19], (L, N_EXPERTS), 0.01),
        'expert_w_gate': nrm(ks[20], (L, N_EXPERTS, D_MODEL, D_EXPERT), D_MODEL ** -0.5),
        'expert_w_up': nrm(ks[21], (L, N_EXPERTS, D_MODEL, D_EXPERT), DEEPNORM_BETA * D_MODEL ** -0.5),
        'expert_w_down': nrm(ks[22], (L, N_EXPERTS, D_EXPERT, D_MODEL), DEEPNORM_BETA * D_EXPERT ** -0.5),
        'ln2_g': 1.0 + nrm(ks[23], (L, D_MODEL), 0.02),
        'ln2_b': nrm(ks[24], (L, D_MODEL), 0.02),
    }


def reference(x, positions, w_in, sgu_ln_g, sgu_ln_b, sgu_ws, sgu_b, hgrn_lb_logits, hgrn_norm_g, mla_qn_g, mla_w_uq, mla_kvn_g, mla_w_ukv, w_out, ln1_g, ln1_b, router_group_w, router_group_b, router_expert_w, router_expert_b, expert_w_gate, expert_w_up, expert_w_down, ln2_g, ln2_b):
    cos, sin = rope_tables(positions)
    lb_cum = jnp.cumsum(jax.nn.softmax(hgrn_lb_logits.astype(jnp.float32), axis=0), axis=0)
    lower_bounds = lb_cum - lb_cum[0:1]
    split_points = _split_points()
    for l in range(DEPTH):
        h = x @ w_in[l]
        a_u, a_v, b_q, b_f, b_i, b_g, c_q, c_kv, c_kr = jnp.split(h, split_points, axis=-1)
        y_a = chunked_spatial_gating(jax.nn.gelu(a_u), jax.nn.gelu(a_v), sgu_ln_g[l], sgu_ln_b[l], sgu_ws[l], sgu_b[l])
        y_b = hgrn2_mixer(b_q, b_f, b_i, b_g, lower_bounds[l], hgrn_norm_g[l])
        y_c = mla_mixer(c_q, c_kv, c_kr, mla_qn_g[l], mla_w_uq[l], mla_kvn_g[l], mla_w_ukv[l], cos, sin)
        mix = jnp.concatenate([y_a, y_b.astype(y_a.dtype), y_c.astype(y_a.dtype)], axis=-1) @ w_out[l]
        x = layer_norm(DEEPNORM_ALPHA * x + mix, ln1_g[l], ln1_b[l])
        ffn = hierarchical_moe(x, router_group_w[l], router_group_b[l], router_expert_w[l], router_expert_b[l], expert_w_gate[l], expert_w_up[l], expert_w_down[l])
        x = layer_norm(DEEPNORM_ALPHA * x + ffn, ln2_g[l], ln2_b[l])
    return x
```

```python
import numpy as np
import concourse.bass as bass
import concourse.mybir as mybir
from concourse.bass_utils import run_bass_kernel_spmd

F32 = mybir.dt.float32
BF16 = mybir.dt.bfloat16
I32 = mybir.dt.int32
AF = mybir.ActivationFunctionType
ALU = mybir.AluOpType
AX = mybir.AxisListType

ENGS = ("pe", "act", "dve", "pool", "sp")
EPOCH = 30000


class Prog:
    def __init__(self):
        self.nc = bass.Bass("TRN2", target_bir_lowering=False)
        self.ops = {e: [] for e in ENGS}
        self.cnt = {e: 0 for e in ENGS}
        self.known = {e: {} for e in ENGS}
        self.state = {}
        self.chan = {}
        self.semnames = []
        self.semset = set()
        self.tensors = []
        self.handles = {}
        self.n_waits = 0
        self.exclusive = set()

    def dram(self, name, shape, dtype, kind):
        t = self.nc.dram_tensor(name, list(shape), dtype, kind=kind)
        return t

    def sbuf(self, name, shape, dtype):
        self.tensors.append(("sbuf", name, list(shape), dtype))
        return _Lazy(self, name)

    def psum(self, name, shape, dtype=F32):
        self.tensors.append(("psum", name, list(shape), dtype))
        return _Lazy(self, name)

    def _sem(self, name):
        if name not in self.semset:
            self.semset.add(name)
            self.semnames.append(name)
        return name

    def _eng_token(self, eng):
        self.cnt[eng] += 1
        c = self.cnt[eng] - 1
        ep, v = divmod(c, EPOCH)
        return (self._sem(f"c_{eng}_{ep}"), v + 1, 1)

    def _chan_token(self, chan):
        self.chan[chan] = self.chan.get(chan, 0) + 1
        c = self.chan[chan] - 1
        ep, v = divmod(c, EPOCH // 16)
        return (self._sem(f"d_{chan}_{ep}"), (v + 1) * 16, 16)

    def _resolve(self, tok):
        name, val, step = tok
        if step == 16:
            chan, ep = name[2:].rsplit("_", 1)
            ep = int(ep)
            tot = self.chan[chan]
            per = EPOCH // 16
            cur_ep = (tot - 1) // per
            if ep == cur_ep:
                val = (tot - ep * per) * 16
            else:
                val = per * 16
        return name, val

    def _collect(self, eng, reads, writes):
        toks = []
        for k in reads:
            st = self.state.get(k)
            if st and st[0]:
                toks.append(st[0])
        for k in writes:
            st = self.state.get(k)
            if st:
                if st[0]:
                    toks.append(st[0])
                toks.extend(st[1].values())
        need = {}
        for t in toks:
            name, val = self._resolve(t)
            if name.startswith("c_pe_") and eng == "pe":
                continue
            if self.known[eng].get(name, 0) >= val:
                continue
            need[name] = max(need.get(name, 0), val)
        for n, v in need.items():
            self.known[eng][n] = v
        return list(need.items())

    def _update(self, tok, reads, writes):
        for k in reads:
            st = self.state.setdefault(k, [None, {}])
            st[1][tok[0]] = tok
        for k in writes:
            self.state[k] = [tok, {}]

    def _excl(self, reads, writes):
        if not self.exclusive:
            return reads, writes
        r2 = [k for k in reads if k not in self.exclusive]
        w2 = list(writes) + [k for k in reads if k in self.exclusive and k not in writes]
        return r2, w2

    def op(self, eng, fn, reads=(), writes=()):
        reads, writes = self._excl(reads, writes)
        waits = self._collect(eng, reads, writes)
        tok = self._eng_token(eng)
        self.n_waits += len(waits)
        self.ops[eng].append((waits, fn, (tok[0], 1)))
        self._update(tok, reads, writes)

    def dma(self, q, out, in_, reads=(), writes=(), chan=None, indirect=None, **kw):
        assert chan is not None
        waits = self._collect(q, reads, writes)
        tok = self._chan_token(chan)
        self.n_waits += len(waits)
        if indirect is None:
            fn = lambda e, out=out, in_=in_, kw=kw: e.dma_start(out=_r(out), in_=_r(in_), **kw)
        else:
            fn = indirect
        self.ops[q].append((waits, fn, (tok[0], 16)))
        self._update(tok, reads, writes)

    def raw(self, eng, fn, reads=(), writes=()):
        reads, writes = self._excl(reads, writes)
        waits = self._collect(eng, reads, writes)
        self.ops[eng].append((waits, fn, None))

    def barrier(self):
        targets = {}
        for e in ENGS:
            c = self.cnt[e]
            if c > 0:
                ep, v = divmod(c - 1, EPOCH)
                targets[f"c_{e}_{ep}"] = v + 1
        for chan, tot in self.chan.items():
            per = EPOCH // 16
            ep = (tot - 1) // per
            targets[f"d_{chan}_{ep}"] = (tot - ep * per) * 16
        for e in ENGS:
            waits = []
            for n, v in targets.items():
                if n.startswith("c_pe_") and e == "pe":
                    continue
                if self.known[e].get(n, 0) >= v:
                    continue
                self.known[e][n] = v
                waits.append((n, v))
            self.ops[e].append((waits, None, None))

    def wait_all(self, eng, keys):
        waits = self._collect(eng, (), keys)
        self.ops[eng].append((waits, None, None))

    def build(self):
        nc = self.nc
        from contextlib import ExitStack
        with ExitStack() as es:
            for kind, name, shape, dtype in self.tensors:
                if kind == "sbuf":
                    self.handles[name] = es.enter_context(nc.sbuf_tensor(name, shape, dtype))
                else:
                    self.handles[name] = es.enter_context(nc.psum_tensor(name, shape, dtype))
            sems = {}
            for n in self.semnames:
                sems[n] = es.enter_context(nc.semaphore(n))
            block = es.enter_context(nc.Block())

            def emit(engobj, lst):
                for waits, fn, inc in lst:
                    for n, v in waits:
                        engobj.wait_ge(sems[n], v)
                    if fn is not None:
                        ins = fn(engobj)
                        if inc is not None:
                            ins.then_inc(sems[inc[0]], inc[1])

            @block.tensor
            def _(e):
                emit(e, self.ops["pe"])

            @block.scalar
            def _(e):
                emit(e, self.ops["act"])

            @block.vector
            def _(e):
                emit(e, self.ops["dve"])

            @block.gpsimd
            def _(e):
                emit(e, self.ops["pool"])

            @block.sync
            def _(e):
                emit(e, self.ops["sp"])
        return nc


class _Lazy:
    def __init__(self, prog, name, chain=()):
        self.prog = prog
        self.name = name
        self.chain = chain

    def __getitem__(self, idx):
        return _Lazy(self.prog, self.name, self.chain + (("idx", idx),))

    def rearrange(self, pat, **kw):
        return _Lazy(self.prog, self.name, self.chain + (("re", pat, kw),))

    def bitcast(self, dt):
        return _Lazy(self.prog, self.name, self.chain + (("bc", dt),))

    def to_broadcast(self, shape):
        return _Lazy(self.prog, self.name, self.chain + (("tb", shape),))

    def resolve(self):
        h = self.prog.handles[self.name]
        if not self.chain:
            return h[:]
        cur = h if self.chain[0][0] == "idx" else h[:]
        for c in self.chain:
            if c[0] == "idx":
                cur = cur[c[1]]
            elif c[0] == "re":
                cur = cur.rearrange(c[1], **c[2])
            elif c[0] == "bc":
                cur = cur.bitcast(c[1])
            elif c[0] == "tb":
                cur = cur.to_broadcast(c[1])
        return cur


def _r(x):
    return x.resolve() if isinstance(x, _Lazy) else x


import math

NTOK = 2048
D = 2048
EPS = 1e-5


class Buf:
    def __init__(self, ap, key):
        self.ap = ap
        self.key = key


class Rot:
    def __init__(self, P, name, n, shape, dtype, psum=False):
        self.bufs = []
        for i in range(n):
            t = P.psum(f"{name}{i}", shape, dtype) if psum else P.sbuf(f"{name}{i}", shape, dtype)
            self.bufs.append(Buf(t, f"{name}{i}"))
        self.i = 0

    def next(self):
        b = self.bufs[self.i % len(self.bufs)]
        self.i += 1
        return b


def keys(lst):
    return [b.key if isinstance(b, Buf) else b for b in lst]


class H:
    def __init__(self, P):
        self.P = P

    def mm(self, out, lhsT, rhs, start, stop, reads, writes):
        self.P.op("pe", lambda e: e.matmul(_r(out), lhsT=_r(lhsT), rhs=_r(rhs), start=start, stop=stop),
                  reads=keys(reads), writes=keys(writes))

    def tr(self, out, in_, ident, reads, writes):
        self.P.op("pe", lambda e: e.transpose(_r(out), _r(in_), _r(ident)), reads=keys(reads), writes=keys(writes))

    def act(self, out, in_, func, reads, writes, scale=1.0, bias=None, accum=None, eng="act"):
        def fn(e):
            kw = {}
            if bias is not None:
                kw["bias"] = _r(bias)
            if accum is not None:
                kw["accum_out"] = _r(accum)
            return e.activation(out=_r(out), in_=_r(in_), func=func, scale=(_r(scale) if not isinstance(scale, float) else scale), **kw)
        self.P.op("act", fn, reads=keys(reads), writes=keys(writes))

    def tt(self, eng, out, a, b, op, reads, writes):
        self.P.op(eng, lambda e: e.tensor_tensor(out=_r(out), in0=_r(a), in1=_r(b), op=op), reads=keys(reads), writes=keys(writes))

    def ts(self, eng, out, a, s1, s2, op0, op1, reads, writes, accum=None):
        def fn(e):
            kw = {}
            if accum is not None:
                kw["accum_out"] = _r(accum)
            if s2 is None:
                return e.tensor_scalar(out=_r(out), in0=_r(a), scalar1=_r(s1), scalar2=None, op0=op0, **kw)
            return e.tensor_scalar(out=_r(out), in0=_r(a), scalar1=_r(s1), scalar2=_r(s2), op0=op0, op1=op1, **kw)
        self.P.op(eng, fn, reads=keys(reads), writes=keys(writes))

    def stt(self, eng, out, a, s, b, op0, op1, reads, writes):
        self.P.op(eng, lambda e: e.scalar_tensor_tensor(out=_r(out), in0=_r(a), scalar=_r(s), in1=_r(b), op0=op0, op1=op1),
                  reads=keys(reads), writes=keys(writes))

    def cp(self, eng, out, in_, reads, writes):
        if eng == "act":
            self.P.op("act", lambda e: e.copy(out=_r(out), in_=_r(in_)), reads=keys(reads), writes=keys(writes))
        else:
            self.P.op(eng, lambda e: e.tensor_copy(out=_r(out), in_=_r(in_)), reads=keys(reads), writes=keys(writes))

    def red(self, eng, out, in_, op, reads, writes):
        self.P.op(eng, lambda e: e.tensor_reduce(out=_r(out), in_=_r(in_), axis=AX.X, op=op), reads=keys(reads), writes=keys(writes))

    def recip(self, out, in_, reads, writes):
        self.P.op("dve", lambda e: e.reciprocal(out=_r(out), in_=_r(in_)), reads=keys(reads), writes=keys(writes))

    def memset(self, eng, out, val, writes):
        self.P.op(eng, lambda e: e.memset(_r(out), val), reads=(), writes=keys(writes))

    def dma(self, q, out, in_, reads, writes, chan):
        self.P.dma(q, out, in_, reads=keys(reads), writes=keys(writes), chan=chan)


def gelu_tanh(h, ps, out, tmp_rot, shape_sl, reads_ps, writes_out, dve="dve"):
    xs = tmp_rot.next(); t1 = tmp_rot.next()
    h.cp("act", xs.ap[shape_sl], ps, reads_ps, [xs])
    h.act(t1.ap[shape_sl], ps, AF.Square, reads_ps, [t1])
    h.ts(dve, t1.ap[shape_sl], t1.ap[shape_sl], 0.044715, 1.0, ALU.mult, ALU.add, [t1], [t1])
    h.tt(dve, t1.ap[shape_sl], t1.ap[shape_sl], xs.ap[shape_sl], ALU.mult, [t1, xs], [t1])
    h.act(t1.ap[shape_sl], t1.ap[shape_sl], AF.Sigmoid, [t1], [t1], scale=2.0 * math.sqrt(2.0 / math.pi))
    h.tt(dve, out, t1.ap[shape_sl], xs.ap[shape_sl], ALU.mult, [t1, xs], writes_out)


def lb_compute(h, Lap, mask_ap, tmp_a, tmp_b, out_oml, keyL, n):
    ta, tb = tmp_a, tmp_b
    h.tt("dve", ta.ap, Lap(0), Lap(1), ALU.max, [keyL], [ta])
    h.tt("dve", ta.ap, ta.ap, Lap(2), ALU.max, [keyL, ta], [ta])
    h.tt("dve", ta.ap, ta.ap, Lap(3), ALU.max, [keyL, ta], [ta])
    for l in range(4):
        h.tt("dve", Lap(l), Lap(l), ta.ap, ALU.subtract, [keyL, ta], [keyL])
        h.act(Lap(l), Lap(l), AF.Exp, [keyL], [keyL])
    h.tt("dve", ta.ap, Lap(0), Lap(1), ALU.add, [keyL], [ta])
    h.tt("dve", ta.ap, ta.ap, Lap(2), ALU.add, [keyL, ta], [ta])
    h.tt("dve", ta.ap, ta.ap, Lap(3), ALU.add, [keyL, ta], [ta])
    h.recip(ta.ap, ta.ap, [ta], [ta])
    h.ts("dve", tb.ap, Lap(0), mask_ap(0), None, ALU.mult, None, [keyL, "lmask"], [tb])
    for l in range(1, 4):
        h.stt("dve", tb.ap, Lap(l), mask_ap(l), tb.ap, ALU.mult, ALU.add, [keyL, "lmask", tb], [tb])
    h.tt("dve", tb.ap, tb.ap, ta.ap, ALU.mult, [ta, tb], [tb])
    h.ts("dve", out_oml.ap, tb.ap, -1.0, 1.0, ALU.mult, ALU.add, [tb], [out_oml])


def build_A():
    P = Prog(); h = H(P)
    HT = NTOK // 2
    xT_d = P.dram("xT", [D, NTOK], F32, "ExternalInput")
    w_in_d = P.dram("w_in", [D, 4160], F32, "ExternalInput")
    w_uq_d = P.dram("w_uq", [512, 1536], F32, "ExternalInput")
    w_ukv_d = P.dram("w_ukv", [512, 2048], F32, "ExternalInput")
    lngb_d = P.dram("lngb", [128, 2, 512], F32, "ExternalInput")
    wsT_d = P.dram("wsT", [128, 4, 128], F32, "ExternalInput")
    tri_d = P.dram("tri", [128, 128], F32, "ExternalInput")
    sgub_d = P.dram("sgub", [128, 4], F32, "ExternalInput")
    lbT_d = P.dram("lbT", [128, 4, 4], F32, "ExternalInput")
    lbB_d = P.dram("lbB", [128, 4, 512], F32, "ExternalInput")
    lmask_d = P.dram("lmask", [128, 4], F32, "ExternalInput")
    qng_d = P.dram("qng", [128, 4], F32, "ExternalInput")
    kvng_d = P.dram("kvng", [128, 4], F32, "ExternalInput")
    pos_d = P.dram("posr", [64, NTOK], I32, "ExternalInput")
    ropec_d = P.dram("ropec", [64, 2], F32, "ExternalInput")
    ones_d = P.dram("ones", [128, 128], F32, "ExternalInput")

    ya_d = P.dram("ya", [NTOK, 512], BF16, "ExternalOutput")
    hqT_d = P.dram("hqT", [512, NTOK], F32, "ExternalOutput")
    hkT_d = P.dram("hkT", [512, NTOK], F32, "ExternalOutput")
    hlogf_d = P.dram("hlogf", [NTOK, 512], F32, "ExternalOutput")
    hkk_d = P.dram("hkk", [NTOK, 512], F32, "ExternalOutput")
    hv_d = P.dram("hv", [NTOK, 512], BF16, "ExternalOutput")
    hgate_d = P.dram("hgate", [NTOK, 512], F32, "ExternalOutput")
    QT_d = P.dram("QT", [8 * 192, NTOK], BF16, "ExternalOutput")
    KnT_d = P.dram("KnT", [1024, NTOK], BF16, "ExternalOutput")
    KrT_d = P.dram("KrT", [64, NTOK], BF16, "ExternalOutput")
    Vv_d = P.dram("Vv", [NTOK, 1024], BF16, "ExternalOutput")
    outs = ["ya_d", "hqT_d", "hkT_d", "hlogf_d", "hkk_d", "hv_d", "hgate_d", "QT_d", "KnT_d", "KrT_d", "Vv_d"]

    xT_bf = [Buf(P.sbuf(f"xTbf{i}", [128, 16, 512], BF16), f"xTbf{i}") for i in range(2)]
    w_st = Rot(P, "wst", 2, [128, 4, 512], F32)
    w_bf = Rot(P, "wbf", 2, [128, 16, 512], BF16)
    upw = Buf(P.sbuf("upw", [128, 4, 2048], BF16), "upw")
    tmp = Rot(P, "tmp", 6, [128, 512], F32)
    gur = Rot(P, "gur", 2, [128, 512], F32)
    gvr = Rot(P, "gvr", 2, [128, 512], F32)
    ostf = Rot(P, "ostf", 4, [128, 512], F32)
    ostb = Rot(P, "ostb", 4, [128, 512], BF16)
    c_sb = Buf(P.sbuf("c_sb", [128, 4, 512], F32), "c_sb")
    sq = Buf(P.sbuf("sq", [128, 4, 512], F32), "sq")
    cn = Rot(P, "cn", 2, [128, 4, 512], BF16)
    rstd = Buf(P.sbuf("rstd", [128, 512], F32), "rstd")
    cos2 = Buf(P.sbuf("cos2", [64, 512], F32), "cos2")
    sinS = Buf(P.sbuf("sinS", [64, 512], F32), "sinS")
    rt = [Buf(P.sbuf(f"rt{i}", [64, 512], F32), f"rt{i}") for i in range(2)]
    rti = Buf(P.sbuf("rti", [64, 512], I32), "rti")
    posi = Buf(P.sbuf("posi", [64, NTOK], I32), "posi")
    lngb = Buf(P.sbuf("lngb_s", [128, 2, 512], F32), "lngb")
    wsT_f = Buf(P.sbuf("wsT_f", [128, 4, 128], F32), "wsT_f")
    wsT_b = Buf(P.sbuf("wsT_b", [128, 4, 128], BF16), "wsT_b")
    tri = Buf(P.sbuf("tri_s", [128, 128], F32), "tri")
    sgub = Buf(P.sbuf("sgub_s", [128, 4], F32), "sgub")
    lbT = Buf(P.sbuf("lbT_s", [128, 4, 4], F32), "lbT")
    lbB = Buf(P.sbuf("lbB_s", [128, 4, 512], F32), "lbB")
    lmask = Buf(P.sbuf("lmask_s", [128, 4], F32), "lmask")
    oml_fm = Buf(P.sbuf("oml_fm", [128, 4], F32), "oml_fm")
    oml_tm = Buf(P.sbuf("oml_tm", [128, 512], F32), "oml_tm")
    sm = [Buf(P.sbuf(f"sm{i}", [128, 4], F32), f"sm{i}") for i in range(2)]
    qng = Buf(P.sbuf("qng_s", [128, 4], F32), "qng")
    kvng = Buf(P.sbuf("kvng_s", [128, 4], F32), "kvng")
    ropec = Buf(P.sbuf("ropec_s", [64, 2], F32), "ropec")
    ones = Buf(P.sbuf("ones_s", [128, 128], F32), "ones")
    stat = Rot(P, "stat", 4, [128, 2], F32)
    mmps = Rot(P, "mmps", 4, [128, 512], F32, psum=True)
    upps = Rot(P, "upps", 3, [128, 512], F32, psum=True)
    ssqps = Buf(P.psum("ssqps", [128, 512], F32), "ssqps")

    for b, d_ in ((lngb, lngb_d), (wsT_f, wsT_d), (tri, tri_d), (sgub, sgub_d), (lbT, lbT_d), (lbB, lbB_d),
                  (lmask, lmask_d), (qng, qng_d), (kvng, kvng_d), (posi, pos_d), (ropec, ropec_d), (ones, ones_d)):
        h.dma("sp", b.ap, d_.ap(), [], [b], chan="const")
    for g in range(4):
        h.tt("dve", wsT_b.ap[:, g, :], wsT_f.ap[:, g, :], tri.ap, ALU.mult, [wsT_f, tri], [wsT_b])
    lb_compute(h, lambda l: lbT.ap[:, l, :], lambda l: lmask.ap[:, l:l + 1], sm[0], sm[1], oml_fm, lbT, 4)
    ta = tmp.next(); tb_ = tmp.next()
    lb_compute(h, lambda l: lbB.ap[:, l, :], lambda l: lmask.ap[:, l:l + 1], ta, tb_, oml_tm, lbB, 512)

    w_in_v = w_in_d.ap().rearrange("(c p) n -> p c n", p=128)
    xT_v = xT_d.ap().rearrange("(c p) t -> p c t", p=128)
    cast_i = [0]

    def cast_eng():
        cast_i[0] += 1
        return ("dve", "pool")[cast_i[0] % 2]

    def load_group(col0, ncols, swap64=False):
        wb = w_bf.next()
        for q in range(4):
            st = w_st.next()
            h.dma("sp", st.ap[:, :, 0:ncols], w_in_v[:, q * 4:(q + 1) * 4, col0:col0 + ncols], [], [st], chan=st.key)
            h.cp(cast_eng(), wb.ap[:, q * 4:(q + 1) * 4, 0:ncols], st.ap[:, :, 0:ncols], [st], [wb])
            if swap64:
                h.cp(cast_eng(), wb.ap[:, q * 4:(q + 1) * 4, 64:96], st.ap[:, :, 32:64], [st], [wb])
                h.cp(cast_eng(), wb.ap[:, q * 4:(q + 1) * 4, 96:128], st.ap[:, :, 0:32], [st], [wb])
        return wb

    def rope_tables(tok0):
        a, b = rt
        h.cp("dve", a.ap, posi.ap[:, tok0:tok0 + 512], [posi], [a])
        h.ts("dve", a.ap, a.ap, ropec.ap[:, 0:1], None, ALU.mult, None, [a, ropec], [a])
        for off, dst in ((0.0, sinS), (0.25, cos2)):
            h.ts("dve", b.ap, a.ap, off, None, ALU.add, None, [a], [b])
            h.cp("dve", rti.ap, b.ap, [b], [rti])
            h.cp("dve", dst.ap, rti.ap, [rti], [dst])
            h.tt("dve", b.ap, b.ap, dst.ap, ALU.subtract, [b, dst], [b])
            h.ts("dve", dst.ap, b.ap, 0.5, None, ALU.is_gt, None, [b], [dst])
            h.tt("dve", b.ap, b.ap, dst.ap, ALU.subtract, [b, dst], [b])
            h.ts("dve", dst.ap, b.ap, -0.5, None, ALU.is_lt, None, [b], [dst])
            h.tt("dve", b.ap, b.ap, dst.ap, ALU.add, [b, dst], [b])
            h.act(dst.ap, b.ap, AF.Sin, [b], [dst], scale=2.0 * math.pi)
        h.ts("dve", sinS.ap, sinS.ap, ropec.ap[:, 1:2], None, ALU.mult, None, [sinS, ropec], [sinS])

    def rope_apply(pa, pb, dst_dram_ap, wkey):
        t1 = tmp.next(); t2 = tmp.next(); ob = ostb.next()
        h.tt("dve", t1.ap[0:64, :], pa.ap[0:64, :], cos2.ap, ALU.mult, [pa, cos2], [t1])
        h.tt("dve", t2.ap[0:64, :], pb.ap[0:64, :], sinS.ap, ALU.mult, [pb, sinS], [t2])
        h.tt("pool", ob.ap[0:64, :], t1.ap[0:64, :], t2.ap[0:64, :], ALU.add, [t1, t2], [ob])
        h.dma("pool", dst_dram_ap, ob.ap[0:64, :], [ob], [wkey], chan=ob.key)

    for half in range(2):
        t0 = half * HT
        for tb in range(2):
            for q in range(4):
                st = w_st.next()
                h.dma("sp", st.ap, xT_v[:, q * 4:(q + 1) * 4, t0 + tb * 512:t0 + (tb + 1) * 512], [], [st], chan=st.key)
                h.cp(cast_eng(), xT_bf[tb].ap[:, q * 4:(q + 1) * 4, :], st.ap, [st], [xT_bf[tb]])

        def fm_mm(ps, wb, j, tb, ncol=128, coff=None):
            co = j * 128 if coff is None else coff
            for c in range(16):
                h.mm(ps.ap[0:ncol, :], wb.ap[:, c, co:co + ncol], xT_bf[tb].ap[:, c, :], c == 0, c == 15, [wb, xT_bf[tb]], [ps])

        def tm_mm(ps, wb, tt):
            tb, r = divmod(tt, 4)
            for c in range(16):
                h.mm(ps.ap, xT_bf[tb].ap[:, c, r * 128:(r + 1) * 128], wb.ap[:, c, :], c == 0, c == 15, [wb, xT_bf[tb]], [ps])

        wb_u = load_group(0, 512)
        wb_v = load_group(512, 512)
        for tt in range(8):
            tok = t0 + tt * 128
            pu = mmps.next(); tm_mm(pu, wb_u, tt)
            pv = mmps.next(); tm_mm(pv, wb_v, tt)
            gu = gur.next()
            gelu_tanh(h, pu.ap, gu.ap, tmp, slice(None), [pu], [gu])
            gv = gvr.next()
            gelu_tanh(h, pv.ap, gv.ap, tmp, slice(None), [pv], [gv])
            st_ = stat.next()
            h.memset("pool", st_.ap, 0.0, [st_])
            h.red("dve", st_.ap[:, 0:1], gv.ap, ALU.add, [gv, st_], [st_])
            h.ts("dve", st_.ap[:, 0:1], st_.ap[:, 0:1], 1.0 / 512, None, ALU.mult, None, [st_], [st_])
            h.ts("dve", gv.ap, gv.ap, st_.ap[:, 0:1], None, ALU.subtract, None, [gv, st_], [gv])
            junk = tmp.next()
            h.act(junk.ap, gv.ap, AF.Square, [gv, st_], [junk, st_], accum=st_.ap[:, 1:2])
            h.ts("dve", st_.ap[:, 1:2], st_.ap[:, 1:2], 1.0 / 512, EPS, ALU.mult, ALU.add, [st_], [st_])
            h.act(st_.ap[:, 1:2], st_.ap[:, 1:2], AF.Sqrt, [st_], [st_])
            h.recip(st_.ap[:, 1:2], st_.ap[:, 1:2], [st_], [st_])
            h.stt("dve", junk.ap, gv.ap, st_.ap[:, 1:2], lngb.ap[:, 0, :], ALU.mult, ALU.mult, [gv, st_, lngb], [junk])
            vn = ostb.next()
            h.tt("pool", vn.ap, junk.ap, lngb.ap[:, 1, :], ALU.add, [junk, lngb], [vn])
            pm = upps.next()
            for g in range(4):
                h.mm(pm.ap[:, g * 128:(g + 1) * 128], wsT_b.ap[:, g, :], vn.ap[:, g * 128:(g + 1) * 128], True, True, [wsT_b, vn], [pm])
            mx = tmp.next()
            for g in range(4):
                h.ts("dve", mx.ap[:, g * 128:(g + 1) * 128], pm.ap[:, g * 128:(g + 1) * 128], sgub.ap[:, g:g + 1], None, ALU.add, None, [pm, sgub], [mx])
            yb = ostb.next()
            h.tt("pool", yb.ap, mx.ap, gu.ap, ALU.mult, [mx, gu], [yb])
            h.dma("pool", ya_d.ap()[tok:tok + 128, :], yb.ap, [yb], ["ya_d"], chan=yb.key)

        wb = load_group(1024, 512)
        for j in range(4):
            for tb in range(2):
                ps = mmps.next(); fm_mm(ps, wb, j, tb)
                o = ostf.next()
                h.act(o.ap, ps.ap, AF.Silu, [ps], [o])
                h.dma("pool", hqT_d.ap()[j * 128:(j + 1) * 128, t0 + tb * 512:t0 + (tb + 1) * 512], o.ap, [o], ["hqT_d"], chan=o.key)
        wb = load_group(1536, 512)
        for j in range(4):
            for tb in range(2):
                ps = mmps.next(); fm_mm(ps, wb, j, tb)
                o = ostf.next()
                h.act(o.ap, ps.ap, AF.Sigmoid, [ps], [o], scale=-1.0)
                h.ts("dve", o.ap, o.ap, oml_fm.ap[:, j:j + 1], None, ALU.mult, None, [o, oml_fm], [o])
                h.dma("pool", hkT_d.ap()[j * 128:(j + 1) * 128, t0 + tb * 512:t0 + (tb + 1) * 512], o.ap, [o], ["hkT_d"], chan=o.key)
        for tt in range(8):
            tok = t0 + tt * 128
            ps = mmps.next(); tm_mm(ps, wb, tt)
            sg = tmp.next(); o1 = ostf.next(); o2 = ostf.next()
            h.act(sg.ap, ps.ap, AF.Sigmoid, [ps], [sg], scale=-1.0)
            h.tt("dve", o1.ap, sg.ap, oml_tm.ap, ALU.mult, [sg, oml_tm], [o1])
            h.ts("dve", sg.ap, o1.ap, -1.0, 1.0, ALU.mult, ALU.add, [o1], [sg])
            h.act(o2.ap, sg.ap, AF.Ln, [sg], [o2])
            h.dma("pool", hkk_d.ap()[tok:tok + 128, :], o1.ap, [o1], ["hkk_d"], chan=o1.key)
            h.dma("pool", hlogf_d.ap()[tok:tok + 128, :], o2.ap, [o2], ["hlogf_d"], chan=o2.key)
        wb = load_group(2048, 512)
        for tt in range(8):
            tok = t0 + tt * 128
            ps = mmps.next(); tm_mm(ps, wb, tt)
            o = ostb.next()
            h.cp("act", o.ap, ps.ap, [ps], [o])
            h.dma("pool", hv_d.ap()[tok:tok + 128, :], o.ap, [o], ["hv_d"], chan=o.key)
        wb = load_group(2560, 512)
        for tt in range(8):
            tok = t0 + tt * 128
            ps = mmps.next(); tm_mm(ps, wb, tt)
            o = ostf.next()
            h.act(o.ap, ps.ap, AF.Silu, [ps], [o])
            h.dma("pool", hgate_d.ap()[tok:tok + 128, :], o.ap, [o], ["hgate_d"], chan=o.key)

        def rms_block(wb, tb, gcol):
            pss = [mmps.next() for _ in range(4)]
            for j in range(4):
                fm_mm(pss[j], wb, j, tb)
            for j in range(4):
                h.cp("act", c_sb.ap[:, j, :], pss[j].ap, [pss[j]], [c_sb])
                h.act(sq.ap[:, j, :], pss[j].ap, AF.Square, [pss[j]], [sq])
            for j in range(4):
                h.mm(ssqps.ap, ones.ap, sq.ap[:, j, :], j == 0, j == 3, [ones, sq], [ssqps])
            h.ts("dve", rstd.ap, ssqps.ap, 1.0 / 512, EPS, ALU.mult, ALU.add, [ssqps], [rstd])
            h.act(rstd.ap, rstd.ap, AF.Sqrt, [rstd], [rstd])
            h.recip(rstd.ap, rstd.ap, [rstd], [rstd])
            c = cn.next()
            for j in range(4):
                h.stt("dve", c.ap[:, j, :], c_sb.ap[:, j, :], gcol.ap[:, j:j + 1], rstd.ap, ALU.mult, ALU.mult, [c_sb, gcol, rstd], [c])
            return c

        wuq_v = w_uq_d.ap().rearrange("(c p) n -> p c n", p=128)
        for j in range(4):
            for part in range(3):
                st = w_st.next()
                h.dma("sp", st.ap[:, 0, :], wuq_v[:, j, part * 512:(part + 1) * 512], [], [st], chan=st.key)
                h.cp(cast_eng(), upw.ap[:, j, part * 512:(part + 1) * 512], st.ap[:, 0, :], [st], [upw])
        for j in range(4):
            src = upw.ap[:, j, 0:1536].rearrange("p (h d) -> p h d", d=192)
            dst = upw.ap[:, j, 1536:2048].rearrange("p (h d) -> p h d", d=64)
            h.cp("dve", dst[:, :, 0:32], src[:, :, 160:192], [upw], [upw])
            h.cp("dve", dst[:, :, 32:64], src[:, :, 128:160], [upw], [upw])
        wb = load_group(3072, 512)
        for tb in range(2):
            tk = t0 + tb * 512
            c = rms_block(wb, tb, qng)
            rope_tables(tk)
            for hh in range(8):
                pq = upps.next()
                for j in range(4):
                    h.mm(pq.ap, upw.ap[:, j, hh * 192:hh * 192 + 128], c.ap[:, j, :], j == 0, j == 3, [upw, c], [pq])
                o = ostb.next()
                h.cp("act", o.ap, pq.ap, [pq], [o])
                h.dma("pool", QT_d.ap()[hh * 192:hh * 192 + 128, tk:tk + 512], o.ap, [o], ["QT_d"], chan=o.key)
                pa = upps.next(); pb = upps.next()
                for j in range(4):
                    h.mm(pa.ap[0:64, :], upw.ap[:, j, hh * 192 + 128:hh * 192 + 192], c.ap[:, j, :], j == 0, j == 3, [upw, c], [pa])
                for j in range(4):
                    h.mm(pb.ap[0:64, :], upw.ap[:, j, 1536 + hh * 64:1536 + (hh + 1) * 64], c.ap[:, j, :], j == 0, j == 3, [upw, c], [pb])
                rope_apply(pa, pb, QT_d.ap()[hh * 192 + 128:hh * 192 + 192, tk:tk + 512], "QT_d")
        wukv_v = w_ukv_d.ap().rearrange("(c p) n -> p c n", p=128)
        for j in range(4):
            for part in range(4):
                st = w_st.next()
                h.dma("sp", st.ap[:, 0, :], wukv_v[:, j, part * 512:(part + 1) * 512], [], [st], chan=st.key)
                src = st.ap[:, 0, :].rearrange("p (h t d) -> p h t d", t=2, d=128)
                dk = upw.ap[:, j, part * 256:(part + 1) * 256].rearrange("p (h d) -> p h d", d=128)
                dv = upw.ap[:, j, 1024 + part * 256:1024 + (part + 1) * 256].rearrange("p (h d) -> p h d", d=128)
                h.cp(cast_eng(), dk, src[:, :, 0, :], [st], [upw])
                h.cp(cast_eng(), dv, src[:, :, 1, :], [st], [upw])
        wb = load_group(3584, 512)
        for tb in range(2):
            tk = t0 + tb * 512
            c = rms_block(wb, tb, kvng)
            for hh in range(8):
                pq = upps.next()
                for j in range(4):
                    h.mm(pq.ap, upw.ap[:, j, hh * 128:(hh + 1) * 128], c.ap[:, j, :], j == 0, j == 3, [upw, c], [pq])
                o = ostb.next()
                h.cp("act", o.ap, pq.ap, [pq], [o])
                h.dma("pool", KnT_d.ap()[hh * 128:(hh + 1) * 128, tk:tk + 512], o.ap, [o], ["KnT_d"], chan=o.key)
            for r in range(4):
                for grp in range(2):
                    pq = upps.next()
                    for j in range(4):
                        h.mm(pq.ap, c.ap[:, j, r * 128:(r + 1) * 128], upw.ap[:, j, 1024 + grp * 512:1024 + (grp + 1) * 512], j == 0, j == 3, [upw, c], [pq])
                    o = ostb.next()
                    h.cp("act", o.ap, pq.ap, [pq], [o])
                    h.dma("pool", Vv_d.ap()[tk + r * 128:tk + (r + 1) * 128, grp * 512:(grp + 1) * 512], o.ap, [o], ["Vv_d"], chan=o.key)
        wb = load_group(4096, 64, swap64=True)
        for tb in range(2):
            tk = t0 + tb * 512
            rope_tables(tk)
            pa = mmps.next(); pb = mmps.next()
            fm_mm(pa, wb, 0, tb, ncol=64, coff=0)
            fm_mm(pb, wb, 0, tb, ncol=64, coff=64)
            rope_apply(pa, pb, KrT_d.ap()[:, tk:tk + 512], "KrT_d")

    P.wait_all("sp", outs)
    P.wait_all("pool", outs)
    return P


S = 4096
CH = 64
NCH = S // CH
GRP = 8


def build_B():
    P = Prog(); h = H(P)
    hqT_d = P.dram("b_hqT", [2, 128, S], F32, "ExternalInput")
    hkT_d = P.dram("b_hkT", [2, 128, S], F32, "ExternalInput")
    hlogf_d = P.dram("b_hlogf", [2, S, 128], F32, "ExternalInput")
    hkk_d = P.dram("b_hkk", [2, S, 128], F32, "ExternalInput")
    hv_d = P.dram("b_hv", [2, S, 128], BF16, "ExternalInput")
    hgate_d = P.dram("b_hgate", [2, S, 128], F32, "ExternalInput")
    ng_d = P.dram("b_ng", [64, 2, 128], F32, "ExternalInput")
    ucat_d = P.dram("b_ucat", [64, 128], F32, "ExternalInput")
    lmat_d = P.dram("b_lmat", [64, 64], F32, "ExternalInput")
    QT_d = P.dram("b_QT", [4, 192, S], BF16, "ExternalInput")
    KnT_d = P.dram("b_KnT", [4, 128, S], BF16, "ExternalInput")
    KrT_d = P.dram("b_KrT", [64, S], BF16, "ExternalInput")
    V_d = P.dram("b_V", [4, 128, 32, 128], BF16, "ExternalInput")
    cmask_d = P.dram("b_cmask", [128, 4, 512], BF16, "ExternalInput")
    onesb_d = P.dram("b_onesb", [128, 128], BF16, "ExternalInput")
    yb_d = P.dram("yb", [S, 256], BF16, "ExternalOutput")
    ycT_d = P.dram("ycT", [512, S], BF16, "ExternalOutput")

    ng = Buf(P.sbuf("ng", [64, 2, 128], F32), "ng")
    ucat = Buf(P.sbuf("ucat", [64, 128], F32), "ucat")
    lmat = Buf(P.sbuf("lmat", [64, 64], F32), "lmat")
    cmask = Buf(P.sbuf("cmask", [128, 4, 512], BF16), "cmask")
    onesb = Buf(P.sbuf("onesb", [128, 128], BF16), "onesb")
    for b, d_ in ((ng, ng_d), (ucat, ucat_d), (lmat, lmat_d), (cmask, cmask_d), (onesb, onesb_d)):
        h.dma("sp", b.ap, d_.ap(), [], [b], chan="const")

    gq = Rot(P, "gq", 2, [128, 512], F32)
    gk = Rot(P, "gk", 2, [128, 512], F32)
    glf = Rot(P, "glf", 2, [64, GRP, 128], F32)
    gkk = Rot(P, "gkk", 2, [64, GRP, 128], F32)
    ggt = Rot(P, "ggt", 2, [64, GRP, 128], F32)
    gv = Rot(P, "gv", 2, [64, GRP, 128], BF16)
    yst = Rot(P, "yst", 2, [64, GRP, 128], BF16)
    Sf = [Buf(P.sbuf(f"Sf{i}", [128, 128], F32), f"Sf{i}") for i in range(2)]
    Sb = [Rot(P, f"Sb{i}_", 2, [128, 128], BF16) for i in range(2)]
    eg = Rot(P, "eg", 2, [128, 128], F32)
    ek = Rot(P, "ek", 2, [128, 64], F32)
    er = Rot(P, "er", 2, [64, 128], F32)
    qt_ = Rot(P, "qt_", 2, [128, 64], BF16)
    kt_ = Rot(P, "kt_", 2, [128, 64], BF16)
    qh_ = Rot(P, "qh_", 2, [128, 64], BF16)
    kb_ = Rot(P, "kb_", 2, [64, 128], BF16)
    atb = Rot(P, "atb", 2, [64, 64], BF16)
    hst = Rot(P, "hst", 2, [64, 2], F32)
    htmp = Rot(P, "htmp", 2, [64, 128], F32)
    banks = [Buf(P.psum(f"bank{i}", [128, 512], F32), f"bank{i}") for i in range(8)]
    for bk in banks:
        P.exclusive.add(bk.key)
    pGG, pR, pA, pO, pSN = banks[0], banks[1], banks[2], banks[3], banks[4]
    kGG, kSN, kR, kO, kA = pGG.key, pSN.key, pR.key, pO.key, pA.key

    def hgrn_head(hd):
        h.memset("pool", Sf[hd].ap, 0.0, [Sf[hd]])
        sb = Sb[hd].next()
        h.memset("pool", sb.ap, 0.0, [sb])
        for g in range(NCH // GRP):
            t0 = g * GRP * CH
            q_ = gq.next(); k_ = gk.next(); lf = glf.next(); kk = gkk.next(); gt = ggt.next(); v_ = gv.next()
            h.dma("sp", q_.ap, hqT_d.ap()[hd, :, t0:t0 + 512], [], [q_], chan=q_.key)
            h.dma("sp", k_.ap, hkT_d.ap()[hd, :, t0:t0 + 512], [], [k_], chan=k_.key)
            for buf, dd in ((lf, hlogf_d), (kk, hkk_d), (gt, hgate_d), (v_, hv_d)):
                h.dma("sp", buf.ap, dd.ap()[hd, t0:t0 + 512, :].rearrange("(c s) k -> s c k", s=CH), [], [buf], chan=buf.key)
            yo = yst.next()
            for c in range(GRP):
                qc = q_.ap[:, c * CH:(c + 1) * CH]; kc = k_.ap[:, c * CH:(c + 1) * CH]
                h.mm(pGG.ap[:, 0:128], lf.ap[:, c, :], ucat.ap, True, True, [lf, ucat], [kGG])
                h.mm(pR.ap[0:64, 0:128], lmat.ap, lf.ap[:, c, :], True, True, [lf, lmat], [kR])
                e1 = eg.next(); e2 = ek.next(); e3 = er.next()
                h.act(e1.ap, pGG.ap[:, 0:128], AF.Exp, [kGG], [e1])
                h.act(e2.ap, pGG.ap[:, 64:128], AF.Exp, [kGG], [e2], scale=-1.0)
                h.act(e3.ap, pR.ap[0:64, 0:128], AF.Exp, [kR], [e3])
                qt = qt_.next(); kt = kt_.next(); qh = qh_.next(); kb = kb_.next()
                h.tt("dve", qt.ap, qc, e1.ap[:, 64:128], ALU.mult, [q_, e1], [qt])
                h.tt("pool", kt.ap, kc, e2.ap, ALU.mult, [k_, e2], [kt])
                h.tt("dve", qh.ap, qc, e1.ap[:, 0:64], ALU.mult, [q_, e1], [qh])
                h.tt("pool", kb.ap, kk.ap[:, c, :], e3.ap, ALU.mult, [kk, e3], [kb])
                h.mm(pA.ap[0:64, 0:64], kt.ap, qt.ap, True, True, [kt, qt], [kA])
                at = atb.next()
                h.tt("dve", at.ap, pA.ap[0:64, 0:64], ucat.ap[:, 0:64], ALU.mult, [kA, ucat], [at])
                h.mm(pO.ap[0:64, 0:128], at.ap, v_.ap[:, c, :], True, False, [at, v_], [kO])
                h.mm(pO.ap[0:64, 0:128], qh.ap, sb.ap, False, True, [qh, sb], [kO])
                h.mm(pSN.ap[:, 0:128], kb.ap, v_.ap[:, c, :], True, True, [kb, v_], [kSN])
                h.stt("dve", Sf[hd].ap, Sf[hd].ap, e1.ap[:, 63:64], pSN.ap[:, 0:128], ALU.mult, ALU.add, [Sf[hd], e1, kSN], [Sf[hd]])
                sb = Sb[hd].next()
                h.cp("pool", sb.ap, Sf[hd].ap, [Sf[hd]], [sb])
                st = hst.next(); tm = htmp.next(); tm2 = htmp.next()
                h.memset("pool", st.ap, 0.0, [st])
                h.act(tm.ap, pO.ap[0:64, 0:128], AF.Square, [kO, st], [tm, st], accum=st.ap[:, 0:1])
                h.ts("dve", st.ap[:, 0:1], st.ap[:, 0:1], 1.0 / 128, EPS, ALU.mult, ALU.add, [st], [st])
                h.act(st.ap[:, 0:1], st.ap[:, 0:1], AF.Sqrt, [st], [st])
                h.recip(st.ap[:, 0:1], st.ap[:, 0:1], [st], [st])
                h.stt("dve", tm2.ap, pO.ap[0:64, 0:128], st.ap[:, 0:1], ng.ap[:, hd, :], ALU.mult, ALU.mult, [kO, st, ng], [tm2])
                h.tt("pool", yo.ap[:, c, :], tm2.ap, gt.ap[:, c, :], ALU.mult, [tm2, gt], [yo])
            h.dma("pool", yb_d.ap()[t0:t0 + 512, hd * 128:(hd + 1) * 128].rearrange("(c s) k -> s c k", s=CH), yo.ap, [yo], ["yb_d"], chan=yo.key)

    aq = Rot(P, "aq", 2, [128, S], BF16)
    aqr = Rot(P, "aqr", 2, [64, S], BF16)
    akn = Rot(P, "akn", 2, [128, S], BF16)
    akr = Buf(P.sbuf("akr", [64, S], BF16), "akr")
    av = Rot(P, "av", 2, [128, 32, 128], BF16)
    pt_ = Rot(P, "pt_", 3, [128, 512], BF16)
    rin = Rot(P, "rin", 2, [128, 512], F32)
    oo = Rot(P, "oo", 2, [128, 512], BF16)
    class BRot:
        def __init__(self, bufs):
            self.bufs = bufs; self.i = 0
        def next(self):
            b_ = self.bufs[self.i % len(self.bufs)]; self.i += 1; return b_
    stps = BRot(banks[0:2]); otps = BRot(banks[2:4]); rsps = BRot(banks[4:6])
    h.dma("sp", akr.ap, KrT_d.ap(), [], [akr], chan="akr")
    SCALE = 192 ** -0.5

    def attn_head(hd):
        q_ = aq.next(); qr = aqr.next(); kn = akn.next(); v_ = av.next()
        h.dma("sp", q_.ap, QT_d.ap()[hd, 0:128, :], [], [q_], chan=q_.key)
        h.dma("sp", qr.ap, QT_d.ap()[hd, 128:192, :], [], [qr], chan=qr.key)
        h.dma("sp", kn.ap, KnT_d.ap()[hd], [], [kn], chan=kn.key)
        h.dma("sp", v_.ap, V_d.ap()[hd], [], [v_], chan=v_.key)
        for i in range(8):
            qs = slice(i * 512, (i + 1) * 512)
            ot = otps.next(); rs = rsps.next()
            nk = 4 * i + 4
            for j in range(nk):
                ks = slice(j * 128, (j + 1) * 128)
                st = stps.next()
                h.mm(st.ap, kn.ap[:, ks], q_.ap[:, qs], True, False, [kn, q_], [st])
                h.mm(st.ap, akr.ap[:, ks], qr.ap[:, qs], False, True, [akr, qr], [st])
                pt = pt_.next()
                h.act(pt.ap, st.ap, AF.Exp, [st], [pt], scale=SCALE)
                if j >= 4 * i:
                    m = j - 4 * i
                    h.tt("dve", pt.ap, pt.ap, cmask.ap[:, m, :], ALU.mult, [pt, cmask], [pt])
                h.mm(ot.ap, v_.ap[:, j, :], pt.ap, j == 0, j == nk - 1, [v_, pt], [ot])
                h.mm(rs.ap, onesb.ap, pt.ap, j == 0, j == nk - 1, [onesb, pt], [rs])
            ri = rin.next(); o = oo.next()
            h.recip(ri.ap, rs.ap, [rs], [ri])
            h.tt("dve", o.ap, ot.ap, ri.ap, ALU.mult, [ot, ri], [o])
            h.dma("pool", ycT_d.ap()[hd * 128:(hd + 1) * 128, qs], o.ap, [o], ["ycT_d"], chan=o.key)

    import os
    mode = os.environ.get("BMODE", "all")
    if mode in ("all", "hgrn"):
        for hd in range(2):
            hgrn_head(hd)
    if mode in ("all", "attn"):
        for hd in range(4):
            attn_head(hd)
    P.wait_all("sp", ["yb_d", "ycT_d"])
    P.wait_all("pool", ["yb_d", "ycT_d"])
    return P


def prep_B_consts(inp, l):
    import ml_dtypes
    c = {}
    s = np.arange(64)
    U = (s[:, None] <= s[None, :]).astype(np.float32)
    Um = U - U[:, 31:32]
    c["b_ucat"] = np.ascontiguousarray(np.concatenate([U, Um], 1))
    c["b_lmat"] = (s[:, None] > s[None, :]).astype(np.float32)
    k = np.arange(128)[:, None]; q = np.arange(512)[None, :]
    c["b_cmask"] = np.ascontiguousarray(np.stack([(128 * m + k <= q) for m in range(4)], 1).astype(np.float32).astype(ml_dtypes.bfloat16))
    c["b_onesb"] = np.ones((128, 128), ml_dtypes.bfloat16)
    return c


NT = 16
ALPHA = (2 * 4) ** 0.25
NBLK = 64


def build_C():
    P = Prog(); h = H(P)
    yT_d = P.dram("c_yT", [NT, 128, 16, 128], BF16, "ExternalInput")
    x_d = P.dram("c_x", [NTOK, D], F32, "ExternalInput")
    wout_d = P.dram("c_wout", [D, D], F32, "ExternalInput")
    ln1_d = P.dram("c_ln1", [128, 2, D], F32, "ExternalInput")
    ln2_d = P.dram("c_ln2", [128, 2, D], F32, "ExternalInput")
    wr_d = P.dram("c_wr", [D, 36], F32, "ExternalInput")
    br_d = P.dram("c_br", [128, 36], F32, "ExternalInput")
    wg_d = P.dram("c_wg", [32, D, 512], F32, "ExternalInput")
    wu_d = P.dram("c_wu", [32, D, 512], F32, "ExternalInput")
    wd_d = P.dram("c_wd", [32, 512, D], F32, "ExternalInput")
    ident_d = P.dram("c_ident", [128, 128], F32, "ExternalInput")
    identb_d = P.dram("c_identb", [128, 128], BF16, "ExternalInput")
    ustr_d = P.dram("c_ustr", [128, 128], BF16, "ExternalInput")
    onesb_d = P.dram("c_onesb", [128, 128], BF16, "ExternalInput")
    blkst_d = P.dram("c_blkst", [128, NBLK], F32, "ExternalInput")
    pq_d = P.dram("c_pq", [128, 4], F32, "ExternalInput")
    x1_d = P.dram("c_x1", [NTOK, D], F32, "Internal")
    x1b_d = P.dram("c_x1b", [NTOK, D], BF16, "Internal")
    xs_d = P.dram("c_xs", [NBLK * 128, D], BF16, "Internal")
    ys_d = P.dram("c_ys", [NBLK * 128, D], F32, "Internal")
    xo_d = P.dram("xo", [NTOK, D], F32, "ExternalOutput")

    arena0 = P.sbuf("arena0", [128, 32768], BF16)
    arena1 = P.sbuf("arena1", [128, 24576], BF16)
    wst = Rot(P, "cwst", 3, [128, 4, 512], F32)
    lngb = Buf(P.sbuf("c_lngb", [128, 2, D], F32), "c_lngb")
    yTr = Rot(P, "yTr", 2, [128, 16, 128], BF16)
    x1br = Rot(P, "x1br", 2, [128, D], BF16)
    x1T = Buf(P.sbuf("x1T", [128, 16, 128], F32), "x1T")
    wr = Buf(P.sbuf("wr", [128, 16, 36], F32), "wr")
    br = Buf(P.sbuf("br", [128, 36], F32), "br")
    ident = Buf(P.sbuf("identf", [128, 128], F32), "identf")
    identb = Buf(P.sbuf("identb", [128, 128], BF16), "identb")
    ustr = Buf(P.sbuf("ustr", [128, 128], BF16), "ustr")
    onesb = Buf(P.sbuf("conesb", [128, 128], BF16), "conesb")
    blkst = Buf(P.sbuf("blkst", [128, NBLK], F32), "blkst")
    pq = Buf(P.sbuf("pq", [128, 4], F32), "pq")
    widxf = Buf(P.sbuf("widxf", [128, NBLK], F32), "widxf")
    widx = Buf(P.sbuf("widx", [128, 4, NBLK], I32), "widx")
    lg = Buf(P.sbuf("lg", [128, 36], F32), "lg")
    rs = Buf(P.sbuf("rs", [128, 64], F32), "rs")
    OHall = Buf(P.sbuf("OHall", [128, NT, 2, 32], F32), "OHall")
    gates = Buf(P.sbuf("gates", [128, NT, 2], F32), "gates")
    cntb = Buf(P.sbuf("cntb", [128, NT, 32], BF16), "cntb")
    stat = Rot(P, "cstat", 2, [128, 2], F32)
    tot = Buf(P.sbuf("tot", [128, 32], F32), "tot")
    scA = Buf(P.sbuf("scA", [128, 32], F32), "scA")
    scB = Buf(P.sbuf("scB", [128, 32], F32), "scB")
    padded = Buf(P.sbuf("padded", [128, 32], F32), "padded")
    pstart = Buf(P.sbuf("pstart", [128, 32], F32), "pstart")
    sci = Buf(P.sbuf("sci", [128, 64], I32), "sci")
    bacc = Buf(P.sbuf("bacc", [128, NBLK], F32), "bacc")
    bexp = Buf(P.sbuf("bexp", [128, NBLK], I32), "bexp")
    posf = Buf(P.sbuf("posf", [128, NT * 2], F32), "posf")
    posi = Buf(P.sbuf("posi_c", [128, NT * 2], I32), "posi_c")
    banks = [Buf(P.psum(f"cbank{i}", [128, 512], F32), f"cbank{i}") for i in range(8)]
    for bk in banks:
        P.exclusive.add(bk.key)

    def view(arena, off, nel, key, pat=None, **kw):
        ap = arena[:, off:off + nel]
        if pat:
            ap = ap.rearrange(pat, **kw)
        return Buf(ap, key)

    def f32view(arena, off_b, n_f32, key, pat=None, **kw):
        ap = arena[:, off_b:off_b + 2 * n_f32].bitcast(F32)
        if pat:
            ap = ap.rearrange(pat, **kw)
        return Buf(ap, key)

    for b_, d_ in ((wr, wr_d.ap().rearrange("(c p) n -> p c n", p=128)), (br, br_d.ap()), (ident, ident_d.ap()), (identb, identb_d.ap()),
                   (ustr, ustr_d.ap()), (onesb, onesb_d.ap()), (blkst, blkst_d.ap()), (pq, pq_d.ap()), (lngb, ln1_d.ap())):
        h.dma("sp", b_.ap, d_, [], [b_], chan="const")

    cast_i = [0]

    def cast_eng():
        cast_i[0] += 1
        return ("dve", "pool")[cast_i[0] % 2]

    def layer_norm(z, junk, out_ap, out_writes):
        st = stat.next()
        h.memset("pool", st.ap, 0.0, [st])
        h.red("dve", st.ap[:, 0:1], z.ap, ALU.add, [z, st], [st])
        h.ts("dve", st.ap[:, 0:1], st.ap[:, 0:1], 1.0 / D, None, ALU.mult, None, [st], [st])
        h.ts("dve", z.ap, z.ap, st.ap[:, 0:1], None, ALU.subtract, None, [z, st], [z])
        h.act(junk.ap, z.ap, AF.Square, [z, st], [junk, st], accum=st.ap[:, 1:2])
        h.ts("dve", st.ap[:, 1:2], st.ap[:, 1:2], 1.0 / D, EPS, ALU.mult, ALU.add, [st], [st])
        h.act(st.ap[:, 1:2], st.ap[:, 1:2], AF.Sqrt, [st], [st])
        h.recip(st.ap[:, 1:2], st.ap[:, 1:2], [st], [st])
        h.stt("dve", junk.ap, z.ap, st.ap[:, 1:2], lngb.ap[:, 0, :], ALU.mult, ALU.mult, [z, st, lngb], [junk])
        h.tt("pool", out_ap, junk.ap, lngb.ap[:, 1, :], ALU.add, [junk, lngb], out_writes)

    wout_bf = view(arena0, 0, 32768, "wout_bf", "p (c n) -> p c n", n=D)
    wout_v = wout_d.ap().rearrange("(c p) n -> p c n", p=128)
    for q in range(4):
        for n in range(4):
            st = wst.next()
            h.dma("sp", st.ap, wout_v[:, q * 4:(q + 1) * 4, n * 512:(n + 1) * 512], [], [st], chan=st.key)
            h.cp(cast_eng(), wout_bf.ap[:, q * 4:(q + 1) * 4, n * 512:(n + 1) * 512], st.ap, [st], [wout_bf])
    xr = [f32view(arena1, i * 4096, 2048, f"xr{i}") for i in range(2)]
    zb = f32view(arena1, 8192, 2048, "zb")
    jk = f32view(arena1, 12288, 2048, "jk")
    for tt in range(NT):
        tok = tt * 128
        yt = yTr.next(); xt = xr[tt % 2]
        h.dma("sp", yt.ap, yT_d.ap()[tt], [], [yt], chan=yt.key)
        h.dma("sp", xt.ap, x_d.ap()[tok:tok + 128, :], [], [xt], chan=xt.key)
        for n in range(4):
            for c in range(16):
                h.mm(banks[n].ap, yt.ap[:, c, :], wout_bf.ap[:, c, n * 512:(n + 1) * 512], c == 0, c == 15, [yt, wout_bf], [banks[n]])
        for n in range(4):
            h.stt("dve", zb.ap[:, n * 512:(n + 1) * 512], xt.ap[:, n * 512:(n + 1) * 512], ALPHA, banks[n].ap, ALU.mult, ALU.add, [xt, banks[n]], [zb])
        layer_norm(zb, jk, xt.ap, [xt])
        h.dma("pool", x1_d.ap()[tok:tok + 128, :], xt.ap, [xt], [("x1_d", tt)], chan="x1st" + xt.key)
        xb = x1br.next()
        h.cp("pool", xb.ap, xt.ap, [xt], [xb])
        h.dma("pool", x1b_d.ap()[tok:tok + 128, :], xb.ap, [xb], [("x1b_d", tt)], chan=xb.key)
        for c4 in range(4):
            pt = banks[4 + c4 % 2]
            for i in range(4):
                c = c4 * 4 + i
                h.tr(pt.ap[:, i * 128:(i + 1) * 128], xt.ap[:, c * 128:(c + 1) * 128], ident.ap, [xt, ident], [pt])
            h.cp("act", x1T.ap[:, c4 * 4:(c4 + 1) * 4, :].rearrange("p c t -> p (c t)"), pt.ap, [pt], [x1T])
        lp = banks[6]
        for c in range(16):
            h.mm(lp.ap[:, 0:36], x1T.ap[:, c, :], wr.ap[:, c, :], c == 0, c == 15, [x1T, wr], [lp])
        h.tt("dve", lg.ap, lp.ap[:, 0:36], br.ap, ALU.add, [lp, br], [lg])
        R = [rs, lg]
        s = rs.ap
        h.memset("pool", s, 0.0, [rs])
        h.red("dve", s[:, 0:1], lg.ap[:, 0:4], ALU.max, R, [rs])
        h.ts("dve", s[:, 4:8], lg.ap[:, 0:4], s[:, 0:1], None, ALU.is_equal, None, R, [rs])
        h.ts("dve", s[:, 8:12], lg.ap[:, 0:4], s[:, 0:1], None, ALU.subtract, None, R, [rs])
        h.act(s[:, 8:12], s[:, 8:12], AF.Exp, [rs], [rs], accum=s[:, 1:2])
        h.recip(s[:, 2:3], s[:, 1:2], [rs], [rs])
        h.ts("dve", s[:, 12:20], lg.ap[:, 4:12], s[:, 4:5], None, ALU.mult, None, R, [rs])
        for g in range(1, 4):
            h.stt("dve", s[:, 12:20], lg.ap[:, 4 + 8 * g:12 + 8 * g], s[:, 4 + g:5 + g], s[:, 12:20], ALU.mult, ALU.add, R, [rs])
        h.red("dve", s[:, 20:21], s[:, 12:20], ALU.max, [rs], [rs])
        h.ts("dve", s[:, 24:32], s[:, 12:20], s[:, 20:21], None, ALU.is_equal, None, [rs], [rs])
        h.stt("dve", s[:, 32:40], s[:, 24:32], -1e30, s[:, 12:20], ALU.mult, ALU.add, [rs], [rs])
        h.red("dve", s[:, 21:22], s[:, 32:40], ALU.max, [rs], [rs])
        h.ts("dve", s[:, 40:48], s[:, 32:40], s[:, 21:22], None, ALU.is_equal, None, [rs], [rs])
        h.tt("dve", s[:, 22:23], s[:, 21:22], s[:, 20:21], ALU.subtract, [rs], [rs])
        h.act(s[:, 22:23], s[:, 22:23], AF.Exp, [rs], [rs])
        h.ts("dve", s[:, 23:24], s[:, 22:23], 1.0, None, ALU.add, None, [rs], [rs])
        h.recip(s[:, 23:24], s[:, 23:24], [rs], [rs])
        h.tt("dve", gates.ap[:, tt, 0:1], s[:, 2:3], s[:, 23:24], ALU.mult, [rs, gates], [gates])
        h.tt("dve", gates.ap[:, tt, 1:2], gates.ap[:, tt, 0:1], s[:, 22:23], ALU.mult, [rs, gates], [gates])
        for g in range(4):
            h.ts("dve", OHall.ap[:, tt, 0, g * 8:(g + 1) * 8], s[:, 24:32], s[:, 4 + g:5 + g], None, ALU.mult, None, [rs, OHall], [OHall])
            h.ts("dve", OHall.ap[:, tt, 1, g * 8:(g + 1) * 8], s[:, 40:48], s[:, 4 + g:5 + g], None, ALU.mult, None, [rs, OHall], [OHall])
        h.tt("dve", cntb.ap[:, tt, :], OHall.ap[:, tt, 0, :], OHall.ap[:, tt, 1, :], ALU.add, [OHall, cntb], [cntb])

    tp = banks[7]
    for t in range(NT):
        h.mm(tp.ap[:, 0:32], onesb.ap, cntb.ap[:, t, :], t == 0, t == NT - 1, [onesb, cntb], [tp])
    h.cp("dve", tot.ap, tp.ap[:, 0:32], [tp], [tot])
    h.ts("dve", scA.ap, tot.ap, 127.0, None, ALU.add, None, [tot], [scA])
    h.ts("dve", scB.ap, scA.ap, 1.0 / 128, None, ALU.mult, None, [scA], [scB])
    h.cp("dve", sci.ap[:, 0:32], scB.ap, [scB], [sci])
    h.cp("dve", scB.ap, sci.ap[:, 0:32], [sci], [scB])
    h.ts("dve", padded.ap, scB.ap, 128.0, None, ALU.mult, None, [scB], [padded])
    h.tt("dve", padded.ap, padded.ap, scA.ap, ALU.is_gt, [padded, scA], [padded])
    h.tt("dve", scB.ap, scB.ap, padded.ap, ALU.subtract, [scB, padded], [scB])
    h.ts("dve", padded.ap, scB.ap, 128.0, None, ALU.mult, None, [scB], [padded])
    h.cp("dve", scA.ap, padded.ap, [padded], [scA])
    A_, B_ = scA, scB
    for sh in (1, 2, 4, 8, 16):
        h.cp("dve", B_.ap[:, 0:sh], A_.ap[:, 0:sh], [A_], [B_])
        h.tt("dve", B_.ap[:, sh:32], A_.ap[:, sh:32], A_.ap[:, 0:32 - sh], ALU.add, [A_], [B_])
        A_, B_ = B_, A_
    pend = A_
    h.tt("dve", pstart.ap, pend.ap, padded.ap, ALU.subtract, [pend, padded], [pstart])
    h.memset("pool", bacc.ap, 0.0, [bacc])
    for e in range(32):
        h.stt("dve", bacc.ap, blkst.ap, pend.ap[:, e:e + 1], bacc.ap, ALU.is_ge, ALU.add, [blkst, pend, bacc], [bacc])
    h.ts("dve", bacc.ap, bacc.ap, 31.0, None, ALU.min, None, [bacc], [bacc])
    h.cp("dve", bexp.ap, bacc.ap, [bacc], [bexp])
    for q in range(4):
        h.ts("dve", widxf.ap, bacc.ap, 512.0, pq.ap[:, q:q + 1], ALU.mult, ALU.add, [bacc, pq], [widxf])
        h.cp("dve", widx.ap[:, q, :], widxf.ap, [widxf], [widx])
    for tt in range(NT):
        rp = banks[tt % 2]
        for t in range(tt):
            h.mm(rp.ap[:, 0:32], onesb.ap, cntb.ap[:, t, :], t == 0, False, [onesb, cntb], [rp])
        h.mm(rp.ap[:, 0:32], ustr.ap, cntb.ap[:, tt, :], tt == 0, True, [ustr, cntb], [rp])
        h.tt("dve", B_.ap, rp.ap[:, 0:32], pstart.ap, ALU.add, [rp, pstart], [B_])
        for k in range(2):
            h.tt("dve", padded.ap, B_.ap, OHall.ap[:, tt, k, :], ALU.mult, [B_, OHall], [padded])
            h.red("dve", posf.ap[:, tt * 2 + k:tt * 2 + k + 1], padded.ap, ALU.add, [padded, posf], [posf])
    h.cp("dve", posi.ap, posf.ap, [posf], [posi])
    for tt in range(NT):
        xb = x1br.next()
        h.dma("sp", xb.ap, x1b_d.ap()[tt * 128:(tt + 1) * 128, :], [("x1b_d", tt)], [xb], chan=xb.key)
        for k in range(2):
            col = tt * 2 + k
            def scat(e, xb=xb, col=col):
                return e.indirect_dma_start(out=xs_d.ap(), out_offset=bass.IndirectOffsetOnAxis(ap=_r(posi.ap)[:, col:col + 1], axis=0),
                                            in_=_r(xb.ap), in_offset=None)
            P.dma("pool", None, None, reads=keys([xb, posi]), writes=[("xs_d", col)], chan=f"scat{col % 4}", indirect=scat)

    P.barrier()
    wsets = []
    for i, ar in enumerate((arena0, arena1)):
        wsets.append((view(ar, 0, 8192, f"wg{i}", "p (c n) -> p c n", n=512),
                      view(ar, 8192, 8192, f"wu{i}", "p (c n) -> p c n", n=512),
                      view(ar, 16384, 8192, f"wd{i}", "p (c n) -> p c n", n=2048)))
    xbl = [view(arena0, 24576 + i * 2048, 2048, f"xbl{i}") for i in range(2)]
    xsT = [view(arena0, 28672 + i * 2048, 2048, f"xsT{i}", "p (c t) -> p c t", t=128) for i in range(2)]
    hsg = Rot(P, "hsg", 2, [128, 512], F32)
    hTb = Rot(P, "hTb", 2, [128, 512], BF16)
    ysb = [Buf(x1T.ap.rearrange("p c t -> p (c t)"), "x1T")]
    h.dma("sp", lngb.ap, ln2_d.ap(), [], [lngb], chan="const2")
    wg_v = wg_d.ap().rearrange("e (p q c) n -> (e p q) (c n)", q=4, c=4)
    wu_v = wu_d.ap().rearrange("e (p q c) n -> (e p q) (c n)", q=4, c=4)
    wd_v = wd_d.ap().rearrange("e (p q) n -> (e p q) n", q=4)
    cast3 = [0]

    def cast3_eng():
        cast3[0] += 1
        return ("dve", "act")[cast3[0] % 2]

    for b in range(NBLK):
        wgb, wub, wdb = wsets[b % 2]
        for wv, wbuf, isdown in ((wg_v, wgb, False), (wu_v, wub, False), (wd_v, wdb, True)):
            for q in range(4):
                st = wst.next()
                def ld(e, st=st, wv=wv, q=q, b=b):
                    return e.indirect_dma_start(out=_r(st.ap).rearrange("p c n -> p (c n)"), out_offset=None, in_=wv,
                                                in_offset=bass.IndirectOffsetOnAxis(ap=_r(widx.ap)[:, q, b:b + 1], axis=0))
                P.dma("pool", None, None, reads=keys([widx]), writes=keys([st]), chan=st.key, indirect=ld)
                if isdown:
                    h.cp(cast3_eng(), wbuf.ap[:, q, :], st.ap.rearrange("p c n -> p (c n)"), [st], [wbuf])
                else:
                    h.cp(cast3_eng(), wbuf.ap[:, q * 4:(q + 1) * 4, :], st.ap, [st], [wbuf])
        xb = xbl[b % 2]; xt_ = xsT[b % 2]
        h.dma("sp", xb.ap, xs_d.ap()[b * 128:(b + 1) * 128, :], [("xs_d", c) for c in range(NT * 2)], [xb], chan=xb.key)
        for half in range(2):
            pt = banks[6 + half]
            ptb = pt.ap.bitcast(BF16)
            for i in range(8):
                c = half * 8 + i
                h.tr(ptb[:, i * 128:(i + 1) * 128], xb.ap.rearrange("s (p c) -> s c p", c=16)[:, c, :], identb.ap, [xb, identb], [pt])
            h.cp(("act", "dve")[half], xt_.ap[:, half * 8:(half + 1) * 8, :].rearrange("p c t -> p (c t)"), ptb, [pt], [xt_])
        hg = banks[4]; hu = banks[5]
        for f in range(4):
            for c in range(16):
                h.mm(hg.ap[:, f * 128:(f + 1) * 128], wgb.ap.rearrange("p c (f4 cf) -> p c cf f4", cf=4)[:, c, f, :], xt_.ap[:, c, :], c == 0, c == 15, [wgb, xt_], [hg])
        for f in range(4):
            for c in range(16):
                h.mm(hu.ap[:, f * 128:(f + 1) * 128], wub.ap.rearrange("p c (f4 cf) -> p c cf f4", cf=4)[:, c, f, :], xt_.ap[:, c, :], c == 0, c == 15, [wub, xt_], [hu])
        sg = hsg.next(); hT = hTb.next()
        h.act(sg.ap, hg.ap, AF.Silu, [hg], [sg])
        h.tt("dve", hT.ap, sg.ap, hu.ap, ALU.mult, [sg, hu], [hT])
        for n in range(4):
            for f in range(4):
                h.mm(banks[n].ap, hT.ap[:, f * 128:(f + 1) * 128], wdb.ap[:, f, n * 512:(n + 1) * 512], f == 0, f == 3, [hT, wdb], [banks[n]])
        yb_ = ysb[0]
        for n in range(4):
            h.cp(("act", "dve")[n % 2], yb_.ap[:, n * 512:(n + 1) * 512], banks[n].ap, [banks[n]], [yb_])
        h.dma("pool", ys_d.ap()[b * 128:(b + 1) * 128, :], yb_.ap, [yb_], [("ys_d", b)], chan="ysst")

    P.barrier()
    y0 = [f32view(arena0, i * 8192, 2048, f"y0_{i}") for i in range(2)]
    y1 = [f32view(arena0, 4096 + i * 8192, 2048, f"y1_{i}") for i in range(2)]
    x1r = [f32view(arena1, i * 4096, 2048, f"x1r{i}") for i in range(2)]
    jk2 = f32view(arena1, 8192, 2048, "jk2")
    ob = [f32view(arena1, 12288 + i * 4096, 2048, f"ob{i}") for i in range(2)]
    ys_reads = [("ys_d", b) for b in range(NBLK)]
    for tt in range(NT):
        tok = tt * 128
        a0 = y0[tt % 2]; a1 = y1[tt % 2]; xt = x1r[tt % 2]; o = ob[tt % 2]
        for k, dst in ((0, a0), (1, a1)):
            col = tt * 2 + k
            def gath(e, dst=dst, col=col):
                return e.indirect_dma_start(out=_r(dst.ap), out_offset=None, in_=ys_d.ap(),
                                            in_offset=bass.IndirectOffsetOnAxis(ap=_r(posi.ap)[:, col:col + 1], axis=0))
            P.dma("pool", None, None, reads=keys([posi]) + ys_reads, writes=keys([dst]), chan=dst.key, indirect=gath)
        h.dma("sp", xt.ap, x1_d.ap()[tok:tok + 128, :], [("x1_d", tt)], [xt], chan=xt.key)
        h.ts("dve", a0.ap, a0.ap, gates.ap[:, tt, 0:1], None, ALU.mult, None, [a0, gates], [a0])
        h.stt("dve", a0.ap, a1.ap, gates.ap[:, tt, 1:2], a0.ap, ALU.mult, ALU.add, [a1, gates, a0], [a0])
        h.stt("dve", a0.ap, xt.ap, ALPHA, a0.ap, ALU.mult, ALU.add, [xt, a0], [a0])
        layer_norm(a0, jk2, o.ap, [o])
        h.dma("sp", xo_d.ap()[tok:tok + 128, :], o.ap, [o], [("xo", tt)], chan=o.key)
    outs = [("xo", tt) for tt in range(NT)]
    P.wait_all("sp", outs)
    P.wait_all("pool", outs)
    return P


def prep_C_consts(inp, l):
    import ml_dtypes
    bf = ml_dtypes.bfloat16
    c = {}
    c["c_wout"] = np.ascontiguousarray(inp["w_out"][l])
    c["c_ln1"] = np.ascontiguousarray(np.broadcast_to(np.stack([inp["ln1_g"][l], inp["ln1_b"][l]])[None], (128, 2, 2048)))
    c["c_ln2"] = np.ascontiguousarray(np.broadcast_to(np.stack([inp["ln2_g"][l], inp["ln2_b"][l]])[None], (128, 2, 2048)))
    c["c_wr"] = np.ascontiguousarray(np.concatenate([inp["router_group_w"][l], inp["router_expert_w"][l]], 1))
    c["c_br"] = np.ascontiguousarray(np.broadcast_to(np.concatenate([inp["router_group_b"][l], inp["router_expert_b"][l]])[None], (128, 36)))
    c["c_wg"] = inp["expert_w_gate"][l]
    c["c_wu"] = inp["expert_w_up"][l]
    c["c_wd"] = inp["expert_w_down"][l]
    c["c_ident"] = np.eye(128, dtype=np.float32)
    c["c_identb"] = np.eye(128, dtype=np.float32).astype(bf)
    n = np.arange(128)
    c["c_ustr"] = (n[:, None] < n[None, :]).astype(np.float32).astype(bf)
    c["c_onesb"] = np.ones((128, 128), bf)
    c["c_pq"] = (np.arange(128, dtype=np.float32)[:, None] * 4 + np.arange(4, dtype=np.float32)[None, :])
    c["c_blkst"] = np.ascontiguousarray(np.broadcast_to((np.arange(NBLK, dtype=np.float32) * 128)[None], (128, NBLK)))
    return c


import numpy as np
NTOK = 2048

def prep_A_consts(inp, l):
    c = {}
    c["w_in"] = np.ascontiguousarray(inp["w_in"][l])
    c["w_uq"] = np.ascontiguousarray(inp["mla_w_uq"][l])
    c["w_ukv"] = np.ascontiguousarray(inp["mla_w_ukv"][l])
    c["lngb"] = np.ascontiguousarray(np.broadcast_to(np.stack([inp["sgu_ln_g"][l], inp["sgu_ln_b"][l]])[None], (128, 2, 512)))
    c["wsT"] = np.ascontiguousarray(inp["sgu_ws"][l].transpose(2, 0, 1))
    s = np.arange(128)
    c["tri"] = (s[:, None] <= s[None, :]).astype(np.float32)
    c["sgub"] = np.ascontiguousarray(inp["sgu_b"][l].T)
    lg = inp["hgrn_lb_logits"]
    c["lbT"] = np.ascontiguousarray(lg.reshape(4, 4, 128).transpose(2, 0, 1))
    c["lbB"] = np.ascontiguousarray(np.broadcast_to(lg[None], (128, 4, 512)))
    m = np.array([1.0 if 1 <= j <= l else 0.0 for j in range(4)], np.float32)
    c["lmask"] = np.ascontiguousarray(np.broadcast_to(m[None], (128, 4)))
    c["qng"] = np.ascontiguousarray(inp["mla_qn_g"][l].reshape(4, 128).T)
    c["kvng"] = np.ascontiguousarray(inp["mla_kvn_g"][l].reshape(4, 128).T)
    inv = (1.0 / (np.float32(10000.0) ** (np.arange(0, 64, 2, dtype=np.float32) / np.float32(64)))).astype(np.float32)
    rc = np.zeros((64, 2), np.float32)
    rc[:, 0] = np.concatenate([inv, inv]) / np.float32(2 * np.pi)
    rc[:32, 1] = -1.0; rc[32:, 1] = 1.0
    c["ropec"] = rc
    c["ones"] = np.ones((128, 128), np.float32)
    return c

def prep_A_core(xflat, posflat, core):
    sl = slice(core * NTOK, (core + 1) * NTOK)
    return {"xT": np.ascontiguousarray(xflat[sl].T),
            "posr": np.ascontiguousarray(np.broadcast_to(posflat[sl][None], (64, NTOK)))}


_CACHE = {}


def _get_prog(name):
    if name not in _CACHE:
        P = {"A": build_A, "B": build_B, "C": build_C}[name]()
        _CACHE[name] = P.build()
    return _CACHE[name]


def _prep_yT(y):
    return np.ascontiguousarray(y.reshape(16, 128, 16, 128).transpose(0, 3, 2, 1))


def kernel(**inputs):
    inp = {k: np.asarray(v) for k, v in inputs.items()}
    x = np.ascontiguousarray(inp["x"], dtype=np.float32).reshape(-1, 2048)
    posflat = np.ascontiguousarray(inp["positions"]).astype(np.int32).reshape(-1)
    ncore = 8
    cores = list(range(ncore))
    ncA = _get_prog("A"); ncB = _get_prog("B"); ncC = _get_prog("C")
    xcur = x
    for l in range(4):
        cA = prep_A_consts(inp, l)
        maps = []
        for c in cores:
            m = dict(cA); m.update(prep_A_core(xcur, posflat, c)); maps.append(m)
        rA = run_bass_kernel_spmd(ncA, maps, core_ids=cores).results
        cB = prep_B_consts(inp, l)
        maps = []
        for b in range(4):
            r0, r1 = rA[2 * b], rA[2 * b + 1]
            catT = lambda k: np.concatenate([r0[k], r1[k]], axis=1)
            catR = lambda k: np.concatenate([r0[k], r1[k]], axis=0)
            hqT, hkT = catT("hqT"), catT("hkT")
            hlogf, hkk, hv, hgate = catR("hlogf"), catR("hkk"), catR("hv"), catR("hgate")
            QT, KnT, KrT, Vv = catT("QT"), catT("KnT"), catT("KrT"), catR("Vv")
            for j in range(2):
                hs = slice(j * 256, (j + 1) * 256)
                m = dict(cB)
                m["b_hqT"] = np.ascontiguousarray(hqT[hs].reshape(2, 128, 4096))
                m["b_hkT"] = np.ascontiguousarray(hkT[hs].reshape(2, 128, 4096))
                tm = lambda a: np.ascontiguousarray(a[:, hs].reshape(4096, 2, 128).transpose(1, 0, 2))
                m["b_hlogf"] = tm(hlogf); m["b_hkk"] = tm(hkk); m["b_hv"] = tm(hv); m["b_hgate"] = tm(hgate)
                m["b_ng"] = np.ascontiguousarray(np.broadcast_to(inp["hgrn_norm_g"][l][hs].reshape(1, 2, 128), (64, 2, 128))).astype(np.float32)
                m["b_QT"] = np.ascontiguousarray(QT.reshape(8, 192, 4096)[4 * j:4 * j + 4])
                m["b_KnT"] = np.ascontiguousarray(KnT.reshape(8, 128, 4096)[4 * j:4 * j + 4])
                m["b_KrT"] = np.ascontiguousarray(KrT)
                v4 = Vv.reshape(4096, 8, 128)[:, 4 * j:4 * j + 4].transpose(1, 0, 2)
                m["b_V"] = np.ascontiguousarray(v4.reshape(4, 32, 128, 128).transpose(0, 2, 1, 3))
                maps.append(m)
        rB = run_bass_kernel_spmd(ncB, maps, core_ids=cores).results
        cC = prep_C_consts(inp, l)
        maps = []
        for c in cores:
            b, half = divmod(c, 2)
            ts_ = slice(half * 2048, (half + 1) * 2048)
            ya = rA[c]["ya"]
            yb = np.concatenate([rB[2 * b]["yb"][ts_], rB[2 * b + 1]["yb"][ts_]], axis=1)
            yc = np.concatenate([rB[2 * b]["ycT"][:, ts_].T, rB[2 * b + 1]["ycT"][:, ts_].T], axis=1)
            y = np.concatenate([ya, yb, yc], axis=1)
            m = dict(cC)
            m["c_yT"] = _prep_yT(y)
            m["c_x"] = np.ascontiguousarray(xcur[c * 2048:(c + 1) * 2048])
            maps.append(m)
        rC = run_bass_kernel_spmd(ncC, maps, core_ids=cores).results
        xcur = np.concatenate([np.asarray(rC[c]["xo"], dtype=np.float32) for c in cores], axis=0)
    return xcur.reshape(4, 4096, 2048).astype(np.float32)
```

```python
import numpy as np
import concourse.bass as bass
import concourse.mybir as mybir
from concourse.bass_utils import run_bass_kernel_spmd

F32 = mybir.dt.float32
BF16 = mybir.dt.bfloat16
I32 = mybir.dt.int32
AF = mybir.ActivationFunctionType
ALU = mybir.AluOpType
AX = mybir.AxisListType

ENGS = ("pe", "act", "dve", "pool", "sp")
EPOCH = 16000
NSEMPOOL = 56
ARENA_BYTES = 204 * 1024


def _dsize(dt):
    return {F32: 4, BF16: 2, I32: 4}[dt]


class Prog:
    def __init__(self):
        self.nc = bass.Bass("TRN2", target_bir_lowering=False)
        self.ops = {e: [] for e in ENGS}
        self.cnt = {e: 0 for e in ENGS}
        self.known = {e: {} for e in ENGS}
        self.state = {}
        self.chan = {}
        self.semnames = []
        self.semset = set()
        self.tensors = []
        self.handles = {}
        self.n_waits = 0
        self.exclusive = set()
        self.arena_mode = False
        self.arena_off = 0
        self.arena_base = 0
        self.arena_peak = 0
        self.bank_i = 0
        self.chan_slot = {}

    def dram(self, name, shape, dtype, kind):
        t = self.nc.dram_tensor(name, list(shape), dtype, kind=kind)
        return t

    def enable_arena(self):
        self.arena_mode = True
        self.tensors.append(("sbuf", "ARENA", [128, ARENA_BYTES // 2], BF16))
        for i in range(8):
            self.tensors.append(("psum", f"BANK{i}", [128, 512], F32))

    def stage_reset(self, keep=False):
        self.barrier()
        if keep:
            self.arena_base = self.arena_off
        self.arena_off = self.arena_base
        self.bank_i = 0

    def sbuf(self, name, shape, dtype):
        if not self.arena_mode:
            self.tensors.append(("sbuf", name, list(shape), dtype))
            return _Lazy(self, name)
        n = 1
        for d_ in shape[1:]:
            n *= d_
        nb = n * _dsize(dtype)
        nb_al = (nb + 31) // 32 * 32
        off = self.arena_off
        self.arena_off += nb_al
        self.arena_peak = max(self.arena_peak, self.arena_off)
        assert self.arena_off <= ARENA_BYTES, f"SBUF arena overflow at {name}: {self.arena_off}"
        chain = [("idx", (slice(0, shape[0]), slice(off // 2, (off + nb) // 2)))]
        if dtype != BF16:
            chain.append(("bc", dtype))
        if len(shape) == 3:
            chain.append(("re", "p (a b) -> p a b", {"a": shape[1], "b": shape[2]}))
        elif len(shape) == 4:
            chain.append(("re", "p (a b c) -> p a b c", {"a": shape[1], "b": shape[2], "c": shape[3]}))
        return _Lazy(self, "ARENA", tuple(chain))

    def psum(self, name, shape, dtype=F32):
        if not self.arena_mode:
            self.tensors.append(("psum", name, list(shape), dtype))
            return _Lazy(self, name)
        assert list(shape) == [128, 512] and dtype == F32
        i = self.bank_i
        self.bank_i += 1
        assert i < 8, "out of PSUM banks"
        return _Lazy(self, f"BANK{i}")

    def _sem(self, name):
        if name not in self.semset:
            self.semset.add(name)
            self.semnames.append(name)
        return name

    def _eng_token(self, eng):
        self.cnt[eng] += 1
        c = self.cnt[eng] - 1
        ep, v = divmod(c, EPOCH)
        return (self._sem(f"c_{eng}_{ep}"), v + 1, 1)

    def _chan_token(self, chan):
        if self.arena_mode:
            if chan not in self.chan_slot:
                self.chan_slot[chan] = len(self.chan_slot) % NSEMPOOL
            chan = f"p{self.chan_slot[chan]}"
        self.chan[chan] = self.chan.get(chan, 0) + 1
        c = self.chan[chan] - 1
        ep, v = divmod(c, EPOCH // 16)
        return (self._sem(f"d_{chan}_{ep}"), (v + 1) * 16, 16)

    def _resolve(self, tok):
        name, val, step = tok
        if step == 16:
            chan, ep = name[2:].rsplit("_", 1)
            ep = int(ep)
            tot = self.chan[chan]
            per = EPOCH // 16
            cur_ep = (tot - 1) // per
            if ep == cur_ep:
                val = (tot - ep * per) * 16
            else:
                val = per * 16
        return name, val

    def _collect(self, eng, reads, writes):
        toks = []
        for k in reads:
            st = self.state.get(k)
            if st and st[0]:
                toks.append(st[0])
        for k in writes:
            st = self.state.get(k)
            if st:
                if st[0]:
                    toks.append(st[0])
                toks.extend(st[1].values())
        need = {}
        for t in toks:
            name, val = self._resolve(t)
            if name.startswith("c_pe_") and eng == "pe":
                continue
            if self.known[eng].get(name, 0) >= val:
                continue
            need[name] = max(need.get(name, 0), val)
        for n, v in need.items():
            self.known[eng][n] = v
        return list(need.items())

    def _update(self, tok, reads, writes):
        for k in reads:
            st = self.state.setdefault(k, [None, {}])
            st[1][tok[0]] = tok
        for k in writes:
            self.state[k] = [tok, {}]

    def _excl(self, reads, writes):
        if not self.exclusive:
            return reads, writes
        r2 = [k for k in reads if k not in self.exclusive]
        w2 = list(writes) + [k for k in reads if k in self.exclusive and k not in writes]
        return r2, w2

    def op(self, eng, fn, reads=(), writes=()):
        reads, writes = self._excl(reads, writes)
        waits = self._collect(eng, reads, writes)
        tok = self._eng_token(eng)
        self.n_waits += len(waits)
        self.ops[eng].append((waits, fn, (tok[0], 1)))
        self._update(tok, reads, writes)

    def dma(self, q, out, in_, reads=(), writes=(), chan=None, indirect=None, **kw):
        assert chan is not None
        waits = self._collect(q, reads, writes)
        tok = self._chan_token(chan)
        self.n_waits += len(waits)
        if indirect is None:
            fn = lambda e, out=out, in_=in_, kw=kw: e.dma_start(out=_r(out), in_=_r(in_), **kw)
        else:
            fn = indirect
        self.ops[q].append((waits, fn, (tok[0], 16)))
        self._update(tok, reads, writes)

    def raw(self, eng, fn, reads=(), writes=()):
        reads, writes = self._excl(reads, writes)
        waits = self._collect(eng, reads, writes)
        self.ops[eng].append((waits, fn, None))

    def barrier(self):
        targets = {}
        for e in ENGS:
            c = self.cnt[e]
            if c > 0:
                ep, v = divmod(c - 1, EPOCH)
                targets[f"c_{e}_{ep}"] = v + 1
        for chan, tot in self.chan.items():
            per = EPOCH // 16
            ep = (tot - 1) // per
            targets[f"d_{chan}_{ep}"] = (tot - ep * per) * 16
        for e in ENGS:
            waits = []
            for n, v in targets.items():
                if n.startswith("c_pe_") and e == "pe":
                    continue
                if self.known[e].get(n, 0) >= v:
                    continue
                self.known[e][n] = v
                waits.append((n, v))
            self.ops[e].append((waits, None, None))

    def wait_all(self, eng, keys):
        waits = self._collect(eng, (), keys)
        self.ops[eng].append((waits, None, None))

    def build(self):
        nc = self.nc
        from contextlib import ExitStack
        with ExitStack() as es:
            for kind, name, shape, dtype in self.tensors:
                if kind == "sbuf":
                    self.handles[name] = es.enter_context(nc.sbuf_tensor(name, shape, dtype))
                else:
                    self.handles[name] = es.enter_context(nc.psum_tensor(name, shape, dtype))
            sems = {}
            for n in self.semnames:
                sems[n] = es.enter_context(nc.semaphore(n))
            block = es.enter_context(nc.Block())

            def emit(engobj, lst):
                for waits, fn, inc in lst:
                    for n, v in waits:
                        engobj.wait_ge(sems[n], v)
                    if fn is not None:
                        ins = fn(engobj)
                        if inc is not None:
                            ins.then_inc(sems[inc[0]], inc[1])

            @block.tensor
            def _(e):
                emit(e, self.ops["pe"])

            @block.scalar
            def _(e):
                emit(e, self.ops["act"])

            @block.vector
            def _(e):
                emit(e, self.ops["dve"])

            @block.gpsimd
            def _(e):
                emit(e, self.ops["pool"])

            @block.sync
            def _(e):
                emit(e, self.ops["sp"])
        return nc


class _Lazy:
    def __init__(self, prog, name, chain=()):
        self.prog = prog
        self.name = name
        self.chain = chain

    def __getitem__(self, idx):
        return _Lazy(self.prog, self.name, self.chain + (("idx", idx),))

    def rearrange(self, pat, **kw):
        return _Lazy(self.prog, self.name, self.chain + (("re", pat, kw),))

    def bitcast(self, dt):
        return _Lazy(self.prog, self.name, self.chain + (("bc", dt),))

    def to_broadcast(self, shape):
        return _Lazy(self.prog, self.name, self.chain + (("tb", shape),))

    def resolve(self):
        h = self.prog.handles[self.name]
        if not self.chain:
            return h[:]
        cur = h if self.chain[0][0] == "idx" else h[:]
        for c in self.chain:
            if c[0] == "idx":
                cur = cur[c[1]]
            elif c[0] == "re":
                cur = cur.rearrange(c[1], **c[2])
            elif c[0] == "bc":
                cur = cur.bitcast(c[1])
            elif c[0] == "tb":
                cur = cur.to_broadcast(c[1])
        return cur


def _r(x):
    return x.resolve() if isinstance(x, _Lazy) else x


import math

NTOK = 2048
D = 2048
EPS = 1e-5


class Buf:
    def __init__(self, ap, key):
        self.ap = ap
        self.key = key


class Rot:
    def __init__(self, P, name, n, shape, dtype, psum=False):
        self.bufs = []
        for i in range(n):
            t = P.psum(f"{name}{i}", shape, dtype) if psum else P.sbuf(f"{name}{i}", shape, dtype)
            self.bufs.append(Buf(t, f"{name}{i}"))
        self.i = 0

    def next(self):
        b = self.bufs[self.i % len(self.bufs)]
        self.i += 1
        return b


def keys(lst):
    return [b.key if isinstance(b, Buf) else b for b in lst]


class H:
    def __init__(self, P):
        self.P = P

    def mm(self, out, lhsT, rhs, start, stop, reads, writes):
        self.P.op("pe", lambda e: e.matmul(_r(out), lhsT=_r(lhsT), rhs=_r(rhs), start=start, stop=stop),
                  reads=keys(reads), writes=keys(writes))

    def tr(self, out, in_, ident, reads, writes):
        self.P.op("pe", lambda e: e.transpose(_r(out), _r(in_), _r(ident)), reads=keys(reads), writes=keys(writes))

    def act(self, out, in_, func, reads, writes, scale=1.0, bias=None, accum=None, eng="act"):
        def fn(e):
            kw = {}
            if bias is not None:
                kw["bias"] = _r(bias)
            if accum is not None:
                kw["accum_out"] = _r(accum)
            return e.activation(out=_r(out), in_=_r(in_), func=func, scale=(_r(scale) if not isinstance(scale, float) else scale), **kw)
        self.P.op("act", fn, reads=keys(reads), writes=keys(writes))

    def tt(self, eng, out, a, b, op, reads, writes):
        self.P.op(eng, lambda e: e.tensor_tensor(out=_r(out), in0=_r(a), in1=_r(b), op=op), reads=keys(reads), writes=keys(writes))

    def ts(self, eng, out, a, s1, s2, op0, op1, reads, writes, accum=None):
        def fn(e):
            kw = {}
            if accum is not None:
                kw["accum_out"] = _r(accum)
            if s2 is None:
                return e.tensor_scalar(out=_r(out), in0=_r(a), scalar1=_r(s1), scalar2=None, op0=op0, **kw)
            return e.tensor_scalar(out=_r(out), in0=_r(a), scalar1=_r(s1), scalar2=_r(s2), op0=op0, op1=op1, **kw)
        self.P.op(eng, fn, reads=keys(reads), writes=keys(writes))

    def stt(self, eng, out, a, s, b, op0, op1, reads, writes):
        self.P.op(eng, lambda e: e.scalar_tensor_tensor(out=_r(out), in0=_r(a), scalar=_r(s), in1=_r(b), op0=op0, op1=op1),
                  reads=keys(reads), writes=keys(writes))

    def cp(self, eng, out, in_, reads, writes):
        if eng == "act":
            self.P.op("act", lambda e: e.copy(out=_r(out), in_=_r(in_)), reads=keys(reads), writes=keys(writes))
        else:
            self.P.op(eng, lambda e: e.tensor_copy(out=_r(out), in_=_r(in_)), reads=keys(reads), writes=keys(writes))

    def red(self, eng, out, in_, op, reads, writes):
        self.P.op(eng, lambda e: e.tensor_reduce(out=_r(out), in_=_r(in_), axis=AX.X, op=op), reads=keys(reads), writes=keys(writes))

    def recip(self, out, in_, reads, writes):
        self.P.op("dve", lambda e: e.reciprocal(out=_r(out), in_=_r(in_)), reads=keys(reads), writes=keys(writes))

    def memset(self, eng, out, val, writes):
        self.P.op(eng, lambda e: e.memset(_r(out), val), reads=(), writes=keys(writes))

    def dma(self, q, out, in_, reads, writes, chan):
        wk = []
        for k in keys(writes):
            if (isinstance(k, str) and (k.startswith("s_") or k == "yaT_s")) or (isinstance(k, tuple) and k[0] == "xT_s"):
                self.uq = getattr(self, "uq", 0) + 1
                k = ("uq", k, self.uq)
            wk.append(k)
        self.P.dma(q, out, in_, reads=keys(reads), writes=wk, chan=chan)


def gelu_tanh(h, ps, out, tmp_rot, shape_sl, reads_ps, writes_out, dve="dve"):
    xs = tmp_rot.next(); t1 = tmp_rot.next()
    h.cp("act", xs.ap[shape_sl], ps, reads_ps, [xs])
    h.act(t1.ap[shape_sl], ps, AF.Square, reads_ps, [t1])
    h.ts(dve, t1.ap[shape_sl], t1.ap[shape_sl], 0.044715, 1.0, ALU.mult, ALU.add, [t1], [t1])
    h.tt(dve, t1.ap[shape_sl], t1.ap[shape_sl], xs.ap[shape_sl], ALU.mult, [t1, xs], [t1])
    h.act(t1.ap[shape_sl], t1.ap[shape_sl], AF.Sigmoid, [t1], [t1], scale=2.0 * math.sqrt(2.0 / math.pi))
    h.tt(dve, out, t1.ap[shape_sl], xs.ap[shape_sl], ALU.mult, [t1, xs], writes_out)


def lb_compute(h, Lap, mask_ap, tmp_a, tmp_b, out_oml, keyL, n):
    ta, tb = tmp_a, tmp_b
    h.tt("dve", ta.ap, Lap(0), Lap(1), ALU.max, [keyL], [ta])
    h.tt("dve", ta.ap, ta.ap, Lap(2), ALU.max, [keyL, ta], [ta])
    h.tt("dve", ta.ap, ta.ap, Lap(3), ALU.max, [keyL, ta], [ta])
    for l in range(4):
        h.tt("dve", Lap(l), Lap(l), ta.ap, ALU.subtract, [keyL, ta], [keyL])
        h.act(Lap(l), Lap(l), AF.Exp, [keyL], [keyL])
    h.tt("dve", ta.ap, Lap(0), Lap(1), ALU.add, [keyL], [ta])
    h.tt("dve", ta.ap, ta.ap, Lap(2), ALU.add, [keyL, ta], [ta])
    h.tt("dve", ta.ap, ta.ap, Lap(3), ALU.add, [keyL, ta], [ta])
    h.recip(ta.ap, ta.ap, [ta], [ta])
    h.ts("dve", tb.ap, Lap(0), mask_ap(0), None, ALU.mult, None, [keyL, "lmask"], [tb])
    for l in range(1, 4):
        h.stt("dve", tb.ap, Lap(l), mask_ap(l), tb.ap, ALU.mult, ALU.add, [keyL, "lmask", tb], [tb])
    h.tt("dve", tb.ap, tb.ap, ta.ap, ALU.mult, [ta, tb], [tb])
    h.ts("dve", out_oml.ap, tb.ap, -1.0, 1.0, ALU.mult, ALU.add, [tb], [out_oml])


def build_A():
    P = Prog(); h = H(P)
    HT = NTOK // 2
    xT_d = P.dram("xT", [D, NTOK], F32, "ExternalInput")
    w_in_d = P.dram("w_in", [D, 4160], F32, "ExternalInput")
    w_uq_d = P.dram("w_uq", [512, 1536], F32, "ExternalInput")
    w_ukv_d = P.dram("w_ukv", [512, 2048], F32, "ExternalInput")
    lngb_d = P.dram("lngb", [128, 2, 512], F32, "ExternalInput")
    wsT_d = P.dram("wsT", [128, 4, 128], F32, "ExternalInput")
    tri_d = P.dram("tri", [128, 128], F32, "ExternalInput")
    sgub_d = P.dram("sgub", [128, 4], F32, "ExternalInput")
    lbT_d = P.dram("lbT", [128, 4, 4], F32, "ExternalInput")
    lbB_d = P.dram("lbB", [128, 4, 512], F32, "ExternalInput")
    lmask_d = P.dram("lmask", [128, 4], F32, "ExternalInput")
    qng_d = P.dram("qng", [128, 4], F32, "ExternalInput")
    kvng_d = P.dram("kvng", [128, 4], F32, "ExternalInput")
    pos_d = P.dram("posr", [64, NTOK], I32, "ExternalInput")
    ropec_d = P.dram("ropec", [64, 2], F32, "ExternalInput")
    ones_d = P.dram("ones", [128, 128], F32, "ExternalInput")

    ya_d = P.dram("ya", [NTOK, 512], BF16, "ExternalOutput")
    hqT_d = P.dram("hqT", [512, NTOK], F32, "ExternalOutput")
    hkT_d = P.dram("hkT", [512, NTOK], F32, "ExternalOutput")
    hlogf_d = P.dram("hlogf", [NTOK, 512], F32, "ExternalOutput")
    hkk_d = P.dram("hkk", [NTOK, 512], F32, "ExternalOutput")
    hv_d = P.dram("hv", [NTOK, 512], BF16, "ExternalOutput")
    hgate_d = P.dram("hgate", [NTOK, 512], F32, "ExternalOutput")
    QT_d = P.dram("QT", [8 * 192, NTOK], BF16, "ExternalOutput")
    KnT_d = P.dram("KnT", [1024, NTOK], BF16, "ExternalOutput")
    KrT_d = P.dram("KrT", [64, NTOK], BF16, "ExternalOutput")
    Vv_d = P.dram("Vv", [NTOK, 1024], BF16, "ExternalOutput")
    outs = ["ya_d", "hqT_d", "hkT_d", "hlogf_d", "hkk_d", "hv_d", "hgate_d", "QT_d", "KnT_d", "KrT_d", "Vv_d"]

    xT_bf = [Buf(P.sbuf(f"xTbf{i}", [128, 16, 512], BF16), f"xTbf{i}") for i in range(2)]
    w_st = Rot(P, "wst", 2, [128, 4, 512], F32)
    w_bf = Rot(P, "wbf", 2, [128, 16, 512], BF16)
    upw = Buf(P.sbuf("upw", [128, 4, 2048], BF16), "upw")
    tmp = Rot(P, "tmp", 6, [128, 512], F32)
    gur = Rot(P, "gur", 2, [128, 512], F32)
    gvr = Rot(P, "gvr", 2, [128, 512], F32)
    ostf = Rot(P, "ostf", 4, [128, 512], F32)
    ostb = Rot(P, "ostb", 4, [128, 512], BF16)
    c_sb = Buf(P.sbuf("c_sb", [128, 4, 512], F32), "c_sb")
    sq = Buf(P.sbuf("sq", [128, 4, 512], F32), "sq")
    cn = Rot(P, "cn", 2, [128, 4, 512], BF16)
    rstd = Buf(P.sbuf("rstd", [128, 512], F32), "rstd")
    cos2 = Buf(P.sbuf("cos2", [64, 512], F32), "cos2")
    sinS = Buf(P.sbuf("sinS", [64, 512], F32), "sinS")
    rt = [Buf(P.sbuf(f"rt{i}", [64, 512], F32), f"rt{i}") for i in range(2)]
    rti = Buf(P.sbuf("rti", [64, 512], I32), "rti")
    posi = Buf(P.sbuf("posi", [64, NTOK], I32), "posi")
    lngb = Buf(P.sbuf("lngb_s", [128, 2, 512], F32), "lngb")
    wsT_f = Buf(P.sbuf("wsT_f", [128, 4, 128], F32), "wsT_f")
    wsT_b = Buf(P.sbuf("wsT_b", [128, 4, 128], BF16), "wsT_b")
    tri = Buf(P.sbuf("tri_s", [128, 128], F32), "tri")
    sgub = Buf(P.sbuf("sgub_s", [128, 4], F32), "sgub")
    lbT = Buf(P.sbuf("lbT_s", [128, 4, 4], F32), "lbT")
    lbB = Buf(P.sbuf("lbB_s", [128, 4, 512], F32), "lbB")
    lmask = Buf(P.sbuf("lmask_s", [128, 4], F32), "lmask")
    oml_fm = Buf(P.sbuf("oml_fm", [128, 4], F32), "oml_fm")
    oml_tm = Buf(P.sbuf("oml_tm", [128, 512], F32), "oml_tm")
    sm = [Buf(P.sbuf(f"sm{i}", [128, 4], F32), f"sm{i}") for i in range(2)]
    qng = Buf(P.sbuf("qng_s", [128, 4], F32), "qng")
    kvng = Buf(P.sbuf("kvng_s", [128, 4], F32), "kvng")
    ropec = Buf(P.sbuf("ropec_s", [64, 2], F32), "ropec")
    ones = Buf(P.sbuf("ones_s", [128, 128], F32), "ones")
    stat = Rot(P, "stat", 4, [128, 2], F32)
    mmps = Rot(P, "mmps", 4, [128, 512], F32, psum=True)
    upps = Rot(P, "upps", 3, [128, 512], F32, psum=True)
    ssqps = Buf(P.psum("ssqps", [128, 512], F32), "ssqps")

    for b, d_ in ((lngb, lngb_d), (wsT_f, wsT_d), (tri, tri_d), (sgub, sgub_d), (lbT, lbT_d), (lbB, lbB_d),
                  (lmask, lmask_d), (qng, qng_d), (kvng, kvng_d), (posi, pos_d), (ropec, ropec_d), (ones, ones_d)):
        h.dma("sp", b.ap, d_.ap(), [], [b], chan="const")
    for g in range(4):
        h.tt("dve", wsT_b.ap[:, g, :], wsT_f.ap[:, g, :], tri.ap, ALU.mult, [wsT_f, tri], [wsT_b])
    lb_compute(h, lambda l: lbT.ap[:, l, :], lambda l: lmask.ap[:, l:l + 1], sm[0], sm[1], oml_fm, lbT, 4)
    ta = tmp.next(); tb_ = tmp.next()
    lb_compute(h, lambda l: lbB.ap[:, l, :], lambda l: lmask.ap[:, l:l + 1], ta, tb_, oml_tm, lbB, 512)

    w_in_v = w_in_d.ap().rearrange("(c p) n -> p c n", p=128)
    xT_v = xT_d.ap().rearrange("(c p) t -> p c t", p=128)
    cast_i = [0]

    def cast_eng():
        cast_i[0] += 1
        return ("dve", "pool")[cast_i[0] % 2]

    def load_group(col0, ncols, swap64=False):
        wb = w_bf.next()
        for q in range(4):
            st = w_st.next()
            h.dma("sp", st.ap[:, :, 0:ncols], w_in_v[:, q * 4:(q + 1) * 4, col0:col0 + ncols], [], [st], chan=st.key)
            h.cp(cast_eng(), wb.ap[:, q * 4:(q + 1) * 4, 0:ncols], st.ap[:, :, 0:ncols], [st], [wb])
            if swap64:
                h.cp(cast_eng(), wb.ap[:, q * 4:(q + 1) * 4, 64:96], st.ap[:, :, 32:64], [st], [wb])
                h.cp(cast_eng(), wb.ap[:, q * 4:(q + 1) * 4, 96:128], st.ap[:, :, 0:32], [st], [wb])
        return wb

    def rope_tables(tok0):
        a, b = rt
        h.cp("dve", a.ap, posi.ap[:, tok0:tok0 + 512], [posi], [a])
        h.ts("dve", a.ap, a.ap, ropec.ap[:, 0:1], None, ALU.mult, None, [a, ropec], [a])
        for off, dst in ((0.0, sinS), (0.25, cos2)):
            h.ts("dve", b.ap, a.ap, off, None, ALU.add, None, [a], [b])
            h.cp("dve", rti.ap, b.ap, [b], [rti])
            h.cp("dve", dst.ap, rti.ap, [rti], [dst])
            h.tt("dve", b.ap, b.ap, dst.ap, ALU.subtract, [b, dst], [b])
            h.ts("dve", dst.ap, b.ap, 0.5, None, ALU.is_gt, None, [b], [dst])
            h.tt("dve", b.ap, b.ap, dst.ap, ALU.subtract, [b, dst], [b])
            h.ts("dve", dst.ap, b.ap, -0.5, None, ALU.is_lt, None, [b], [dst])
            h.tt("dve", b.ap, b.ap, dst.ap, ALU.add, [b, dst], [b])
            h.act(dst.ap, b.ap, AF.Sin, [b], [dst], scale=2.0 * math.pi)
        h.ts("dve", sinS.ap, sinS.ap, ropec.ap[:, 1:2], None, ALU.mult, None, [sinS, ropec], [sinS])

    def rope_apply(pa, pb, dst_dram_ap, wkey):
        t1 = tmp.next(); t2 = tmp.next(); ob = ostb.next()
        h.tt("dve", t1.ap[0:64, :], pa.ap[0:64, :], cos2.ap, ALU.mult, [pa, cos2], [t1])
        h.tt("dve", t2.ap[0:64, :], pb.ap[0:64, :], sinS.ap, ALU.mult, [pb, sinS], [t2])
        h.tt("pool", ob.ap[0:64, :], t1.ap[0:64, :], t2.ap[0:64, :], ALU.add, [t1, t2], [ob])
        h.dma("pool", dst_dram_ap, ob.ap[0:64, :], [ob], [wkey], chan=ob.key)

    for half in range(2):
        t0 = half * HT
        for tb in range(2):
            for q in range(4):
                st = w_st.next()
                h.dma("sp", st.ap, xT_v[:, q * 4:(q + 1) * 4, t0 + tb * 512:t0 + (tb + 1) * 512], [], [st], chan=st.key)
                h.cp(cast_eng(), xT_bf[tb].ap[:, q * 4:(q + 1) * 4, :], st.ap, [st], [xT_bf[tb]])

        def fm_mm(ps, wb, j, tb, ncol=128, coff=None):
            co = j * 128 if coff is None else coff
            for c in range(16):
                h.mm(ps.ap[0:ncol, :], wb.ap[:, c, co:co + ncol], xT_bf[tb].ap[:, c, :], c == 0, c == 15, [wb, xT_bf[tb]], [ps])

        def tm_mm(ps, wb, tt):
            tb, r = divmod(tt, 4)
            for c in range(16):
                h.mm(ps.ap, xT_bf[tb].ap[:, c, r * 128:(r + 1) * 128], wb.ap[:, c, :], c == 0, c == 15, [wb, xT_bf[tb]], [ps])

        wb_u = load_group(0, 512)
        wb_v = load_group(512, 512)
        for tt in range(8):
            tok = t0 + tt * 128
            pu = mmps.next(); tm_mm(pu, wb_u, tt)
            pv = mmps.next(); tm_mm(pv, wb_v, tt)
            gu = gur.next()
            gelu_tanh(h, pu.ap, gu.ap, tmp, slice(None), [pu], [gu])
            gv = gvr.next()
            gelu_tanh(h, pv.ap, gv.ap, tmp, slice(None), [pv], [gv])
            st_ = stat.next()
            h.memset("pool", st_.ap, 0.0, [st_])
            h.red("dve", st_.ap[:, 0:1], gv.ap, ALU.add, [gv, st_], [st_])
            h.ts("dve", st_.ap[:, 0:1], st_.ap[:, 0:1], 1.0 / 512, None, ALU.mult, None, [st_], [st_])
            h.ts("dve", gv.ap, gv.ap, st_.ap[:, 0:1], None, ALU.subtract, None, [gv, st_], [gv])
            junk = tmp.next()
            h.act(junk.ap, gv.ap, AF.Square, [gv, st_], [junk, st_], accum=st_.ap[:, 1:2])
            h.ts("dve", st_.ap[:, 1:2], st_.ap[:, 1:2], 1.0 / 512, EPS, ALU.mult, ALU.add, [st_], [st_])
            h.act(st_.ap[:, 1:2], st_.ap[:, 1:2], AF.Sqrt, [st_], [st_])
            h.recip(st_.ap[:, 1:2], st_.ap[:, 1:2], [st_], [st_])
            h.stt("dve", junk.ap, gv.ap, st_.ap[:, 1:2], lngb.ap[:, 0, :], ALU.mult, ALU.mult, [gv, st_, lngb], [junk])
            vn = ostb.next()
            h.tt("pool", vn.ap, junk.ap, lngb.ap[:, 1, :], ALU.add, [junk, lngb], [vn])
            pm = upps.next()
            for g in range(4):
                h.mm(pm.ap[:, g * 128:(g + 1) * 128], wsT_b.ap[:, g, :], vn.ap[:, g * 128:(g + 1) * 128], True, True, [wsT_b, vn], [pm])
            mx = tmp.next()
            for g in range(4):
                h.ts("dve", mx.ap[:, g * 128:(g + 1) * 128], pm.ap[:, g * 128:(g + 1) * 128], sgub.ap[:, g:g + 1], None, ALU.add, None, [pm, sgub], [mx])
            yb = ostb.next()
            h.tt("pool", yb.ap, mx.ap, gu.ap, ALU.mult, [mx, gu], [yb])
            h.dma("pool", ya_d.ap()[tok:tok + 128, :], yb.ap, [yb], ["ya_d"], chan=yb.key)

        wb = load_group(1024, 512)
        for j in range(4):
            for tb in range(2):
                ps = mmps.next(); fm_mm(ps, wb, j, tb)
                o = ostf.next()
                h.act(o.ap, ps.ap, AF.Silu, [ps], [o])
                h.dma("pool", hqT_d.ap()[j * 128:(j + 1) * 128, t0 + tb * 512:t0 + (tb + 1) * 512], o.ap, [o], ["hqT_d"], chan=o.key)
        wb = load_group(1536, 512)
        for j in range(4):
            for tb in range(2):
                ps = mmps.next(); fm_mm(ps, wb, j, tb)
                o = ostf.next()
                h.act(o.ap, ps.ap, AF.Sigmoid, [ps], [o], scale=-1.0)
                h.ts("dve", o.ap, o.ap, oml_fm.ap[:, j:j + 1], None, ALU.mult, None, [o, oml_fm], [o])
                h.dma("pool", hkT_d.ap()[j * 128:(j + 1) * 128, t0 + tb * 512:t0 + (tb + 1) * 512], o.ap, [o], ["hkT_d"], chan=o.key)
        for tt in range(8):
            tok = t0 + tt * 128
            ps = mmps.next(); tm_mm(ps, wb, tt)
            sg = tmp.next(); o1 = ostf.next(); o2 = ostf.next()
            h.act(sg.ap, ps.ap, AF.Sigmoid, [ps], [sg], scale=-1.0)
            h.tt("dve", o1.ap, sg.ap, oml_tm.ap, ALU.mult, [sg, oml_tm], [o1])
            h.ts("dve", sg.ap, o1.ap, -1.0, 1.0, ALU.mult, ALU.add, [o1], [sg])
            h.act(o2.ap, sg.ap, AF.Ln, [sg], [o2])
            h.dma("pool", hkk_d.ap()[tok:tok + 128, :], o1.ap, [o1], ["hkk_d"], chan=o1.key)
            h.dma("pool", hlogf_d.ap()[tok:tok + 128, :], o2.ap, [o2], ["hlogf_d"], chan=o2.key)
        wb = load_group(2048, 512)
        for tt in range(8):
            tok = t0 + tt * 128
            ps = mmps.next(); tm_mm(ps, wb, tt)
            o = ostb.next()
            h.cp("act", o.ap, ps.ap, [ps], [o])
            h.dma("pool", hv_d.ap()[tok:tok + 128, :], o.ap, [o], ["hv_d"], chan=o.key)
        wb = load_group(2560, 512)
        for tt in range(8):
            tok = t0 + tt * 128
            ps = mmps.next(); tm_mm(ps, wb, tt)
            o = ostf.next()
            h.act(o.ap, ps.ap, AF.Silu, [ps], [o])
            h.dma("pool", hgate_d.ap()[tok:tok + 128, :], o.ap, [o], ["hgate_d"], chan=o.key)

        def rms_block(wb, tb, gcol):
            pss = [mmps.next() for _ in range(4)]
            for j in range(4):
                fm_mm(pss[j], wb, j, tb)
            for j in range(4):
                h.cp("act", c_sb.ap[:, j, :], pss[j].ap, [pss[j]], [c_sb])
                h.act(sq.ap[:, j, :], pss[j].ap, AF.Square, [pss[j]], [sq])
            for j in range(4):
                h.mm(ssqps.ap, ones.ap, sq.ap[:, j, :], j == 0, j == 3, [ones, sq], [ssqps])
            h.ts("dve", rstd.ap, ssqps.ap, 1.0 / 512, EPS, ALU.mult, ALU.add, [ssqps], [rstd])
            h.act(rstd.ap, rstd.ap, AF.Sqrt, [rstd], [rstd])
            h.recip(rstd.ap, rstd.ap, [rstd], [rstd])
            c = cn.next()
            for j in range(4):
                h.stt("dve", c.ap[:, j, :], c_sb.ap[:, j, :], gcol.ap[:, j:j + 1], rstd.ap, ALU.mult, ALU.mult, [c_sb, gcol, rstd], [c])
            return c

        wuq_v = w_uq_d.ap().rearrange("(c p) n -> p c n", p=128)
        for j in range(4):
            for part in range(3):
                st = w_st.next()
                h.dma("sp", st.ap[:, 0, :], wuq_v[:, j, part * 512:(part + 1) * 512], [], [st], chan=st.key)
                h.cp(cast_eng(), upw.ap[:, j, part * 512:(part + 1) * 512], st.ap[:, 0, :], [st], [upw])
        for j in range(4):
            src = upw.ap[:, j, 0:1536].rearrange("p (h d) -> p h d", d=192)
            dst = upw.ap[:, j, 1536:2048].rearrange("p (h d) -> p h d", d=64)
            h.cp("dve", dst[:, :, 0:32], src[:, :, 160:192], [upw], [upw])
            h.cp("dve", dst[:, :, 32:64], src[:, :, 128:160], [upw], [upw])
        wb = load_group(3072, 512)
        for tb in range(2):
            tk = t0 + tb * 512
            c = rms_block(wb, tb, qng)
            rope_tables(tk)
            for hh in range(8):
                pq = upps.next()
                for j in range(4):
                    h.mm(pq.ap, upw.ap[:, j, hh * 192:hh * 192 + 128], c.ap[:, j, :], j == 0, j == 3, [upw, c], [pq])
                o = ostb.next()
                h.cp("act", o.ap, pq.ap, [pq], [o])
                h.dma("pool", QT_d.ap()[hh * 192:hh * 192 + 128, tk:tk + 512], o.ap, [o], ["QT_d"], chan=o.key)
                pa = upps.next(); pb = upps.next()
                for j in range(4):
                    h.mm(pa.ap[0:64, :], upw.ap[:, j, hh * 192 + 128:hh * 192 + 192], c.ap[:, j, :], j == 0, j == 3, [upw, c], [pa])
                for j in range(4):
                    h.mm(pb.ap[0:64, :], upw.ap[:, j, 1536 + hh * 64:1536 + (hh + 1) * 64], c.ap[:, j, :], j == 0, j == 3, [upw, c], [pb])
                rope_apply(pa, pb, QT_d.ap()[hh * 192 + 128:hh * 192 + 192, tk:tk + 512], "QT_d")
        wukv_v = w_ukv_d.ap().rearrange("(c p) n -> p c n", p=128)
        for j in range(4):
            for part in range(4):
                st = w_st.next()
                h.dma("sp", st.ap[:, 0, :], wukv_v[:, j, part * 512:(part + 1) * 512], [], [st], chan=st.key)
                src = st.ap[:, 0, :].rearrange("p (h t d) -> p h t d", t=2, d=128)
                dk = upw.ap[:, j, part * 256:(part + 1) * 256].rearrange("p (h d) -> p h d", d=128)
                dv = upw.ap[:, j, 1024 + part * 256:1024 + (part + 1) * 256].rearrange("p (h d) -> p h d", d=128)
                h.cp(cast_eng(), dk, src[:, :, 0, :], [st], [upw])
                h.cp(cast_eng(), dv, src[:, :, 1, :], [st], [upw])
        wb = load_group(3584, 512)
        for tb in range(2):
            tk = t0 + tb * 512
            c = rms_block(wb, tb, kvng)
            for hh in range(8):
                pq = upps.next()
                for j in range(4):
                    h.mm(pq.ap, upw.ap[:, j, hh * 128:(hh + 1) * 128], c.ap[:, j, :], j == 0, j == 3, [upw, c], [pq])
                o = ostb.next()
                h.cp("act", o.ap, pq.ap, [pq], [o])
                h.dma("pool", KnT_d.ap()[hh * 128:(hh + 1) * 128, tk:tk + 512], o.ap, [o], ["KnT_d"], chan=o.key)
            for r in range(4):
                for grp in range(2):
                    pq = upps.next()
                    for j in range(4):
                        h.mm(pq.ap, c.ap[:, j, r * 128:(r + 1) * 128], upw.ap[:, j, 1024 + grp * 512:1024 + (grp + 1) * 512], j == 0, j == 3, [upw, c], [pq])
                    o = ostb.next()
                    h.cp("act", o.ap, pq.ap, [pq], [o])
                    h.dma("pool", Vv_d.ap()[tk + r * 128:tk + (r + 1) * 128, grp * 512:(grp + 1) * 512], o.ap, [o], ["Vv_d"], chan=o.key)
        wb = load_group(4096, 64, swap64=True)
        for tb in range(2):
            tk = t0 + tb * 512
            rope_tables(tk)
            pa = mmps.next(); pb = mmps.next()
            fm_mm(pa, wb, 0, tb, ncol=64, coff=0)
            fm_mm(pb, wb, 0, tb, ncol=64, coff=64)
            rope_apply(pa, pb, KrT_d.ap()[:, tk:tk + 512], "KrT_d")

    P.wait_all("sp", outs)
    P.wait_all("pool", outs)
    return P


L = 4
FT = 4096
NHALF = FT // 1024


def declare_dram(P):
    d = {}
    I = "ExternalInput"
    d["x"] = P.dram("x", [FT, D], F32, I)
    d["posr"] = P.dram("posr", [64, FT], I32, I)
    d["w_in"] = P.dram("w_in", [L, D, 4160], F32, I)
    d["w_uq"] = P.dram("w_uq", [L, 512, 1536], F32, I)
    d["w_ukv"] = P.dram("w_ukv", [L, 512, 2048], F32, I)
    d["lngb"] = P.dram("lngb", [L, 128, 2, 512], F32, I)
    d["wsT"] = P.dram("wsT", [L, 128, 4, 128], F32, I)
    d["tri"] = P.dram("tri", [128, 128], F32, I)
    d["sgub"] = P.dram("sgub", [L, 128, 4], F32, I)
    d["lbT"] = P.dram("lbT", [128, 4, 4], F32, I)
    d["lbB"] = P.dram("lbB", [128, 4, 512], F32, I)
    d["lmask"] = P.dram("lmask", [L, 128, 4], F32, I)
    d["qng"] = P.dram("qng", [L, 128, 4], F32, I)
    d["kvng"] = P.dram("kvng", [L, 128, 4], F32, I)
    d["ropec"] = P.dram("ropec", [64, 2], F32, I)
    d["ones"] = P.dram("ones", [128, 128], F32, I)
    d["b_ng"] = P.dram("b_ng", [L, 64, 4, 128], F32, I)
    d["b_ucat"] = P.dram("b_ucat", [64, 128], F32, I)
    d["b_lmat"] = P.dram("b_lmat", [64, 64], F32, I)
    d["b_cmask"] = P.dram("b_cmask", [128, 4, 512], BF16, I)
    d["b_onesb"] = P.dram("b_onesb", [128, 128], BF16, I)
    d["c_wout"] = P.dram("c_wout", [L, D, D], F32, I)
    d["c_ln1"] = P.dram("c_ln1", [L, 128, 2, D], F32, I)
    d["c_ln2"] = P.dram("c_ln2", [L, 128, 2, D], F32, I)
    d["c_wr"] = P.dram("c_wr", [L, D, 36], F32, I)
    d["c_br"] = P.dram("c_br", [L, 128, 36], F32, I)
    d["c_wg"] = P.dram("c_wg", [L, 32, D, 512], F32, I)
    d["c_wu"] = P.dram("c_wu", [L, 32, D, 512], F32, I)
    d["c_wd"] = P.dram("c_wd", [L, 32, 512, D], F32, I)
    d["c_ident"] = P.dram("c_ident", [128, 128], F32, I)
    d["c_identb"] = P.dram("c_identb", [128, 128], BF16, I)
    d["c_ustr"] = P.dram("c_ustr", [128, 128], BF16, I)
    d["c_blkst"] = P.dram("c_blkst", [128, 53], F32, I)
    d["c_pq"] = P.dram("c_pq", [128, 4], F32, I)
    d["xo"] = P.dram("xo", [FT, D], F32, "ExternalOutput")
    N = "Internal"
    d["xT_s"] = P.dram("xT_s", [FT // 512, 128, 16, 512], BF16, N)
    d["xres_s"] = P.dram("xres_s", [FT, D], F32, N)
    d["yaT_s"] = P.dram("yaT_s", [512, FT], BF16, N)
    for nm, shp, dt in (("hqT", [512, FT], F32), ("hkT", [512, FT], F32),
                        ("hlogf", [FT, 512], F32), ("hkk", [FT, 512], F32),
                        ("hv", [FT, 512], BF16), ("hgate", [FT, 512], F32),
                        ("QT", [1536, FT], BF16), ("KnT", [1024, FT], BF16),
                        ("KrT", [64, FT], BF16), ("Vv", [FT, 1024], BF16),
                        ("ybT", [512, FT], BF16), ("ycT", [1024, FT], BF16)):
        d["s_" + nm] = P.dram("s_" + nm, shp, dt, N)
    d["c_x1"] = P.dram("c_x1", [FT, D], F32, N)
    d["c_x1b"] = P.dram("c_x1b", [FT, D], BF16, N)
    d["c_xs"] = P.dram("c_xs", [159 * 128, D], BF16, N)
    d["c_ys"] = P.dram("c_ys", [159 * 128, D], F32, N)
    return d


def exchange(P, d, names, tag):
    for nm in names:
        src = d["s_" + nm]; dst = d["g_" + nm]
        def cc(e, src=src, dst=dst):
            return e.collective_compute("AllToAll", ALU.bypass, replica_groups=[[0, 1]], ins=[src.ap()], outs=[dst.ap()])
        P.dma("pool", None, None, reads=["s_" + nm], writes=["g_" + nm], chan="cc_" + nm, indirect=cc)


def emit_xT(P, h, d, o, tt, ident, xtb_rot, banks4):
    tb, r = divmod(tt, 4)
    xtb = xtb_rot.next()
    for c4 in range(4):
        pt = banks4[c4 % len(banks4)]
        for i in range(4):
            c = c4 * 4 + i
            h.tr(pt.ap[:, i * 128:(i + 1) * 128], o.ap[:, c * 128:(c + 1) * 128], ident.ap, [o, ident], [pt])
        h.cp(("act", "dve")[c4 % 2], xtb.ap[:, c4 * 4:(c4 + 1) * 4, :].rearrange("p c t -> p (c t)"), pt.ap, [pt], [xtb])
    for c4 in range(4):
        h.dma("pool", d["xT_s"].ap()[tb, :, c4 * 4:(c4 + 1) * 4, r * 128:(r + 1) * 128], xtb.ap[:, c4 * 4:(c4 + 1) * 4, :], [xtb], [("xT_s", tb)], chan=xtb.key)


def emit_X0(P, h, d):
    ident = Buf(P.sbuf("x0_ident", [128, 128], F32), "x0_ident")
    h.dma("sp", ident.ap, d["c_ident"].ap(), [], [ident], chan="const")
    xr = Rot(P, "x0_x", 2, [128, D], F32)
    xtb = Rot(P, "x0_xtb", 2, [128, 16, 128], BF16)
    banks = [Buf(P.psum(f"x0b{i}", [128, 512], F32), f"x0b{i}") for i in range(4)]
    for bk in banks:
        P.exclusive.add(bk.key)
    for tt in range(FT // 128):
        xt = xr.next()
        h.dma("sp", xt.ap, d["x"].ap()[tt * 128:(tt + 1) * 128, :], [], [xt], chan=xt.key)
        emit_xT(P, h, d, xt, tt, ident, xtb, banks)


def emit_A(P, h, d, l):
    HT = NTOK // 2
    xT_bf = [Buf(P.sbuf(f"xTbf{i}", [128, 16, 512], BF16), f"xTbf{i}") for i in range(2)]
    w_st = Rot(P, "wst", 2, [128, 4, 512], F32)
    w_bf = Rot(P, "wbf", 2, [128, 16, 512], BF16)
    upw = Buf(P.sbuf("upw", [128, 4, 2048], BF16), "upw")
    tmp = Rot(P, "tmp", 6, [128, 512], F32)
    gur = Rot(P, "gur", 2, [128, 512], F32)
    gvr = Rot(P, "gvr", 2, [128, 512], F32)
    ostf = Rot(P, "ostf", 4, [128, 512], F32)
    ostb = Rot(P, "ostb", 4, [128, 512], BF16)
    c_sb = Buf(P.sbuf("c_sb", [128, 4, 512], F32), "c_sb")
    sq = Buf(P.sbuf("sq", [128, 4, 512], F32), "sq")
    cn = Rot(P, "cn", 2, [128, 4, 512], BF16)
    rstd = Buf(P.sbuf("rstd", [128, 512], F32), "rstd")
    cos2 = Buf(P.sbuf("cos2", [64, 512], F32), "cos2")
    sinS = Buf(P.sbuf("sinS", [64, 512], F32), "sinS")
    rt = [Buf(P.sbuf(f"rt{i}", [64, 512], F32), f"rt{i}") for i in range(2)]
    rti = Buf(P.sbuf("rti", [64, 512], I32), "rti")
    posi = Buf(P.sbuf("posi", [64, FT], I32), "posi")
    lngb = Buf(P.sbuf("lngb_s", [128, 2, 512], F32), "lngb")
    wsT_f = Buf(P.sbuf("wsT_f", [128, 4, 128], F32), "wsT_f")
    wsT_b = Buf(P.sbuf("wsT_b", [128, 4, 128], BF16), "wsT_b")
    tri = Buf(P.sbuf("tri_s", [128, 128], F32), "tri")
    sgub = Buf(P.sbuf("sgub_s", [128, 4], F32), "sgub")
    lbT = Buf(P.sbuf("lbT_s", [128, 4, 4], F32), "lbT")
    lbB = Buf(P.sbuf("lbB_s", [128, 4, 512], F32), "lbB")
    lmask = Buf(P.sbuf("lmask_s", [128, 4], F32), "lmask")
    oml_fm = Buf(P.sbuf("oml_fm", [128, 4], F32), "oml_fm")
    oml_tm = Buf(P.sbuf("oml_tm", [128, 512], F32), "oml_tm")
    sm = [Buf(P.sbuf(f"sm{i}", [128, 4], F32), f"sm{i}") for i in range(2)]
    qng = Buf(P.sbuf("qng_s", [128, 4], F32), "qng")
    kvng = Buf(P.sbuf("kvng_s", [128, 4], F32), "kvng")
    ropec = Buf(P.sbuf("ropec_s", [64, 2], F32), "ropec")
    ones = Buf(P.sbuf("ones_s", [128, 128], F32), "ones")
    identb = Buf(P.sbuf("a_identb", [128, 128], BF16), "a_identb")
    yaT = Rot(P, "yaT", 2, [128, 4, 128], BF16)
    stat = Rot(P, "stat", 4, [128, 2], F32)
    mmps = Rot(P, "mmps", 4, [128, 512], F32, psum=True)
    upps = Rot(P, "upps", 3, [128, 512], F32, psum=True)
    ssqps = Buf(P.psum("ssqps", [128, 512], F32), "ssqps")
    for r_ in (mmps, upps):
        for b_ in r_.bufs:
            P.exclusive.add(b_.key)
    P.exclusive.add(ssqps.key)

    for b, ap_ in ((lngb, d["lngb"].ap()[l]), (wsT_f, d["wsT"].ap()[l]), (tri, d["tri"].ap()), (sgub, d["sgub"].ap()[l]),
                   (lbT, d["lbT"].ap()), (lbB, d["lbB"].ap()), (lmask, d["lmask"].ap()[l]), (qng, d["qng"].ap()[l]),
                   (kvng, d["kvng"].ap()[l]), (posi, d["posr"].ap()), (ropec, d["ropec"].ap()), (ones, d["ones"].ap()),
                   (identb, d["c_identb"].ap())):
        h.dma("sp", b.ap, ap_, [], [b], chan="const")
    for g in range(4):
        h.tt("dve", wsT_b.ap[:, g, :], wsT_f.ap[:, g, :], tri.ap, ALU.mult, [wsT_f, tri], [wsT_b])
    lb_compute(h, lambda l_: lbT.ap[:, l_, :], lambda l_: lmask.ap[:, l_:l_ + 1], sm[0], sm[1], oml_fm, lbT, 4)
    ta = tmp.next(); tb_ = tmp.next()
    lb_compute(h, lambda l_: lbB.ap[:, l_, :], lambda l_: lmask.ap[:, l_:l_ + 1], ta, tb_, oml_tm, lbB, 512)

    w_in_v = d["w_in"].ap()[l].rearrange("(c p) n -> p c n", p=128)
    cast_i = [0]

    def cast_eng():
        cast_i[0] += 1
        return ("dve", "pool")[cast_i[0] % 2]

    def load_group(col0, ncols, swap64=False):
        wb = w_bf.next()
        for q in range(4):
            st = w_st.next()
            h.dma("sp", st.ap[:, :, 0:ncols], w_in_v[:, q * 4:(q + 1) * 4, col0:col0 + ncols], [], [st], chan=st.key)
            h.cp(cast_eng(), wb.ap[:, q * 4:(q + 1) * 4, 0:ncols], st.ap[:, :, 0:ncols], [st], [wb])
            if swap64:
                h.cp(cast_eng(), wb.ap[:, q * 4:(q + 1) * 4, 64:96], st.ap[:, :, 32:64], [st], [wb])
                h.cp(cast_eng(), wb.ap[:, q * 4:(q + 1) * 4, 96:128], st.ap[:, :, 0:32], [st], [wb])
        return wb

    def rope_tables(tok0):
        a, b = rt
        h.cp("dve", a.ap, posi.ap[:, tok0:tok0 + 512], [posi], [a])
        h.ts("dve", a.ap, a.ap, ropec.ap[:, 0:1], None, ALU.mult, None, [a, ropec], [a])
        for off, dst in ((0.0, sinS), (0.25, cos2)):
            h.ts("dve", b.ap, a.ap, off, None, ALU.add, None, [a], [b])
            h.cp("dve", rti.ap, b.ap, [b], [rti])
            h.cp("dve", dst.ap, rti.ap, [rti], [dst])
            h.tt("dve", b.ap, b.ap, dst.ap, ALU.subtract, [b, dst], [b])
            h.ts("dve", dst.ap, b.ap, 0.5, None, ALU.is_gt, None, [b], [dst])
            h.tt("dve", b.ap, b.ap, dst.ap, ALU.subtract, [b, dst], [b])
            h.ts("dve", dst.ap, b.ap, -0.5, None, ALU.is_lt, None, [b], [dst])
            h.tt("dve", b.ap, b.ap, dst.ap, ALU.add, [b, dst], [b])
            h.act(dst.ap, b.ap, AF.Sin, [b], [dst], scale=2.0 * math.pi)
        h.ts("dve", sinS.ap, sinS.ap, ropec.ap[:, 1:2], None, ALU.mult, None, [sinS, ropec], [sinS])

    def rope_apply(pa, pb, dsts, wkey):
        t1 = tmp.next(); t2 = tmp.next(); ob = ostb.next()
        h.tt("dve", t1.ap[0:64, :], pa.ap[0:64, :], cos2.ap, ALU.mult, [pa, cos2], [t1])
        h.tt("dve", t2.ap[0:64, :], pb.ap[0:64, :], sinS.ap, ALU.mult, [pb, sinS], [t2])
        h.tt("pool", ob.ap[0:64, :], t1.ap[0:64, :], t2.ap[0:64, :], ALU.add, [t1, t2], [ob])
        for dst in dsts:
            h.dma("pool", dst, ob.ap[0:64, :], [ob], [wkey], chan=ob.key)

    for half in range(NHALF):
        t0 = half * HT
        for tb in range(2):
            h.dma("sp", xT_bf[tb].ap, d["xT_s"].ap()[half * 2 + tb], [("xT_s", half * 2 + tb)], [xT_bf[tb]], chan=xT_bf[tb].key)

        def fm_mm(ps, wb, j, tb, ncol=128, coff=None):
            co = j * 128 if coff is None else coff
            for c in range(16):
                h.mm(ps.ap[0:ncol, :], wb.ap[:, c, co:co + ncol], xT_bf[tb].ap[:, c, :], c == 0, c == 15, [wb, xT_bf[tb]], [ps])

        def tm_mm(ps, wb, tt):
            tb, r = divmod(tt, 4)
            for c in range(16):
                h.mm(ps.ap, xT_bf[tb].ap[:, c, r * 128:(r + 1) * 128], wb.ap[:, c, :], c == 0, c == 15, [wb, xT_bf[tb]], [ps])

        wb_u = load_group(0, 512)
        wb_v = load_group(512, 512)
        def sgu_mm(tt_):
            pu_ = mmps.next(); tm_mm(pu_, wb_u, tt_)
            pv_ = mmps.next(); tm_mm(pv_, wb_v, tt_)
            return pu_, pv_
        nxt_uv = sgu_mm(0)
        for tt in range(8):
            tok = t0 + tt * 128
            pu, pv = nxt_uv
            if tt + 1 < 8:
                nxt_uv = sgu_mm(tt + 1)
            gu = gur.next()
            gelu_tanh(h, pu.ap, gu.ap, tmp, slice(None), [pu], [gu])
            gv = gvr.next()
            gelu_tanh(h, pv.ap, gv.ap, tmp, slice(None), [pv], [gv])
            st_ = stat.next()
            h.memset("pool", st_.ap, 0.0, [st_])
            h.red("dve", st_.ap[:, 0:1], gv.ap, ALU.add, [gv, st_], [st_])
            h.ts("dve", st_.ap[:, 0:1], st_.ap[:, 0:1], 1.0 / 512, None, ALU.mult, None, [st_], [st_])
            h.ts("dve", gv.ap, gv.ap, st_.ap[:, 0:1], None, ALU.subtract, None, [gv, st_], [gv])
            junk = tmp.next()
            h.act(junk.ap, gv.ap, AF.Square, [gv, st_], [junk, st_], accum=st_.ap[:, 1:2])
            h.ts("dve", st_.ap[:, 1:2], st_.ap[:, 1:2], 1.0 / 512, EPS, ALU.mult, ALU.add, [st_], [st_])
            h.act(st_.ap[:, 1:2], st_.ap[:, 1:2], AF.Sqrt, [st_], [st_])
            h.recip(st_.ap[:, 1:2], st_.ap[:, 1:2], [st_], [st_])
            h.stt("dve", junk.ap, gv.ap, st_.ap[:, 1:2], lngb.ap[:, 0, :], ALU.mult, ALU.mult, [gv, st_, lngb], [junk])
            vn = ostb.next()
            h.tt("pool", vn.ap, junk.ap, lngb.ap[:, 1, :], ALU.add, [junk, lngb], [vn])
            pm = upps.next()
            for g in range(4):
                h.mm(pm.ap[:, g * 128:(g + 1) * 128], wsT_b.ap[:, g, :], vn.ap[:, g * 128:(g + 1) * 128], True, True, [wsT_b, vn], [pm])
            mx = tmp.next()
            for g in range(4):
                h.ts("dve", mx.ap[:, g * 128:(g + 1) * 128], pm.ap[:, g * 128:(g + 1) * 128], sgub.ap[:, g:g + 1], None, ALU.add, None, [pm, sgub], [mx])
            yb = ostb.next()
            h.tt("pool", yb.ap, mx.ap, gu.ap, ALU.mult, [mx, gu], [yb])
            pt = upps.next()
            ptb = pt.ap.bitcast(BF16)
            for g in range(4):
                h.tr(ptb[:, g * 128:(g + 1) * 128], yb.ap[:, g * 128:(g + 1) * 128], identb.ap, [yb, identb], [pt])
            yt = yaT.next()
            h.cp("act", yt.ap.rearrange("p g t -> p (g t)"), ptb[:, 0:512], [pt], [yt])
            h.dma("pool", d["yaT_s"].ap().rearrange("(g p) t -> p g t", p=128)[:, :, tok:tok + 128], yt.ap, [yt], ["yaT_s"], chan=yt.key)

        wb = load_group(1024, 512)
        for j in range(4):
            for tb in range(2):
                ps = mmps.next(); fm_mm(ps, wb, j, tb)
                o = ostf.next()
                h.act(o.ap, ps.ap, AF.Silu, [ps], [o])
                h.dma("pool", d["s_hqT"].ap()[j * 128:(j + 1) * 128, t0 + tb * 512:t0 + (tb + 1) * 512], o.ap, [o], ["s_hqT"], chan=o.key)
        wb = load_group(1536, 512)
        for j in range(4):
            for tb in range(2):
                ps = mmps.next(); fm_mm(ps, wb, j, tb)
                o = ostf.next()
                h.act(o.ap, ps.ap, AF.Sigmoid, [ps], [o], scale=-1.0)
                h.ts("dve", o.ap, o.ap, oml_fm.ap[:, j:j + 1], None, ALU.mult, None, [o, oml_fm], [o])
                h.dma("pool", d["s_hkT"].ap()[j * 128:(j + 1) * 128, t0 + tb * 512:t0 + (tb + 1) * 512], o.ap, [o], ["s_hkT"], chan=o.key)

        def tm_dst(nm, tok, ncol):
            return d["s_" + nm].ap()[tok:tok + 128, :]

        for tt in range(8):
            tok = t0 + tt * 128
            ps = mmps.next(); tm_mm(ps, wb, tt)
            sg = tmp.next(); o1 = ostf.next(); o2 = ostf.next()
            h.act(sg.ap, ps.ap, AF.Sigmoid, [ps], [sg], scale=-1.0)
            h.tt("dve", o1.ap, sg.ap, oml_tm.ap, ALU.mult, [sg, oml_tm], [o1])
            h.ts("dve", sg.ap, o1.ap, -1.0, 1.0, ALU.mult, ALU.add, [o1], [sg])
            h.act(o2.ap, sg.ap, AF.Ln, [sg], [o2])
            h.dma("pool", tm_dst("hkk", tok, 256), o1.ap, [o1], ["s_hkk"], chan=o1.key)
            h.dma("pool", tm_dst("hlogf", tok, 256), o2.ap, [o2], ["s_hlogf"], chan=o2.key)
        wb = load_group(2048, 512)
        for tt in range(8):
            tok = t0 + tt * 128
            ps = mmps.next(); tm_mm(ps, wb, tt)
            o = ostb.next()
            h.cp("act", o.ap, ps.ap, [ps], [o])
            h.dma("pool", tm_dst("hv", tok, 256), o.ap, [o], ["s_hv"], chan=o.key)
        wb = load_group(2560, 512)
        for tt in range(8):
            tok = t0 + tt * 128
            ps = mmps.next(); tm_mm(ps, wb, tt)
            o = ostf.next()
            h.act(o.ap, ps.ap, AF.Silu, [ps], [o])
            h.dma("pool", tm_dst("hgate", tok, 256), o.ap, [o], ["s_hgate"], chan=o.key)

        def rms_block(wb, tb, gcol):
            pss = [mmps.next() for _ in range(4)]
            for j in range(4):
                fm_mm(pss[j], wb, j, tb)
            for j in range(4):
                h.cp("act", c_sb.ap[:, j, :], pss[j].ap, [pss[j]], [c_sb])
                h.act(sq.ap[:, j, :], pss[j].ap, AF.Square, [pss[j]], [sq])
            for j in range(4):
                h.mm(ssqps.ap, ones.ap, sq.ap[:, j, :], j == 0, j == 3, [ones, sq], [ssqps])
            h.ts("dve", rstd.ap, ssqps.ap, 1.0 / 512, EPS, ALU.mult, ALU.add, [ssqps], [rstd])
            h.act(rstd.ap, rstd.ap, AF.Sqrt, [rstd], [rstd])
            h.recip(rstd.ap, rstd.ap, [rstd], [rstd])
            c = cn.next()
            for j in range(4):
                h.stt("dve", c.ap[:, j, :], c_sb.ap[:, j, :], gcol.ap[:, j:j + 1], rstd.ap, ALU.mult, ALU.mult, [c_sb, gcol, rstd], [c])
            return c

        wuq_v = d["w_uq"].ap()[l].rearrange("(c p) n -> p c n", p=128)
        for j in range(4):
            for part in range(3):
                st = w_st.next()
                h.dma("sp", st.ap[:, 0, :], wuq_v[:, j, part * 512:(part + 1) * 512], [], [st], chan=st.key)
                h.cp(cast_eng(), upw.ap[:, j, part * 512:(part + 1) * 512], st.ap[:, 0, :], [st], [upw])
        for j in range(4):
            src = upw.ap[:, j, 0:1536].rearrange("p (h d) -> p h d", d=192)
            dst = upw.ap[:, j, 1536:2048].rearrange("p (h d) -> p h d", d=64)
            h.cp("dve", dst[:, :, 0:32], src[:, :, 160:192], [upw], [upw])
            h.cp("dve", dst[:, :, 32:64], src[:, :, 128:160], [upw], [upw])
        wb = load_group(3072, 512)
        for tb in range(2):
            tk = t0 + tb * 512
            c = rms_block(wb, tb, qng)
            rope_tables(tk)
            for hh in range(8):
                pq = upps.next()
                for j in range(4):
                    h.mm(pq.ap, upw.ap[:, j, hh * 192:hh * 192 + 128], c.ap[:, j, :], j == 0, j == 3, [upw, c], [pq])
                o = ostb.next()
                h.cp("act", o.ap, pq.ap, [pq], [o])
                h.dma("pool", d["s_QT"].ap()[hh * 192:hh * 192 + 128, tk:tk + 512], o.ap, [o], ["s_QT"], chan=o.key)
                pa = upps.next(); pb = upps.next()
                for j in range(4):
                    h.mm(pa.ap[0:64, :], upw.ap[:, j, hh * 192 + 128:hh * 192 + 192], c.ap[:, j, :], j == 0, j == 3, [upw, c], [pa])
                for j in range(4):
                    h.mm(pb.ap[0:64, :], upw.ap[:, j, 1536 + hh * 64:1536 + (hh + 1) * 64], c.ap[:, j, :], j == 0, j == 3, [upw, c], [pb])
                rope_apply(pa, pb, [d["s_QT"].ap()[hh * 192 + 128:hh * 192 + 192, tk:tk + 512]], "s_QT")
        wukv_v = d["w_ukv"].ap()[l].rearrange("(c p) n -> p c n", p=128)
        for j in range(4):
            for part in range(4):
                st = w_st.next()
                h.dma("sp", st.ap[:, 0, :], wukv_v[:, j, part * 512:(part + 1) * 512], [], [st], chan=st.key)
                src = st.ap[:, 0, :].rearrange("p (h t d) -> p h t d", t=2, d=128)
                dk = upw.ap[:, j, part * 256:(part + 1) * 256].rearrange("p (h d) -> p h d", d=128)
                dv = upw.ap[:, j, 1024 + part * 256:1024 + (part + 1) * 256].rearrange("p (h d) -> p h d", d=128)
                h.cp(cast_eng(), dk, src[:, :, 0, :], [st], [upw])
                h.cp(cast_eng(), dv, src[:, :, 1, :], [st], [upw])
        wb = load_group(3584, 512)
        for tb in range(2):
            tk = t0 + tb * 512
            c = rms_block(wb, tb, kvng)
            for hh in range(8):
                pq = upps.next()
                for j in range(4):
                    h.mm(pq.ap, upw.ap[:, j, hh * 128:(hh + 1) * 128], c.ap[:, j, :], j == 0, j == 3, [upw, c], [pq])
                o = ostb.next()
                h.cp("act", o.ap, pq.ap, [pq], [o])
                h.dma("pool", d["s_KnT"].ap()[hh * 128:(hh + 1) * 128, tk:tk + 512], o.ap, [o], ["s_KnT"], chan=o.key)
            for r in range(4):
                for grp in range(2):
                    pq = upps.next()
                    for j in range(4):
                        h.mm(pq.ap, c.ap[:, j, r * 128:(r + 1) * 128], upw.ap[:, j, 1024 + grp * 512:1024 + (grp + 1) * 512], j == 0, j == 3, [upw, c], [pq])
                    o = ostb.next()
                    h.cp("act", o.ap, pq.ap, [pq], [o])
                    h.dma("pool", d["s_Vv"].ap()[tk + r * 128:tk + (r + 1) * 128, grp * 512:(grp + 1) * 512], o.ap, [o], ["s_Vv"], chan=o.key)
        wb = load_group(4096, 64, swap64=True)
        for tb in range(2):
            tk = t0 + tb * 512
            rope_tables(tk)
            pa = mmps.next(); pb = mmps.next()
            fm_mm(pa, wb, 0, tb, ncol=64, coff=0)
            fm_mm(pb, wb, 0, tb, ncol=64, coff=64)
            rope_apply(pa, pb, [d["s_KrT"].ap()[0:64, tk:tk + 512]], "s_KrT")


S = 4096
CH = 64
NCH = S // CH
GRP = 8


def emit_B(P, h, d, l):
    ng = Buf(P.sbuf("ng", [64, 4, 128], F32), "ng")
    ucat = Buf(P.sbuf("ucat", [64, 128], F32), "ucat")
    lmat = Buf(P.sbuf("lmat", [64, 64], F32), "lmat")
    cmask = Buf(P.sbuf("cmask", [128, 4, 512], BF16), "cmask")
    onesb = Buf(P.sbuf("onesb", [128, 128], BF16), "onesb")
    identb = Buf(P.sbuf("b_identb", [128, 128], BF16), "b_identb")
    for b, ap_ in ((ng, d["b_ng"].ap()[l]), (ucat, d["b_ucat"].ap()), (lmat, d["b_lmat"].ap()), (cmask, d["b_cmask"].ap()),
                   (onesb, d["b_onesb"].ap()), (identb, d["c_identb"].ap())):
        h.dma("sp", b.ap, ap_, [], [b], chan="const")
    banks = [Buf(P.psum(f"bbank{i}", [128, 512], F32), f"bbank{i}") for i in range(8)]
    for bk in banks:
        P.exclusive.add(bk.key)

    gq = Rot(P, "gq", 2, [128, 512], F32)
    gk = Rot(P, "gk", 2, [128, 512], F32)
    glf = Rot(P, "glf", 2, [64, GRP, 128], F32)
    gkk = Rot(P, "gkk", 2, [64, GRP, 128], F32)
    ggt = Rot(P, "ggt", 2, [64, GRP, 128], F32)
    gv = Rot(P, "gv", 2, [64, GRP, 128], BF16)
    yst = Rot(P, "yst", 2, [128, 512], BF16)
    Sf = [Buf(P.sbuf(f"Sf{i}", [128, 128], F32), f"Sf{i}") for i in range(4)]
    Sb = [Rot(P, f"Sb{i}_", 2, [128, 128], BF16) for i in range(4)]
    eg = Rot(P, "eg", 2, [128, 128], F32)
    ek = Rot(P, "ek", 2, [128, 64], F32)
    er = Rot(P, "er", 2, [64, 128], F32)
    qt_ = Rot(P, "qt_", 2, [128, 64], BF16)
    kt_ = Rot(P, "kt_", 2, [128, 64], BF16)
    qh_ = Rot(P, "qh_", 2, [128, 64], BF16)
    kb_ = Rot(P, "kb_", 2, [64, 128], BF16)
    atb = Rot(P, "atb", 2, [64, 64], BF16)
    hst = Rot(P, "hst", 2, [64, 2], F32)
    htmp = Rot(P, "htmp", 2, [64, 128], F32)
    ych = Rot(P, "ych", 2, [64, 128], BF16)
    bkX, bkY = banks[6], banks[7]

    def tm_src(nm, t0, hd):
        return d["s_" + nm].ap()[t0:t0 + 512, hd * 128:(hd + 1) * 128].rearrange("(c s) k -> s c k", s=CH)

    def hgrn_head(hd):
        h.memset("pool", Sf[hd].ap, 0.0, [Sf[hd]])
        sb = Sb[hd].next()
        h.memset("pool", sb.ap, 0.0, [sb])
        for g in range(NCH // GRP):
            t0 = g * 512
            q_ = gq.next(); k_ = gk.next(); lf = glf.next(); kk = gkk.next(); gt = ggt.next(); v_ = gv.next()
            h.dma("sp", q_.ap, d["s_hqT"].ap()[hd * 128:(hd + 1) * 128, t0:t0 + 512], ["s_hqT"], [q_], chan=q_.key)
            h.dma("sp", k_.ap, d["s_hkT"].ap()[hd * 128:(hd + 1) * 128, t0:t0 + 512], ["s_hkT"], [k_], chan=k_.key)
            for buf, nm in ((lf, "hlogf"), (kk, "hkk"), (gt, "hgate"), (v_, "hv")):
                h.dma("sp", buf.ap, tm_src(nm, t0, hd), ["s_" + nm], [buf], chan=buf.key)
            yo = yst.next()
            for c in range(GRP):
                qc = q_.ap[:, c * CH:(c + 1) * CH]; kc = k_.ap[:, c * CH:(c + 1) * CH]
                h.mm(bkX.ap[:, 0:128], lf.ap[:, c, :], ucat.ap, True, True, [lf, ucat], [bkX])
                h.mm(bkX.ap[0:64, 128:256], lmat.ap, lf.ap[:, c, :], True, True, [lf, lmat], [bkX])
                e1 = eg.next(); e2 = ek.next(); e3 = er.next()
                h.act(e1.ap, bkX.ap[:, 0:128], AF.Exp, [bkX], [e1])
                h.act(e2.ap, bkX.ap[:, 64:128], AF.Exp, [bkX], [e2], scale=-1.0)
                h.act(e3.ap, bkX.ap[0:64, 128:256], AF.Exp, [bkX], [e3])
                qt = qt_.next(); kt = kt_.next(); qh = qh_.next(); kb = kb_.next()
                h.tt("dve", qt.ap, qc, e1.ap[:, 64:128], ALU.mult, [q_, e1], [qt])
                h.tt("pool", kt.ap, kc, e2.ap, ALU.mult, [k_, e2], [kt])
                h.tt("dve", qh.ap, qc, e1.ap[:, 0:64], ALU.mult, [q_, e1], [qh])
                h.tt("pool", kb.ap, kk.ap[:, c, :], e3.ap, ALU.mult, [kk, e3], [kb])
                yield
                h.mm(bkX.ap[0:64, 256:320], kt.ap, qt.ap, True, True, [kt, qt], [bkX])
                at = atb.next()
                h.tt("dve", at.ap, bkX.ap[0:64, 256:320], ucat.ap[:, 0:64], ALU.mult, [bkX, ucat], [at])
                yield
                h.mm(bkY.ap[0:64, 0:128], at.ap, v_.ap[:, c, :], True, False, [at, v_], [bkY])
                h.mm(bkY.ap[0:64, 0:128], qh.ap, sb.ap, False, True, [qh, sb], [bkY])
                h.mm(bkY.ap[:, 128:256], kb.ap, v_.ap[:, c, :], True, True, [kb, v_], [bkY])
                h.stt("dve", Sf[hd].ap, Sf[hd].ap, e1.ap[:, 63:64], bkY.ap[:, 128:256], ALU.mult, ALU.add, [Sf[hd], e1, bkY], [Sf[hd]])
                sb = Sb[hd].next()
                h.cp("pool", sb.ap, Sf[hd].ap, [Sf[hd]], [sb])
                yield
                st = hst.next(); tm = htmp.next(); tm2 = htmp.next()
                h.memset("pool", st.ap, 0.0, [st])
                h.act(tm.ap, bkY.ap[0:64, 0:128], AF.Square, [bkY, st], [tm, st], accum=st.ap[:, 0:1])
                h.ts("dve", st.ap[:, 0:1], st.ap[:, 0:1], 1.0 / 128, EPS, ALU.mult, ALU.add, [st], [st])
                h.act(st.ap[:, 0:1], st.ap[:, 0:1], AF.Sqrt, [st], [st])
                h.recip(st.ap[:, 0:1], st.ap[:, 0:1], [st], [st])
                h.stt("dve", tm2.ap, bkY.ap[0:64, 0:128], st.ap[:, 0:1], ng.ap[:, hd, :], ALU.mult, ALU.mult, [bkY, st, ng], [tm2])
                yc_ = ych.next()
                h.tt("pool", yc_.ap, tm2.ap, gt.ap[:, c, :], ALU.mult, [tm2, gt], [yc_])
                ptb = bkY.ap.bitcast(BF16)
                h.tr(ptb[:, 512:576], yc_.ap, identb.ap[0:64, 0:64], [yc_, identb], [bkY])
                h.cp("act", yo.ap[:, c * CH:(c + 1) * CH], ptb[:, 512:576], [bkY], [yo])
                yield
            h.dma("pool", d["s_ybT"].ap()[hd * 128:(hd + 1) * 128, t0:t0 + 512], yo.ap, [yo], ["s_ybT"], chan=yo.key)

    aq = Rot(P, "aq", 2, [128, S], BF16)
    aqr = Rot(P, "aqr", 2, [64, S], BF16)
    akn = Rot(P, "akn", 2, [128, S], BF16)
    akr = Buf(P.sbuf("akr", [64, S], BF16), "akr")
    av = Rot(P, "av", 2, [128, 32, 128], BF16)
    pt_ = Rot(P, "pt_", 3, [128, 512], BF16)
    rin = Rot(P, "rin", 2, [128, 512], F32)
    oo = Rot(P, "oo", 2, [128, 512], BF16)

    class BRot:
        def __init__(self, bufs):
            self.bufs = bufs; self.i = 0

        def next(self):
            b_ = self.bufs[self.i % len(self.bufs)]; self.i += 1; return b_
    stps = BRot(banks[0:2]); otps = BRot(banks[2:4]); rsps = BRot(banks[4:6])
    h.dma("sp", akr.ap, d["s_KrT"].ap(), ["s_KrT"], [akr], chan="akr")
    SCALE = 192 ** -0.5

    def attn_head(hd):
        q_ = aq.next(); qr = aqr.next(); kn = akn.next(); v_ = av.next()
        r0 = hd * 192
        h.dma("sp", q_.ap, d["s_QT"].ap()[r0:r0 + 128, :], ["s_QT"], [q_], chan=q_.key)
        h.dma("sp", qr.ap, d["s_QT"].ap()[r0 + 128:r0 + 192, :], ["s_QT"], [qr], chan=qr.key)
        h.dma("sp", kn.ap, d["s_KnT"].ap()[hd * 128:(hd + 1) * 128, :], ["s_KnT"], [kn], chan=kn.key)
        for part in range(4):
            rows = slice(part * 1024, (part + 1) * 1024)
            h.dma("sp", v_.ap[:, part * 8:(part + 1) * 8, :],
                  d["s_Vv"].ap()[rows, hd * 128:(hd + 1) * 128].rearrange("(j k) d -> k j d", k=128), ["s_Vv"], [v_], chan=v_.key)
        for i in range(8):
            qs = slice(i * 512, (i + 1) * 512)
            ot = otps.next(); rs = rsps.next()
            nk = 4 * i + 4

            def emit_st(j):
                ks = slice(j * 128, (j + 1) * 128)
                st = stps.next()
                h.mm(st.ap, kn.ap[:, ks], q_.ap[:, qs], True, False, [kn, q_], [st])
                h.mm(st.ap, akr.ap[:, ks], qr.ap[:, qs], False, True, [akr, qr], [st])
                return st
            st_next = emit_st(0)
            for j in range(nk):
                st = st_next
                if j + 1 < nk:
                    st_next = emit_st(j + 1)
                pt = pt_.next()
                h.act(pt.ap, st.ap, AF.Exp, [st], [pt], scale=SCALE)
                if j >= 4 * i:
                    m = j - 4 * i
                    h.tt("dve", pt.ap, pt.ap, cmask.ap[:, m, :], ALU.mult, [pt, cmask], [pt])
                h.mm(ot.ap, v_.ap[:, j, :], pt.ap, j == 0, j == nk - 1, [v_, pt], [ot])
                h.mm(rs.ap, onesb.ap, pt.ap, j == 0, j == nk - 1, [onesb, pt], [rs])
                yield
            ri = rin.next(); o = oo.next()
            h.recip(ri.ap, rs.ap, [rs], [ri])
            h.tt("dve", o.ap, ot.ap, ri.ap, ALU.mult, [ot, ri], [o])
            h.dma("pool", d["s_ycT"].ap()[hd * 128:(hd + 1) * 128, qs], o.ap, [o], ["s_ycT"], chan=o.key)

    def chain(gens):
        for g_ in gens:
            yield from g_
    Hs = chain([hgrn_head(hd) for hd in range(4)])
    Ts = chain([attn_head(hd) for hd in range(8)])
    alive_h = alive_t = True
    while alive_h or alive_t:
        if alive_t:
            try:
                next(Ts)
            except StopIteration:
                alive_t = False
        if alive_h:
            try:
                next(Hs)
            except StopIteration:
                alive_h = False


NT = FT // 128
ALPHA = (2 * 4) ** 0.25
SBK = 3
PADG = SBK * 128
NSB = (2 * FT) // PADG + 32
NBLK = NSB * SBK


def emit_C(P, h, d, l, last):
    x_src = d["x"] if l == 0 else d["xres_s"]
    x_dst = d["xo"] if last else d["xres_s"]
    lngb = Buf(P.sbuf("c_lngb", [128, 2, D], F32), "c_lngb")
    wr = Buf(P.sbuf("wr", [128, 16, 36], F32), "wr")
    br = Buf(P.sbuf("br", [128, 36], F32), "br")
    ident = Buf(P.sbuf("identf", [128, 128], F32), "identf")
    identb = Buf(P.sbuf("identb", [128, 128], BF16), "identb")
    ustr = Buf(P.sbuf("ustr", [128, 128], BF16), "ustr")
    onesb = Buf(P.sbuf("conesb", [128, 128], BF16), "conesb")
    blkst = Buf(P.sbuf("blkst", [128, NSB], F32), "blkst")
    pq = Buf(P.sbuf("pq", [128, 4], F32), "pq")
    widxf = Buf(P.sbuf("widxf", [128, NSB], F32), "widxf")
    trl = Buf(P.sbuf("trl", [128, NSB], F32), "trl")
    widx = Buf(P.sbuf("widx", [128, 4, NSB], I32), "widx")
    lg = Buf(P.sbuf("lg", [128, 36], F32), "lg")
    rs = Buf(P.sbuf("rs", [128, 64], F32), "rs")
    OHall = Buf(P.sbuf("OHall", [128, NT, 2, 32], F32), "OHall")
    gates = Buf(P.sbuf("gates", [128, NT, 2], F32), "gates")
    cntb = Buf(P.sbuf("cntb", [128, NT, 32], BF16), "cntb")
    stat = Rot(P, "cstat", 2, [128, 2], F32)
    tot = Buf(P.sbuf("tot", [128, 32], F32), "tot")
    scA = Buf(P.sbuf("scA", [128, 32], F32), "scA")
    scB = Buf(P.sbuf("scB", [128, 32], F32), "scB")
    padded = Buf(P.sbuf("padded", [128, 32], F32), "padded")
    pstart = Buf(P.sbuf("pstart", [128, 32], F32), "pstart")
    sci = Buf(P.sbuf("sci", [128, 64], I32), "sci")
    bacc = Buf(P.sbuf("bacc", [128, NSB], F32), "bacc")
    posf = Buf(P.sbuf("posf", [128, NT * 2], F32), "posf")
    posi = Buf(P.sbuf("posi_c", [128, NT * 2], I32), "posi_c")
    wst = Rot(P, "cwst", 3, [128, 4, 512], F32)
    x1br = Rot(P, "x1br", 2, [128, D], BF16)
    P.stage_reset(keep=True)
    banks = [Buf(P.psum(f"cbank{i}", [128, 512], F32), f"cbank{i}") for i in range(8)]
    for bk in banks:
        P.exclusive.add(bk.key)

    for b_, ap_ in ((wr, d["c_wr"].ap()[l].rearrange("(c p) n -> p c n", p=128)), (br, d["c_br"].ap()[l]), (ident, d["c_ident"].ap()),
                    (identb, d["c_identb"].ap()), (ustr, d["c_ustr"].ap()), (onesb, d["b_onesb"].ap()), (blkst, d["c_blkst"].ap()),
                    (pq, d["c_pq"].ap()), (lngb, d["c_ln1"].ap()[l])):
        h.dma("sp", b_.ap, ap_, [], [b_], chan="const")

    cast_i = [0]

    def cast_eng():
        cast_i[0] += 1
        return ("dve", "pool")[cast_i[0] % 2]

    def layer_norm(z, junk, out_ap, out_writes):
        st = stat.next()
        h.memset("pool", st.ap, 0.0, [st])
        h.red("dve", st.ap[:, 0:1], z.ap, ALU.add, [z, st], [st])
        h.ts("dve", st.ap[:, 0:1], st.ap[:, 0:1], 1.0 / D, None, ALU.mult, None, [st], [st])
        h.ts("dve", z.ap, z.ap, st.ap[:, 0:1], None, ALU.subtract, None, [z, st], [z])
        h.act(junk.ap, z.ap, AF.Square, [z, st], [junk, st], accum=st.ap[:, 1:2])
        h.ts("dve", st.ap[:, 1:2], st.ap[:, 1:2], 1.0 / D, EPS, ALU.mult, ALU.add, [st], [st])
        h.act(st.ap[:, 1:2], st.ap[:, 1:2], AF.Sqrt, [st], [st])
        h.recip(st.ap[:, 1:2], st.ap[:, 1:2], [st], [st])
        h.stt("dve", junk.ap, z.ap, st.ap[:, 1:2], lngb.ap[:, 0, :], ALU.mult, ALU.mult, [z, st, lngb], [junk])
        h.tt("pool", out_ap, junk.ap, lngb.ap[:, 1, :], ALU.add, [junk, lngb], out_writes)

    wout_bf = Buf(P.sbuf("wout_bf", [128, 16, D], BF16), "wout_bf")
    yTr = Rot(P, "yTr", 2, [128, 16, 128], BF16)
    x1T = Buf(P.sbuf("x1T", [128, 16, 128], F32), "x1T")
    xr = Rot(P, "xr", 3, [128, D], F32)
    zb = Buf(P.sbuf("zb", [128, D], F32), "zb")
    jk = Buf(P.sbuf("jk", [128, D], F32), "jk")
    wout_v = d["c_wout"].ap()[l].rearrange("(c p) n -> p c n", p=128)
    for q in range(4):
        for n in range(4):
            st = wst.next()
            h.dma("sp", st.ap, wout_v[:, q * 4:(q + 1) * 4, n * 512:(n + 1) * 512], [], [st], chan=st.key)
            h.cp(cast_eng(), wout_bf.ap[:, q * 4:(q + 1) * 4, n * 512:(n + 1) * 512], st.ap, [st], [wout_bf])
    fm = lambda t_, rows: t_.ap()[rows, :].rearrange("(c p) t -> p c t", p=128)
    def mix_mm(tt):
        tok = tt * 128
        yt = yTr.next(); xt = xr.next()
        tsl = slice(tok, tok + 128)
        h.dma("sp", yt.ap[:, 0:4, :], fm(d["yaT_s"], slice(0, 512))[:, :, tsl], ["yaT_s"], [yt], chan=yt.key)
        h.dma("sp", yt.ap[:, 4:8, :], fm(d["s_ybT"], slice(0, 512))[:, :, tsl], ["s_ybT"], [yt], chan=yt.key)
        h.dma("sp", yt.ap[:, 8:12, :], fm(d["s_ycT"], slice(0, 512))[:, :, tsl], ["s_ycT"], [yt], chan=yt.key)
        h.dma("sp", yt.ap[:, 12:16, :], fm(d["s_ycT"], slice(512, 1024))[:, :, tsl], ["s_ycT"], [yt], chan=yt.key)
        h.dma("sp", xt.ap, x_src.ap()[tok:tok + 128, :], [("xres", tt)], [xt], chan=xt.key)
        for n in range(4):
            for c in range(16):
                h.mm(banks[n].ap, yt.ap[:, c, :], wout_bf.ap[:, c, n * 512:(n + 1) * 512], c == 0, c == 15, [yt, wout_bf], [banks[n]])
        return xt
    nxt_xt = mix_mm(0)
    for tt in range(NT):
        tok = tt * 128
        xt = nxt_xt
        for n in range(4):
            h.stt("dve", zb.ap[:, n * 512:(n + 1) * 512], xt.ap[:, n * 512:(n + 1) * 512], ALPHA, banks[n].ap, ALU.mult, ALU.add, [xt, banks[n]], [zb])
        if tt + 1 < NT:
            nxt_xt = mix_mm(tt + 1)
        layer_norm(zb, jk, xt.ap, [xt])
        h.dma("pool", d["c_x1"].ap()[tok:tok + 128, :], xt.ap, [xt], [("x1_d", tt)], chan="x1st" + xt.key)
        xb = x1br.next()
        h.cp("pool", xb.ap, xt.ap, [xt], [xb])
        h.dma("pool", d["c_x1b"].ap()[tok:tok + 128, :], xb.ap, [xb], [("x1b_d", tt)], chan=xb.key)
        for c4 in range(4):
            pt = banks[4 + c4 % 2]
            for i in range(4):
                c = c4 * 4 + i
                h.tr(pt.ap[:, i * 128:(i + 1) * 128], xt.ap[:, c * 128:(c + 1) * 128], ident.ap, [xt, ident], [pt])
            h.cp("act", x1T.ap[:, c4 * 4:(c4 + 1) * 4, :].rearrange("p c t -> p (c t)"), pt.ap, [pt], [x1T])
        lp = banks[6]
        for c in range(16):
            h.mm(lp.ap[:, 0:36], x1T.ap[:, c, :], wr.ap[:, c, :], c == 0, c == 15, [x1T, wr], [lp])
        h.tt("dve", lg.ap, lp.ap[:, 0:36], br.ap, ALU.add, [lp, br], [lg])
        R = [rs, lg]
        s = rs.ap
        h.memset("pool", s, 0.0, [rs])
        h.red("dve", s[:, 0:1], lg.ap[:, 0:4], ALU.max, R, [rs])
        h.ts("dve", s[:, 4:8], lg.ap[:, 0:4], s[:, 0:1], None, ALU.is_equal, None, R, [rs])
        h.ts("dve", s[:, 8:12], lg.ap[:, 0:4], s[:, 0:1], None, ALU.subtract, None, R, [rs])
        h.act(s[:, 8:12], s[:, 8:12], AF.Exp, [rs], [rs], accum=s[:, 1:2])
        h.recip(s[:, 2:3], s[:, 1:2], [rs], [rs])
        h.ts("dve", s[:, 12:20], lg.ap[:, 4:12], s[:, 4:5], None, ALU.mult, None, R, [rs])
        for g in range(1, 4):
            h.stt("dve", s[:, 12:20], lg.ap[:, 4 + 8 * g:12 + 8 * g], s[:, 4 + g:5 + g], s[:, 12:20], ALU.mult, ALU.add, R, [rs])
        h.red("dve", s[:, 20:21], s[:, 12:20], ALU.max, [rs], [rs])
        h.ts("dve", s[:, 24:32], s[:, 12:20], s[:, 20:21], None, ALU.is_equal, None, [rs], [rs])
        h.stt("dve", s[:, 32:40], s[:, 24:32], -1e30, s[:, 12:20], ALU.mult, ALU.add, [rs], [rs])
        h.red("dve", s[:, 21:22], s[:, 32:40], ALU.max, [rs], [rs])
        h.ts("dve", s[:, 40:48], s[:, 32:40], s[:, 21:22], None, ALU.is_equal, None, [rs], [rs])
        h.tt("dve", s[:, 22:23], s[:, 21:22], s[:, 20:21], ALU.subtract, [rs], [rs])
        h.act(s[:, 22:23], s[:, 22:23], AF.Exp, [rs], [rs])
        h.ts("dve", s[:, 23:24], s[:, 22:23], 1.0, None, ALU.add, None, [rs], [rs])
        h.recip(s[:, 23:24], s[:, 23:24], [rs], [rs])
        h.tt("dve", gates.ap[:, tt, 0:1], s[:, 2:3], s[:, 23:24], ALU.mult, [rs, gates], [gates])
        h.tt("dve", gates.ap[:, tt, 1:2], gates.ap[:, tt, 0:1], s[:, 22:23], ALU.mult, [rs, gates], [gates])
        for g in range(4):
            h.ts("dve", OHall.ap[:, tt, 0, g * 8:(g + 1) * 8], s[:, 24:32], s[:, 4 + g:5 + g], None, ALU.mult, None, [rs, OHall], [OHall])
            h.ts("dve", OHall.ap[:, tt, 1, g * 8:(g + 1) * 8], s[:, 40:48], s[:, 4 + g:5 + g], None, ALU.mult, None, [rs, OHall], [OHall])
        h.tt("dve", cntb.ap[:, tt, :], OHall.ap[:, tt, 0, :], OHall.ap[:, tt, 1, :], ALU.add, [OHall, cntb], [cntb])

    tp = banks[7]
    for t in range(NT):
        h.mm(tp.ap[:, 0:32], onesb.ap, cntb.ap[:, t, :], t == 0, t == NT - 1, [onesb, cntb], [tp])
    h.cp("dve", tot.ap, tp.ap[:, 0:32], [tp], [tot])
    h.ts("dve", scA.ap, tot.ap, float(PADG - 1), None, ALU.add, None, [tot], [scA])
    h.ts("dve", scB.ap, scA.ap, 1.0 / PADG, 0.001, ALU.mult, ALU.add, [scA], [scB])
    h.cp("dve", sci.ap[:, 0:32], scB.ap, [scB], [sci])
    h.cp("dve", scB.ap, sci.ap[:, 0:32], [sci], [scB])
    h.ts("dve", padded.ap, scB.ap, float(PADG), None, ALU.mult, None, [scB], [padded])
    h.tt("dve", padded.ap, padded.ap, scA.ap, ALU.is_gt, [padded, scA], [padded])
    h.tt("dve", scB.ap, scB.ap, padded.ap, ALU.subtract, [scB, padded], [scB])
    h.ts("dve", padded.ap, scB.ap, float(PADG), None, ALU.mult, None, [scB], [padded])
    h.cp("dve", scA.ap, padded.ap, [padded], [scA])
    A_, B_ = scA, scB
    for sh in (1, 2, 4, 8, 16):
        h.cp("dve", B_.ap[:, 0:sh], A_.ap[:, 0:sh], [A_], [B_])
        h.tt("dve", B_.ap[:, sh:32], A_.ap[:, sh:32], A_.ap[:, 0:32 - sh], ALU.add, [A_], [B_])
        A_, B_ = B_, A_
    pend = A_
    h.tt("dve", pstart.ap, pend.ap, padded.ap, ALU.subtract, [pend, padded], [pstart])
    h.memset("pool", bacc.ap, 0.0, [bacc])
    for e in range(32):
        h.stt("dve", bacc.ap, blkst.ap, pend.ap[:, e:e + 1], bacc.ap, ALU.is_ge, ALU.add, [blkst, pend, bacc], [bacc])
    h.ts("dve", bacc.ap, bacc.ap, 31.0, None, ALU.min, None, [bacc], [bacc])
    h.ts("dve", trl.ap, blkst.ap, pend.ap[:, 31:32], 1.0e6, ALU.is_ge, ALU.mult, [blkst, pend], [trl])
    for q in range(4):
        h.ts("dve", widxf.ap, bacc.ap, 512.0, pq.ap[:, q:q + 1], ALU.mult, ALU.add, [bacc, pq], [widxf])
        h.ts("dve", widxf.ap, widxf.ap, float(l * 16384), None, ALU.add, None, [widxf], [widxf])
        h.tt("dve", widxf.ap, widxf.ap, trl.ap, ALU.add, [widxf, trl], [widxf])
        h.cp("dve", widx.ap[:, q, :], widxf.ap, [widxf], [widx])
    for tt in range(NT):
        rp = banks[tt % 2]
        for t in range(tt):
            h.mm(rp.ap[:, 0:32], onesb.ap, cntb.ap[:, t, :], t == 0, False, [onesb, cntb], [rp])
        h.mm(rp.ap[:, 0:32], ustr.ap, cntb.ap[:, tt, :], tt == 0, True, [ustr, cntb], [rp])
        h.tt("dve", B_.ap, rp.ap[:, 0:32], pstart.ap, ALU.add, [rp, pstart], [B_])
        for k in range(2):
            h.tt("dve", padded.ap, B_.ap, OHall.ap[:, tt, k, :], ALU.mult, [B_, OHall], [padded])
            h.red("dve", posf.ap[:, tt * 2 + k:tt * 2 + k + 1], padded.ap, ALU.add, [padded, posf], [posf])
    h.cp("dve", posi.ap, posf.ap, [posf], [posi])
    xs_d = d["c_xs"]; ys_d = d["c_ys"]
    for tt in range(NT):
        xb = x1br.next()
        h.dma("sp", xb.ap, d["c_x1b"].ap()[tt * 128:(tt + 1) * 128, :], [("x1b_d", tt)], [xb], chan=xb.key)
        for k in range(2):
            col = tt * 2 + k
            def scat(e, xb=xb, col=col):
                return e.indirect_dma_start(out=xs_d.ap(), out_offset=bass.IndirectOffsetOnAxis(ap=_r(posi.ap)[:, col:col + 1], axis=0),
                                            in_=_r(xb.ap), in_offset=None)
            P.dma("pool", None, None, reads=keys([xb, posi]), writes=[("xs_d", col)], chan=f"scat{col % 4}", indirect=scat)

    P.stage_reset()
    wsets = []
    for i in range(2):
        wsets.append((Buf(P.sbuf(f"wg{i}", [128, 16, 512], BF16), f"wg{i}"),
                      Buf(P.sbuf(f"wu{i}", [128, 16, 512], BF16), f"wu{i}"),
                      Buf(P.sbuf(f"wd{i}", [128, 4, 2048], BF16), f"wd{i}")))
    xbl = [Buf(P.sbuf(f"xbl{i}", [128, D], BF16), f"xbl{i}") for i in range(2)]
    xsT = [Buf(P.sbuf(f"xsT{i}", [128, 16, 128], BF16), f"xsT{i}") for i in range(2)]
    hsg = Rot(P, "hsg", 2, [128, 512], F32)
    hTb = Rot(P, "hTb", 2, [128, 512], BF16)
    ysb = Rot(P, "ysb", 2, [128, D], F32)
    h.dma("sp", lngb.ap, d["c_ln2"].ap()[l], [], [lngb], chan="const2")
    wg_v = d["c_wg"].ap().rearrange("l e (p q c) n -> (l e p q) (c n)", q=4, c=4)
    wu_v = d["c_wu"].ap().rearrange("l e (p q c) n -> (l e p q) (c n)", q=4, c=4)
    wd_v = d["c_wd"].ap().rearrange("l e (p q) n -> (l e p q) n", q=4)
    cast3 = [0]

    def cast3_eng():
        cast3[0] += 1
        return ("dve", "act")[cast3[0] % 2]

    ybk = [banks[0], banks[1]]
    ybi = [0]

    def load_weights(sb):
        wgb, wub, wdb = wsets[sb % 2]
        for wv, wbuf, isdown in ((wg_v, wgb, False), (wu_v, wub, False), (wd_v, wdb, True)):
            for q in range(4):
                st = wst.next()
                def ld(e, st=st, wv=wv, q=q, sb=sb):
                    if not hasattr(P, "_bcv"):
                        r_ = e.alloc_register("bcreg")
                        e.reg_mov(r_, 4 * 32 * 512 - 1)
                        P._bcv = e.snap(r_)
                    return e.indirect_dma_start(out=_r(st.ap).rearrange("p c n -> p (c n)"), out_offset=None, in_=wv,
                                                in_offset=bass.IndirectOffsetOnAxis(ap=_r(widx.ap)[:, q, sb:sb + 1], axis=0),
                                                bounds_check=P._bcv, oob_is_err=False)
                P.dma("pool", None, None, reads=keys([widx]), writes=keys([st]), chan=st.key, indirect=ld)
                if isdown:
                    h.cp(cast3_eng(), wbuf.ap[:, q, :], st.ap.rearrange("p c n -> p (c n)"), [st], [wbuf])
                else:
                    h.cp(cast3_eng(), wbuf.ap[:, q * 4:(q + 1) * 4, :], st.ap, [st], [wbuf])

    def emit_T(b):
        xb = xbl[b % 2]; xt_ = xsT[b % 2]
        h.dma("sp", xb.ap, xs_d.ap()[b * 128:(b + 1) * 128, :], [("xs_d", c_) for c_ in range(NT * 2)], [xb], chan=xb.key)
        for half in range(2):
            pt = banks[6 + half]
            ptb = pt.ap.bitcast(BF16)
            for i in range(8):
                c_ = half * 8 + i
                h.tr(ptb[:, i * 128:(i + 1) * 128], xb.ap.rearrange("s (p c) -> s c p", c=16)[:, c_, :], identb.ap, [xb, identb], [pt])
            h.cp(("act", "dve")[half], xt_.ap[:, half * 8:(half + 1) * 8, :].rearrange("p c t -> p (c t)"), ptb, [pt], [xt_])

    hT_of = {}
    hm_of = {}
    hmr = Rot(P, "hmr", 2, [128, 512], BF16)

    def emit_G(b):
        wgb, wub, wdb = wsets[(b // SBK) % 2]
        xt_ = xsT[b % 2]
        hg = banks[2 + (b % 2) * 2]; hu = banks[3 + (b % 2) * 2]
        for wbuf, bank in ((wgb, hg), (wub, hu)):
            for c_ in range(16):
                h.mm(bank.ap, xt_.ap[:, c_, :], wbuf.ap[:, c_, :], c_ == 0, c_ == 15, [wbuf, xt_], [bank])
        sg = hsg.next(); hm = hmr.next()
        h.act(sg.ap, hg.ap, AF.Silu, [hg], [sg])
        h.tt("dve", hm.ap, sg.ap, hu.ap, ALU.mult, [sg, hu], [hm])
        hm_of[b] = hm

    def emit_HT(b):
        hm = hm_of.pop(b)
        yk = ybk[ybi[0] % 2]; ybi[0] += 1
        ykb = yk.ap.bitcast(BF16)
        for cf in range(4):
            h.tr(ykb[:, cf * 128:(cf + 1) * 128], hm.ap.rearrange("s (f4 cf) -> s cf f4", cf=4)[:, cf, :], identb.ap, [hm, identb], [yk])
        hT = hTb.next()
        h.cp("act", hT.ap, ykb[:, 0:512], [yk], [hT])
        hT_of[b] = hT

    def emit_D(b):
        wgb, wub, wdb = wsets[(b // SBK) % 2]
        hT = hT_of.pop(b)
        yb_ = ysb.next()
        for n in range(4):
            yk = ybk[ybi[0] % 2]; ybi[0] += 1
            for f in range(4):
                h.mm(yk.ap, hT.ap[:, f * 128:(f + 1) * 128], wdb.ap[:, f, n * 512:(n + 1) * 512], f == 0, f == 3, [hT, wdb], [yk])
            h.cp(("act", "dve")[n % 2], yb_.ap[:, n * 512:(n + 1) * 512], yk.ap, [yk], [yb_])
        h.dma("sp", ys_d.ap()[b * 128:(b + 1) * 128, :], yb_.ap, [yb_], [("ys_d", b)], chan=yb_.key)

    load_weights(0)
    load_weights(1)
    emit_T(0); emit_G(0); emit_HT(0); emit_T(1)
    for b in range(NBLK):
        if b + 1 < NBLK:
            emit_G(b + 1)
        emit_D(b)
        if b + 1 < NBLK:
            emit_HT(b + 1)
        if b + 2 < NBLK:
            emit_T(b + 2)
        if b % SBK == SBK - 1:
            sb_next = b // SBK + 2
            if sb_next < NSB:
                load_weights(sb_next)

    P.stage_reset()
    y0 = Rot(P, "y0_", 2, [128, D], F32)
    y1 = Rot(P, "y1_", 2, [128, D], F32)
    x1r = Rot(P, "x1r", 2, [128, D], F32)
    jk2 = Buf(P.sbuf("jk2", [128, D], F32), "jk2")
    ob = Rot(P, "ob", 2, [128, D], F32)
    xtb_rot = Rot(P, "c_xtb", 2, [128, 16, 128], BF16)
    ys_reads = [("ys_d", b) for b in range(NBLK)]
    for tt in range(NT):
        tok = tt * 128
        a0 = y0.next(); a1 = y1.next(); xt = x1r.next(); o = ob.next()
        for k, dst in ((0, a0), (1, a1)):
            col = tt * 2 + k
            def gath(e, dst=dst, col=col):
                return e.indirect_dma_start(out=_r(dst.ap), out_offset=None, in_=ys_d.ap(),
                                            in_offset=bass.IndirectOffsetOnAxis(ap=_r(posi.ap)[:, col:col + 1], axis=0))
            P.dma("pool", None, None, reads=keys([posi]) + ys_reads, writes=keys([dst]), chan=dst.key, indirect=gath)
        h.dma("sp", xt.ap, d["c_x1"].ap()[tok:tok + 128, :], [("x1_d", tt)], [xt], chan=xt.key)
        h.act(a0.ap, a0.ap, AF.Copy, [a0, gates], [a0], scale=gates.ap[:, tt, 0:1])
        h.stt("dve", a0.ap, a1.ap, gates.ap[:, tt, 1:2], a0.ap, ALU.mult, ALU.add, [a1, gates, a0], [a0])
        h.stt("dve", a0.ap, xt.ap, ALPHA, a0.ap, ALU.mult, ALU.add, [xt, a0], [a0])
        layer_norm(a0, jk2, o.ap, [o])
        h.dma("sp", x_dst.ap()[tok:tok + 128, :], o.ap, [o], [("xres", tt)] if not last else [("xo", tt)], chan=o.key)
        if not last:
            emit_xT(P, h, d, o, tt, ident, xtb_rot, banks[4:8])
    P.arena_base = 0
    P.stage_reset()


import ml_dtypes as _mld

AB_NAMES = ["hqT", "hkT", "hlogf", "hkk", "hv", "hgate", "QT", "KnT", "KrT", "Vv"]
BC_NAMES = ["ybT", "ycT"]


def build_fused(nlayers=4):
    P = Prog(); P.enable_arena(); h = H(P)
    d = declare_dram(P)
    emit_X0(P, h, d)
    P.stage_reset()
    for l in range(nlayers):
        emit_A(P, h, d, l)
        P.stage_reset()
        emit_B(P, h, d, l)
        P.stage_reset()
        emit_C(P, h, d, l, last=(l == nlayers - 1))
    outs = [("xo", tt) for tt in range(FT // 128)]
    P.wait_all("sp", outs)
    P.wait_all("pool", outs)
    return P


def prep_fused_inputs(inp):
    bf = _mld.bfloat16
    Lr = range(4)
    sh = {}
    sh["w_in"] = np.ascontiguousarray(inp["w_in"], dtype=np.float32)
    sh["w_uq"] = np.ascontiguousarray(inp["mla_w_uq"], dtype=np.float32)
    sh["w_ukv"] = np.ascontiguousarray(inp["mla_w_ukv"], dtype=np.float32)
    sh["lngb"] = np.ascontiguousarray(np.broadcast_to(np.stack([inp["sgu_ln_g"], inp["sgu_ln_b"]], 1)[:, None], (4, 128, 2, 512))).astype(np.float32)
    sh["wsT"] = np.ascontiguousarray(inp["sgu_ws"].transpose(0, 3, 1, 2)).astype(np.float32)
    s_ = np.arange(128)
    sh["tri"] = (s_[:, None] <= s_[None, :]).astype(np.float32)
    sh["sgub"] = np.ascontiguousarray(inp["sgu_b"].transpose(0, 2, 1)).astype(np.float32)
    lg = np.asarray(inp["hgrn_lb_logits"], dtype=np.float32)
    sh["lbT"] = np.ascontiguousarray(lg.reshape(4, 4, 128).transpose(2, 0, 1))
    sh["lbB"] = np.ascontiguousarray(np.broadcast_to(lg[None], (128, 4, 512)))
    lm = np.zeros((4, 128, 4), np.float32)
    for l in Lr:
        for j in range(4):
            if 1 <= j <= l:
                lm[l, :, j] = 1.0
    sh["lmask"] = lm
    sh["qng"] = np.ascontiguousarray(inp["mla_qn_g"].reshape(4, 4, 128).transpose(0, 2, 1)).astype(np.float32)
    sh["kvng"] = np.ascontiguousarray(inp["mla_kvn_g"].reshape(4, 4, 128).transpose(0, 2, 1)).astype(np.float32)
    inv = (1.0 / (np.float32(10000.0) ** (np.arange(0, 64, 2, dtype=np.float32) / np.float32(64)))).astype(np.float32)
    rc = np.zeros((64, 2), np.float32)
    rc[:, 0] = np.concatenate([inv, inv]) / np.float32(2 * np.pi)
    rc[:32, 1] = -1.0; rc[32:, 1] = 1.0
    sh["ropec"] = rc
    sh["ones"] = np.ones((128, 128), np.float32)
    s64 = np.arange(64)
    U = (s64[:, None] <= s64[None, :]).astype(np.float32)
    sh["b_ucat"] = np.ascontiguousarray(np.concatenate([U, U - U[:, 31:32]], 1))
    sh["b_lmat"] = (s64[:, None] > s64[None, :]).astype(np.float32)
    k = np.arange(128)[:, None]; q = np.arange(512)[None, :]
    sh["b_cmask"] = np.ascontiguousarray(np.stack([(128 * m + k <= q) for m in range(4)], 1).astype(np.float32).astype(bf))
    sh["b_onesb"] = np.ones((128, 128), bf)
    sh["c_wout"] = np.ascontiguousarray(inp["w_out"], dtype=np.float32)
    sh["c_ln1"] = np.ascontiguousarray(np.broadcast_to(np.stack([inp["ln1_g"], inp["ln1_b"]], 1)[:, None], (4, 128, 2, 2048))).astype(np.float32)
    sh["c_ln2"] = np.ascontiguousarray(np.broadcast_to(np.stack([inp["ln2_g"], inp["ln2_b"]], 1)[:, None], (4, 128, 2, 2048))).astype(np.float32)
    sh["c_wr"] = np.ascontiguousarray(np.concatenate([inp["router_group_w"], inp["router_expert_w"]], 2)).astype(np.float32)
    sh["c_br"] = np.ascontiguousarray(np.broadcast_to(np.concatenate([inp["router_group_b"], inp["router_expert_b"]], 1)[:, None], (4, 128, 36))).astype(np.float32)
    sh["c_wg"] = np.ascontiguousarray(inp["expert_w_gate"], dtype=np.float32)
    sh["c_wu"] = np.ascontiguousarray(inp["expert_w_up"], dtype=np.float32)
    sh["c_wd"] = np.ascontiguousarray(inp["expert_w_down"], dtype=np.float32)
    sh["c_ident"] = np.eye(128, dtype=np.float32)
    sh["c_identb"] = np.eye(128, dtype=np.float32).astype(bf)
    n_ = np.arange(128)
    sh["c_ustr"] = (n_[:, None] < n_[None, :]).astype(np.float32).astype(bf)
    sh["c_blkst"] = np.ascontiguousarray(np.broadcast_to((np.arange(53, dtype=np.float32) * 384)[None], (128, 53)))
    sh["c_pq"] = (np.arange(128, dtype=np.float32)[:, None] * 4 + np.arange(4, dtype=np.float32)[None, :])
    xflat = np.ascontiguousarray(inp["x"], dtype=np.float32).reshape(-1, 2048)
    posflat = np.ascontiguousarray(inp["positions"]).astype(np.int32).reshape(-1)
    ng = np.asarray(inp["hgrn_norm_g"], dtype=np.float32)
    per = []
    sh["b_ng"] = np.ascontiguousarray(np.broadcast_to(ng.reshape(4, 1, 4, 128), (4, 64, 4, 128)))
    for c in range(8):
        b = c // 2
        m = {}
        m["x"] = np.ascontiguousarray(xflat[b * 4096:(b + 1) * 4096])
        m["posr"] = np.ascontiguousarray(np.broadcast_to(posflat[b * 4096:(b + 1) * 4096][None], (64, 4096)))
        per.append(m)
    return sh, per


_FCACHE = {}


def kernel(**inputs):
    inp = {k: np.asarray(v) for k, v in inputs.items()}
    if "nc" not in _FCACHE:
        _FCACHE["nc"] = build_fused(4).build()
    nc = _FCACHE["nc"]
    sh, per = prep_fused_inputs(inp)
    maps = []
    for c in range(8):
        m = dict(sh); m.update(per[c]); maps.append(m)
    res = run_bass_kernel_spmd(nc, maps, core_ids=list(range(8))).results
    out = np.concatenate([np.asarray(res[c]["xo"], dtype=np.float32)[(c % 2) * 2048:(c % 2 + 1) * 2048] for c in range(8)], axis=0)
    return out.reshape(4, 4096, 2048).astype(np.float32)
```

```python
import numpy as np
import concourse.bass as bass
import concourse.mybir as mybir
from concourse.bass_utils import run_bass_kernel_spmd

F32 = mybir.dt.float32
BF16 = mybir.dt.bfloat16
I32 = mybir.dt.int32
AF = mybir.ActivationFunctionType
ALU = mybir.AluOpType
AX = mybir.AxisListType

ENGS = ("pe", "act", "dve", "pool", "sp")
EPOCH = 16000
NSEMPOOL = 56
ARENA_BYTES = 204 * 1024


def _dsize(dt):
    return {F32: 4, BF16: 2, I32: 4}[dt]


class Prog:
    def __init__(self):
        self.nc = bass.Bass("TRN2", target_bir_lowering=False)
        self.ops = {e: [] for e in ENGS}
        self.cnt = {e: 0 for e in ENGS}
        self.known = {e: {} for e in ENGS}
        self.state = {}
        self.chan = {}
        self.semnames = []
        self.semset = set()
        self.tensors = []
        self.handles = {}
        self.n_waits = 0
        self.exclusive = set()
        self.arena_mode = False
        self.arena_off = 0
        self.arena_base = 0
        self.arena_peak = 0
        self.bank_i = 0
        self.chan_slot = {}

    def dram(self, name, shape, dtype, kind):
        t = self.nc.dram_tensor(name, list(shape), dtype, kind=kind)
        return t

    def enable_arena(self):
        self.arena_mode = True
        self.tensors.append(("sbuf", "ARENA", [128, ARENA_BYTES // 2], BF16))
        for i in range(8):
            self.tensors.append(("psum", f"BANK{i}", [128, 512], F32))

    def stage_reset(self, keep=False):
        self.barrier()
        if keep:
            self.arena_base = self.arena_off
        self.arena_off = self.arena_base
        self.bank_i = 0

    def sbuf(self, name, shape, dtype):
        if not self.arena_mode:
            self.tensors.append(("sbuf", name, list(shape), dtype))
            return _Lazy(self, name)
        n = 1
        for d_ in shape[1:]:
            n *= d_
        nb = n * _dsize(dtype)
        nb_al = (nb + 31) // 32 * 32
        off = self.arena_off
        self.arena_off += nb_al
        self.arena_peak = max(self.arena_peak, self.arena_off)
        assert self.arena_off <= ARENA_BYTES, f"SBUF arena overflow at {name}: {self.arena_off}"
        chain = [("idx", (slice(0, shape[0]), slice(off // 2, (off + nb) // 2)))]
        if dtype != BF16:
            chain.append(("bc", dtype))
        if len(shape) == 3:
            chain.append(("re", "p (a b) -> p a b", {"a": shape[1], "b": shape[2]}))
        elif len(shape) == 4:
            chain.append(("re", "p (a b c) -> p a b c", {"a": shape[1], "b": shape[2], "c": shape[3]}))
        return _Lazy(self, "ARENA", tuple(chain))

    def psum(self, name, shape, dtype=F32):
        if not self.arena_mode:
            self.tensors.append(("psum", name, list(shape), dtype))
            return _Lazy(self, name)
        assert list(shape) == [128, 512] and dtype == F32
        i = self.bank_i
        self.bank_i += 1
        assert i < 8, "out of PSUM banks"
        return _Lazy(self, f"BANK{i}")

    def _sem(self, name):
        if name not in self.semset:
            self.semset.add(name)
            self.semnames.append(name)
        return name

    def _eng_token(self, eng):
        self.cnt[eng] += 1
        c = self.cnt[eng] - 1
        ep, v = divmod(c, EPOCH)
        return (self._sem(f"c_{eng}_{ep}"), v + 1, 1)

    def _chan_token(self, chan):
        if self.arena_mode:
            if chan not in self.chan_slot:
                self.chan_slot[chan] = len(self.chan_slot) % NSEMPOOL
            chan = f"p{self.chan_slot[chan]}"
        self.chan[chan] = self.chan.get(chan, 0) + 1
        c = self.chan[chan] - 1
        ep, v = divmod(c, EPOCH // 16)
        return (self._sem(f"d_{chan}_{ep}"), (v + 1) * 16, 16)

    def _resolve(self, tok):
        name, val, step = tok
        if step == 16:
            chan, ep = name[2:].rsplit("_", 1)
            ep = int(ep)
            tot = self.chan[chan]
            per = EPOCH // 16
            cur_ep = (tot - 1) // per
            if ep == cur_ep:
                val = (tot - ep * per) * 16
            else:
                val = per * 16
        return name, val

    def _collect(self, eng, reads, writes):
        toks = []
        for k in reads:
            st = self.state.get(k)
            if st and st[0]:
                toks.append(st[0])
        for k in writes:
            st = self.state.get(k)
            if st:
                if st[0]:
                    toks.append(st[0])
                toks.extend(st[1].values())
        need = {}
        for t in toks:
            name, val = self._resolve(t)
            if name.startswith("c_pe_") and eng == "pe":
                continue
            if self.known[eng].get(name, 0) >= val:
                continue
            need[name] = max(need.get(name, 0), val)
        for n, v in need.items():
            self.known[eng][n] = v
        return list(need.items())

    def _update(self, tok, reads, writes):
        for k in reads:
            st = self.state.setdefault(k, [None, {}])
            st[1][tok[0]] = tok
        for k in writes:
            self.state[k] = [tok, {}]

    def _excl(self, reads, writes):
        if not self.exclusive:
            return reads, writes
        r2 = [k for k in reads if k not in self.exclusive]
        w2 = list(writes) + [k for k in reads if k in self.exclusive and k not in writes]
        return r2, w2

    def op(self, eng, fn, reads=(), writes=()):
        reads, writes = self._excl(reads, writes)
        waits = self._collect(eng, reads, writes)
        tok = self._eng_token(eng)
        self.n_waits += len(waits)
        self.ops[eng].append((waits, fn, (tok[0], 1)))
        self._update(tok, reads, writes)

    def dma(self, q, out, in_, reads=(), writes=(), chan=None, indirect=None, **kw):
        assert chan is not None
        waits = self._collect(q, reads, writes)
        tok = self._chan_token(chan)
        self.n_waits += len(waits)
        if indirect is None:
            fn = lambda e, out=out, in_=in_, kw=kw: e.dma_start(out=_r(out), in_=_r(in_), **kw)
        else:
            fn = indirect
        self.ops[q].append((waits, fn, (tok[0], 16)))
        self._update(tok, reads, writes)

    def raw(self, eng, fn, reads=(), writes=()):
        reads, writes = self._excl(reads, writes)
        waits = self._collect(eng, reads, writes)
        self.ops[eng].append((waits, fn, None))

    def barrier(self):
        targets = {}
        for e in ENGS:
            c = self.cnt[e]
            if c > 0:
                ep, v = divmod(c - 1, EPOCH)
                targets[f"c_{e}_{ep}"] = v + 1
        for chan, tot in self.chan.items():
            per = EPOCH // 16
            ep = (tot - 1) // per
            targets[f"d_{chan}_{ep}"] = (tot - ep * per) * 16
        for e in ENGS:
            waits = []
            for n, v in targets.items():
                if n.startswith("c_pe_") and e == "pe":
                    continue
                if self.known[e].get(n, 0) >= v:
                    continue
                self.known[e][n] = v
                waits.append((n, v))
            self.ops[e].append((waits, None, None))

    def wait_all(self, eng, keys):
        waits = self._collect(eng, (), keys)
        self.ops[eng].append((waits, None, None))

    def build(self):
        nc = self.nc
        from contextlib import ExitStack
        with ExitStack() as es:
            for kind, name, shape, dtype in self.tensors:
                if kind == "sbuf":
                    self.handles[name] = es.enter_context(nc.sbuf_tensor(name, shape, dtype))
                else:
                    self.handles[name] = es.enter_context(nc.psum_tensor(name, shape, dtype))
            sems = {}
            for n in self.semnames:
                sems[n] = es.enter_context(nc.semaphore(n))
            block = es.enter_context(nc.Block())

            def emit(engobj, lst):
                for waits, fn, inc in lst:
                    for n, v in waits:
                        engobj.wait_ge(sems[n], v)
                    if fn is not None:
                        ins = fn(engobj)
                        if inc is not None:
                            ins.then_inc(sems[inc[0]], inc[1])

            @block.tensor
            def _(e):
                emit(e, self.ops["pe"])

            @block.scalar
            def _(e):
                emit(e, self.ops["act"])

            @block.vector
            def _(e):
                emit(e, self.ops["dve"])

            @block.gpsimd
            def _(e):
                emit(e, self.ops["pool"])

            @block.sync
            def _(e):
                emit(e, self.ops["sp"])
        return nc


class _Lazy:
    def __init__(self, prog, name, chain=()):
        self.prog = prog
        self.name = name
        self.chain = chain

    def __getitem__(self, idx):
        return _Lazy(self.prog, self.name, self.chain + (("idx", idx),))

    def rearrange(self, pat, **kw):
        return _Lazy(self.prog, self.name, self.chain + (("re", pat, kw),))

    def bitcast(self, dt):
        return _Lazy(self.prog, self.name, self.chain + (("bc", dt),))

    def to_broadcast(self, shape):
        return _Lazy(self.prog, self.name, self.chain + (("tb", shape),))

    def resolve(self):
        h = self.prog.handles[self.name]
        if not self.chain:
            return h[:]
        cur = h if self.chain[0][0] == "idx" else h[:]
        for c in self.chain:
            if c[0] == "idx":
                cur = cur[c[1]]
            elif c[0] == "re":
                cur = cur.rearrange(c[1], **c[2])
            elif c[0] == "bc":
                cur = cur.bitcast(c[1])
            elif c[0] == "tb":
                cur = cur.to_broadcast(c[1])
        return cur


def _r(x):
    return x.resolve() if isinstance(x, _Lazy) else x


import math

NTOK = 2048
D = 2048
EPS = 1e-5


class Buf:
    def __init__(self, ap, key):
        self.ap = ap
        self.key = key


class Rot:
    def __init__(self, P, name, n, shape, dtype, psum=False):
        self.bufs = []
        for i in range(n):
            t = P.psum(f"{name}{i}", shape, dtype) if psum else P.sbuf(f"{name}{i}", shape, dtype)
            self.bufs.append(Buf(t, f"{name}{i}"))
        self.i = 0

    def next(self):
        b = self.bufs[self.i % len(self.bufs)]
        self.i += 1
        return b


def keys(lst):
    return [b.key if isinstance(b, Buf) else b for b in lst]


class H:
    def __init__(self, P):
        self.P = P

    def mm(self, out, lhsT, rhs, start, stop, reads, writes):
        self.P.op("pe", lambda e: e.matmul(_r(out), lhsT=_r(lhsT), rhs=_r(rhs), start=start, stop=stop),
                  reads=keys(reads), writes=keys(writes))

    def tr(self, out, in_, ident, reads, writes):
        self.P.op("pe", lambda e: e.transpose(_r(out), _r(in_), _r(ident)), reads=keys(reads), writes=keys(writes))

    def act(self, out, in_, func, reads, writes, scale=1.0, bias=None, accum=None, eng="act"):
        def fn(e):
            kw = {}
            if bias is not None:
                kw["bias"] = _r(bias)
            if accum is not None:
                kw["accum_out"] = _r(accum)
            return e.activation(out=_r(out), in_=_r(in_), func=func, scale=(_r(scale) if not isinstance(scale, float) else scale), **kw)
        self.P.op("act", fn, reads=keys(reads), writes=keys(writes))

    def tt(self, eng, out, a, b, op, reads, writes):
        self.P.op(eng, lambda e: e.tensor_tensor(out=_r(out), in0=_r(a), in1=_r(b), op=op), reads=keys(reads), writes=keys(writes))

    def ts(self, eng, out, a, s1, s2, op0, op1, reads, writes, accum=None):
        def fn(e):
            kw = {}
            if accum is not None:
                kw["accum_out"] = _r(accum)
            if s2 is None:
                return e.tensor_scalar(out=_r(out), in0=_r(a), scalar1=_r(s1), scalar2=None, op0=op0, **kw)
            return e.tensor_scalar(out=_r(out), in0=_r(a), scalar1=_r(s1), scalar2=_r(s2), op0=op0, op1=op1, **kw)
        self.P.op(eng, fn, reads=keys(reads), writes=keys(writes))

    def stt(self, eng, out, a, s, b, op0, op1, reads, writes):
        self.P.op(eng, lambda e: e.scalar_tensor_tensor(out=_r(out), in0=_r(a), scalar=_r(s), in1=_r(b), op0=op0, op1=op1),
                  reads=keys(reads), writes=keys(writes))

    def cp(self, eng, out, in_, reads, writes):
        if eng == "act":
            self.P.op("act", lambda e: e.copy(out=_r(out), in_=_r(in_)), reads=keys(reads), writes=keys(writes))
        else:
            self.P.op(eng, lambda e: e.tensor_copy(out=_r(out), in_=_r(in_)), reads=keys(reads), writes=keys(writes))

    def red(self, eng, out, in_, op, reads, writes):
        self.P.op(eng, lambda e: e.tensor_reduce(out=_r(out), in_=_r(in_), axis=AX.X, op=op), reads=keys(reads), writes=keys(writes))

    def recip(self, out, in_, reads, writes):
        self.P.op("dve", lambda e: e.reciprocal(out=_r(out), in_=_r(in_)), reads=keys(reads), writes=keys(writes))

    def memset(self, eng, out, val, writes):
        self.P.op(eng, lambda e: e.memset(_r(out), val), reads=(), writes=keys(writes))

    def dma(self, q, out, in_, reads, writes, chan):
        wk = []
        for k in keys(writes):
            if (isinstance(k, str) and (k.startswith("s_") or k == "yaT_s")) or (isinstance(k, tuple) and k[0] == "xT_s"):
                self.uq = getattr(self, "uq", 0) + 1
                k = ("uq", k, self.uq)
            wk.append(k)
        self.P.dma(q, out, in_, reads=keys(reads), writes=wk, chan=chan)


def gelu_tanh(h, ps, out, tmp_rot, shape_sl, reads_ps, writes_out, dve="dve"):
    xs = tmp_rot.next(); t1 = tmp_rot.next()
    h.cp("act", xs.ap[shape_sl], ps, reads_ps, [xs])
    h.act(t1.ap[shape_sl], ps, AF.Square, reads_ps, [t1])
    h.ts(dve, t1.ap[shape_sl], t1.ap[shape_sl], 0.044715, 1.0, ALU.mult, ALU.add, [t1], [t1])
    h.tt(dve, t1.ap[shape_sl], t1.ap[shape_sl], xs.ap[shape_sl], ALU.mult, [t1, xs], [t1])
    h.act(t1.ap[shape_sl], t1.ap[shape_sl], AF.Sigmoid, [t1], [t1], scale=2.0 * math.sqrt(2.0 / math.pi))
    h.tt(dve, out, t1.ap[shape_sl], xs.ap[shape_sl], ALU.mult, [t1, xs], writes_out)


def lb_compute(h, Lap, mask_ap, tmp_a, tmp_b, out_oml, keyL, n):
    ta, tb = tmp_a, tmp_b
    h.tt("dve", ta.ap, Lap(0), Lap(1), ALU.max, [keyL], [ta])
    h.tt("dve", ta.ap, ta.ap, Lap(2), ALU.max, [keyL, ta], [ta])
    h.tt("dve", ta.ap, ta.ap, Lap(3), ALU.max, [keyL, ta], [ta])
    for l in range(4):
        h.tt("dve", Lap(l), Lap(l), ta.ap, ALU.subtract, [keyL, ta], [keyL])
        h.act(Lap(l), Lap(l), AF.Exp, [keyL], [keyL])
    h.tt("dve", ta.ap, Lap(0), Lap(1), ALU.add, [keyL], [ta])
    h.tt("dve", ta.ap, ta.ap, Lap(2), ALU.add, [keyL, ta], [ta])
    h.tt("dve", ta.ap, ta.ap, Lap(3), ALU.add, [keyL, ta], [ta])
    h.recip(ta.ap, ta.ap, [ta], [ta])
    h.ts("dve", tb.ap, Lap(0), mask_ap(0), None, ALU.mult, None, [keyL, "lmask"], [tb])
    for l in range(1, 4):
        h.stt("dve", tb.ap, Lap(l), mask_ap(l), tb.ap, ALU.mult, ALU.add, [keyL, "lmask", tb], [tb])
    h.tt("dve", tb.ap, tb.ap, ta.ap, ALU.mult, [ta, tb], [tb])
    h.ts("dve", out_oml.ap, tb.ap, -1.0, 1.0, ALU.mult, ALU.add, [tb], [out_oml])


def build_A():
    P = Prog(); h = H(P)
    HT = NTOK // 2
    xT_d = P.dram("xT", [D, NTOK], F32, "ExternalInput")
    w_in_d = P.dram("w_in", [D, 4160], F32, "ExternalInput")
    w_uq_d = P.dram("w_uq", [512, 1536], F32, "ExternalInput")
    w_ukv_d = P.dram("w_ukv", [512, 2048], F32, "ExternalInput")
    lngb_d = P.dram("lngb", [128, 2, 512], F32, "ExternalInput")
    wsT_d = P.dram("wsT", [128, 4, 128], F32, "ExternalInput")
    tri_d = P.dram("tri", [128, 128], F32, "ExternalInput")
    sgub_d = P.dram("sgub", [128, 4], F32, "ExternalInput")
    lbT_d = P.dram("lbT", [128, 4, 4], F32, "ExternalInput")
    lbB_d = P.dram("lbB", [128, 4, 512], F32, "ExternalInput")
    lmask_d = P.dram("lmask", [128, 4], F32, "ExternalInput")
    qng_d = P.dram("qng", [128, 4], F32, "ExternalInput")
    kvng_d = P.dram("kvng", [128, 4], F32, "ExternalInput")
    pos_d = P.dram("posr", [64, NTOK], I32, "ExternalInput")
    ropec_d = P.dram("ropec", [64, 2], F32, "ExternalInput")
    ones_d = P.dram("ones", [128, 128], F32, "ExternalInput")

    ya_d = P.dram("ya", [NTOK, 512], BF16, "ExternalOutput")
    hqT_d = P.dram("hqT", [512, NTOK], F32, "ExternalOutput")
    hkT_d = P.dram("hkT", [512, NTOK], F32, "ExternalOutput")
    hlogf_d = P.dram("hlogf", [NTOK, 512], F32, "ExternalOutput")
    hkk_d = P.dram("hkk", [NTOK, 512], F32, "ExternalOutput")
    hv_d = P.dram("hv", [NTOK, 512], BF16, "ExternalOutput")
    hgate_d = P.dram("hgate", [NTOK, 512], F32, "ExternalOutput")
    QT_d = P.dram("QT", [8 * 192, NTOK], BF16, "ExternalOutput")
    KnT_d = P.dram("KnT", [1024, NTOK], BF16, "ExternalOutput")
    KrT_d = P.dram("KrT", [64, NTOK], BF16, "ExternalOutput")
    Vv_d = P.dram("Vv", [NTOK, 1024], BF16, "ExternalOutput")
    outs = ["ya_d", "hqT_d", "hkT_d", "hlogf_d", "hkk_d", "hv_d", "hgate_d", "QT_d", "KnT_d", "KrT_d", "Vv_d"]

    xT_bf = [Buf(P.sbuf(f"xTbf{i}", [128, 16, 512], BF16), f"xTbf{i}") for i in range(2)]
    w_st = Rot(P, "wst", 2, [128, 4, 512], F32)
    w_bf = Rot(P, "wbf", 2, [128, 16, 512], BF16)
    upw = Buf(P.sbuf("upw", [128, 4, 2048], BF16), "upw")
    tmp = Rot(P, "tmp", 6, [128, 512], F32)
    gur = Rot(P, "gur", 2, [128, 512], F32)
    gvr = Rot(P, "gvr", 2, [128, 512], F32)
    ostf = Rot(P, "ostf", 4, [128, 512], F32)
    ostb = Rot(P, "ostb", 4, [128, 512], BF16)
    c_sb = Buf(P.sbuf("c_sb", [128, 4, 512], F32), "c_sb")
    sq = Buf(P.sbuf("sq", [128, 4, 512], F32), "sq")
    cn = Rot(P, "cn", 2, [128, 4, 512], BF16)
    rstd = Buf(P.sbuf("rstd", [128, 512], F32), "rstd")
    cos2 = Buf(P.sbuf("cos2", [64, 512], F32), "cos2")
    sinS = Buf(P.sbuf("sinS", [64, 512], F32), "sinS")
    rt = [Buf(P.sbuf(f"rt{i}", [64, 512], F32), f"rt{i}") for i in range(2)]
    rti = Buf(P.sbuf("rti", [64, 512], I32), "rti")
    posi = Buf(P.sbuf("posi", [64, NTOK], I32), "posi")
    lngb = Buf(P.sbuf("lngb_s", [128, 2, 512], F32), "lngb")
    wsT_f = Buf(P.sbuf("wsT_f", [128, 4, 128], F32), "wsT_f")
    wsT_b = Buf(P.sbuf("wsT_b", [128, 4, 128], BF16), "wsT_b")
    tri = Buf(P.sbuf("tri_s", [128, 128], F32), "tri")
    sgub = Buf(P.sbuf("sgub_s", [128, 4], F32), "sgub")
    lbT = Buf(P.sbuf("lbT_s", [128, 4, 4], F32), "lbT")
    lbB = Buf(P.sbuf("lbB_s", [128, 4, 512], F32), "lbB")
    lmask = Buf(P.sbuf("lmask_s", [128, 4], F32), "lmask")
    oml_fm = Buf(P.sbuf("oml_fm", [128, 4], F32), "oml_fm")
    oml_tm = Buf(P.sbuf("oml_tm", [128, 512], F32), "oml_tm")
    sm = [Buf(P.sbuf(f"sm{i}", [128, 4], F32), f"sm{i}") for i in range(2)]
    qng = Buf(P.sbuf("qng_s", [128, 4], F32), "qng")
    kvng = Buf(P.sbuf("kvng_s", [128, 4], F32), "kvng")
    ropec = Buf(P.sbuf("ropec_s", [64, 2], F32), "ropec")
    ones = Buf(P.sbuf("ones_s", [128, 128], F32), "ones")
    stat = Rot(P, "stat", 4, [128, 2], F32)
    mmps = Rot(P, "mmps", 4, [128, 512], F32, psum=True)
    upps = Rot(P, "upps", 3, [128, 512], F32, psum=True)
    ssqps = Buf(P.psum("ssqps", [128, 512], F32), "ssqps")

    for b, d_ in ((lngb, lngb_d), (wsT_f, wsT_d), (tri, tri_d), (sgub, sgub_d), (lbT, lbT_d), (lbB, lbB_d),
                  (lmask, lmask_d), (qng, qng_d), (kvng, kvng_d), (posi, pos_d), (ropec, ropec_d), (ones, ones_d)):
        h.dma("sp", b.ap, d_.ap(), [], [b], chan="const")
    for g in range(4):
        h.tt("dve", wsT_b.ap[:, g, :], wsT_f.ap[:, g, :], tri.ap, ALU.mult, [wsT_f, tri], [wsT_b])
    lb_compute(h, lambda l: lbT.ap[:, l, :], lambda l: lmask.ap[:, l:l + 1], sm[0], sm[1], oml_fm, lbT, 4)
    ta = tmp.next(); tb_ = tmp.next()
    lb_compute(h, lambda l: lbB.ap[:, l, :], lambda l: lmask.ap[:, l:l + 1], ta, tb_, oml_tm, lbB, 512)

    w_in_v = w_in_d.ap().rearrange("(c p) n -> p c n", p=128)
    xT_v = xT_d.ap().rearrange("(c p) t -> p c t", p=128)
    cast_i = [0]

    def cast_eng():
        cast_i[0] += 1
        return ("dve", "pool")[cast_i[0] % 2]

    def load_group(col0, ncols, swap64=False):
        wb = w_bf.next()
        for q in range(4):
            st = w_st.next()
            h.dma("sp", st.ap[:, :, 0:ncols], w_in_v[:, q * 4:(q + 1) * 4, col0:col0 + ncols], [], [st], chan=st.key)
            h.cp(cast_eng(), wb.ap[:, q * 4:(q + 1) * 4, 0:ncols], st.ap[:, :, 0:ncols], [st], [wb])
            if swap64:
                h.cp(cast_eng(), wb.ap[:, q * 4:(q + 1) * 4, 64:96], st.ap[:, :, 32:64], [st], [wb])
                h.cp(cast_eng(), wb.ap[:, q * 4:(q + 1) * 4, 96:128], st.ap[:, :, 0:32], [st], [wb])
        return wb

    def rope_tables(tok0):
        a, b = rt
        h.cp("dve", a.ap, posi.ap[:, tok0:tok0 + 512], [posi], [a])
        h.ts("dve", a.ap, a.ap, ropec.ap[:, 0:1], None, ALU.mult, None, [a, ropec], [a])
        for off, dst in ((0.0, sinS), (0.25, cos2)):
            h.ts("dve", b.ap, a.ap, off, None, ALU.add, None, [a], [b])
            h.cp("dve", rti.ap, b.ap, [b], [rti])
            h.cp("dve", dst.ap, rti.ap, [rti], [dst])
            h.tt("dve", b.ap, b.ap, dst.ap, ALU.subtract, [b, dst], [b])
            h.ts("dve", dst.ap, b.ap, 0.5, None, ALU.is_gt, None, [b], [dst])
            h.tt("dve", b.ap, b.ap, dst.ap, ALU.subtract, [b, dst], [b])
            h.ts("dve", dst.ap, b.ap, -0.5, None, ALU.is_lt, None, [b], [dst])
            h.tt("dve", b.ap, b.ap, dst.ap, ALU.add, [b, dst], [b])
            h.act(dst.ap, b.ap, AF.Sin, [b], [dst], scale=2.0 * math.pi)
        h.ts("dve", sinS.ap, sinS.ap, ropec.ap[:, 1:2], None, ALU.mult, None, [sinS, ropec], [sinS])

    def rope_apply(pa, pb, dst_dram_ap, wkey):
        t1 = tmp.next(); t2 = tmp.next(); ob = ostb.next()
        h.tt("dve", t1.ap[0:64, :], pa.ap[0:64, :], cos2.ap, ALU.mult, [pa, cos2], [t1])
        h.tt("dve", t2.ap[0:64, :], pb.ap[0:64, :], sinS.ap, ALU.mult, [pb, sinS], [t2])
        h.tt("pool", ob.ap[0:64, :], t1.ap[0:64, :], t2.ap[0:64, :], ALU.add, [t1, t2], [ob])
        h.dma("pool", dst_dram_ap, ob.ap[0:64, :], [ob], [wkey], chan=ob.key)

    for half in range(2):
        t0 = half * HT
        for tb in range(2):
            for q in range(4):
                st = w_st.next()
                h.dma("sp", st.ap, xT_v[:, q * 4:(q + 1) * 4, t0 + tb * 512:t0 + (tb + 1) * 512], [], [st], chan=st.key)
                h.cp(cast_eng(), xT_bf[tb].ap[:, q * 4:(q + 1) * 4, :], st.ap, [st], [xT_bf[tb]])

        def fm_mm(ps, wb, j, tb, ncol=128, coff=None):
            co = j * 128 if coff is None else coff
            for c in range(16):
                h.mm(ps.ap[0:ncol, :], wb.ap[:, c, co:co + ncol], xT_bf[tb].ap[:, c, :], c == 0, c == 15, [wb, xT_bf[tb]], [ps])

        def tm_mm(ps, wb, tt):
            tb, r = divmod(tt, 4)
            for c in range(16):
                h.mm(ps.ap, xT_bf[tb].ap[:, c, r * 128:(r + 1) * 128], wb.ap[:, c, :], c == 0, c == 15, [wb, xT_bf[tb]], [ps])

        wb_u = load_group(0, 512)
        wb_v = load_group(512, 512)
        for tt in range(8):
            tok = t0 + tt * 128
            pu = mmps.next(); tm_mm(pu, wb_u, tt)
            pv = mmps.next(); tm_mm(pv, wb_v, tt)
            gu = gur.next()
            gelu_tanh(h, pu.ap, gu.ap, tmp, slice(None), [pu], [gu])
            gv = gvr.next()
            gelu_tanh(h, pv.ap, gv.ap, tmp, slice(None), [pv], [gv])
            st_ = stat.next()
            h.memset("pool", st_.ap, 0.0, [st_])
            h.red("dve", st_.ap[:, 0:1], gv.ap, ALU.add, [gv, st_], [st_])
            h.ts("dve", st_.ap[:, 0:1], st_.ap[:, 0:1], 1.0 / 512, None, ALU.mult, None, [st_], [st_])
            h.ts("dve", gv.ap, gv.ap, st_.ap[:, 0:1], None, ALU.subtract, None, [gv, st_], [gv])
            junk = tmp.next()
            h.act(junk.ap, gv.ap, AF.Square, [gv, st_], [junk, st_], accum=st_.ap[:, 1:2])
            h.ts("dve", st_.ap[:, 1:2], st_.ap[:, 1:2], 1.0 / 512, EPS, ALU.mult, ALU.add, [st_], [st_])
            h.act(st_.ap[:, 1:2], st_.ap[:, 1:2], AF.Sqrt, [st_], [st_])
            h.recip(st_.ap[:, 1:2], st_.ap[:, 1:2], [st_], [st_])
            h.stt("dve", junk.ap, gv.ap, st_.ap[:, 1:2], lngb.ap[:, 0, :], ALU.mult, ALU.mult, [gv, st_, lngb], [junk])
            vn = ostb.next()
            h.tt("pool", vn.ap, junk.ap, lngb.ap[:, 1, :], ALU.add, [junk, lngb], [vn])
            pm = upps.next()
            for g in range(4):
                h.mm(pm.ap[:, g * 128:(g + 1) * 128], wsT_b.ap[:, g, :], vn.ap[:, g * 128:(g + 1) * 128], True, True, [wsT_b, vn], [pm])
            mx = tmp.next()
            for g in range(4):
                h.ts("dve", mx.ap[:, g * 128:(g + 1) * 128], pm.ap[:, g * 128:(g + 1) * 128], sgub.ap[:, g:g + 1], None, ALU.add, None, [pm, sgub], [mx])
            yb = ostb.next()
            h.tt("pool", yb.ap, mx.ap, gu.ap, ALU.mult, [mx, gu], [yb])
            h.dma("pool", ya_d.ap()[tok:tok + 128, :], yb.ap, [yb], ["ya_d"], chan=yb.key)

        wb = load_group(1024, 512)
        for j in range(4):
            for tb in range(2):
                ps = mmps.next(); fm_mm(ps, wb, j, tb)
                o = ostf.next()
                h.act(o.ap, ps.ap, AF.Silu, [ps], [o])
                h.dma("pool", hqT_d.ap()[j * 128:(j + 1) * 128, t0 + tb * 512:t0 + (tb + 1) * 512], o.ap, [o], ["hqT_d"], chan=o.key)
        wb = load_group(1536, 512)
        for j in range(4):
            for tb in range(2):
                ps = mmps.next(); fm_mm(ps, wb, j, tb)
                o = ostf.next()
                h.act(o.ap, ps.ap, AF.Sigmoid, [ps], [o], scale=-1.0)
                h.ts("dve", o.ap, o.ap, oml_fm.ap[:, j:j + 1], None, ALU.mult, None, [o, oml_fm], [o])
                h.dma("pool", hkT_d.ap()[j * 128:(j + 1) * 128, t0 + tb * 512:t0 + (tb + 1) * 512], o.ap, [o], ["hkT_d"], chan=o.key)
        for tt in range(8):
            tok = t0 + tt * 128
            ps = mmps.next(); tm_mm(ps, wb, tt)
            sg = tmp.next(); o1 = ostf.next(); o2 = ostf.next()
            h.act(sg.ap, ps.ap, AF.Sigmoid, [ps], [sg], scale=-1.0)
            h.tt("dve", o1.ap, sg.ap, oml_tm.ap, ALU.mult, [sg, oml_tm], [o1])
            h.ts("dve", sg.ap, o1.ap, -1.0, 1.0, ALU.mult, ALU.add, [o1], [sg])
            h.act(o2.ap, sg.ap, AF.Ln, [sg], [o2])
            h.dma("pool", hkk_d.ap()[tok:tok + 128, :], o1.ap, [o1], ["hkk_d"], chan=o1.key)
            h.dma("pool", hlogf_d.ap()[tok:tok + 128, :], o2.ap, [o2], ["hlogf_d"], chan=o2.key)
        wb = load_group(2048, 512)
        for tt in range(8):
            tok = t0 + tt * 128
            ps = mmps.next(); tm_mm(ps, wb, tt)
            o = ostb.next()
            h.cp("act", o.ap, ps.ap, [ps], [o])
            h.dma("pool", hv_d.ap()[tok:tok + 128, :], o.ap, [o], ["hv_d"], chan=o.key)
        wb = load_group(2560, 512)
        for tt in range(8):
            tok = t0 + tt * 128
            ps = mmps.next(); tm_mm(ps, wb, tt)
            o = ostf.next()
            h.act(o.ap, ps.ap, AF.Silu, [ps], [o])
            h.dma("pool", hgate_d.ap()[tok:tok + 128, :], o.ap, [o], ["hgate_d"], chan=o.key)

        def rms_block(wb, tb, gcol):
            pss = [mmps.next() for _ in range(4)]
            for j in range(4):
                fm_mm(pss[j], wb, j, tb)
            for j in range(4):
                h.cp("act", c_sb.ap[:, j, :], pss[j].ap, [pss[j]], [c_sb])
                h.act(sq.ap[:, j, :], pss[j].ap, AF.Square, [pss[j]], [sq])
            for j in range(4):
                h.mm(ssqps.ap, ones.ap, sq.ap[:, j, :], j == 0, j == 3, [ones, sq], [ssqps])
            h.ts("dve", rstd.ap, ssqps.ap, 1.0 / 512, EPS, ALU.mult, ALU.add, [ssqps], [rstd])
            h.act(rstd.ap, rstd.ap, AF.Sqrt, [rstd], [rstd])
            h.recip(rstd.ap, rstd.ap, [rstd], [rstd])
            c = cn.next()
            for j in range(4):
                h.stt("dve", c.ap[:, j, :], c_sb.ap[:, j, :], gcol.ap[:, j:j + 1], rstd.ap, ALU.mult, ALU.mult, [c_sb, gcol, rstd], [c])
            return c

        wuq_v = w_uq_d.ap().rearrange("(c p) n -> p c n", p=128)
        for j in range(4):
            for part in range(3):
                st = w_st.next()
                h.dma("sp", st.ap[:, 0, :], wuq_v[:, j, part * 512:(part + 1) * 512], [], [st], chan=st.key)
                h.cp(cast_eng(), upw.ap[:, j, part * 512:(part + 1) * 512], st.ap[:, 0, :], [st], [upw])
        for j in range(4):
            src = upw.ap[:, j, 0:1536].rearrange("p (h d) -> p h d", d=192)
            dst = upw.ap[:, j, 1536:2048].rearrange("p (h d) -> p h d", d=64)
            h.cp("dve", dst[:, :, 0:32], src[:, :, 160:192], [upw], [upw])
            h.cp("dve", dst[:, :, 32:64], src[:, :, 128:160], [upw], [upw])
        wb = load_group(3072, 512)
        for tb in range(2):
            tk = t0 + tb * 512
            c = rms_block(wb, tb, qng)
            rope_tables(tk)
            for hh in range(8):
                pq = upps.next()
                for j in range(4):
                    h.mm(pq.ap, upw.ap[:, j, hh * 192:hh * 192 + 128], c.ap[:, j, :], j == 0, j == 3, [upw, c], [pq])
                o = ostb.next()
                h.cp("act", o.ap, pq.ap, [pq], [o])
                h.dma("pool", QT_d.ap()[hh * 192:hh * 192 + 128, tk:tk + 512], o.ap, [o], ["QT_d"], chan=o.key)
                pa = upps.next(); pb = upps.next()
                for j in range(4):
                    h.mm(pa.ap[0:64, :], upw.ap[:, j, hh * 192 + 128:hh * 192 + 192], c.ap[:, j, :], j == 0, j == 3, [upw, c], [pa])
                for j in range(4):
                    h.mm(pb.ap[0:64, :], upw.ap[:, j, 1536 + hh * 64:1536 + (hh + 1) * 64], c.ap[:, j, :], j == 0, j == 3, [upw, c], [pb])
                rope_apply(pa, pb, QT_d.ap()[hh * 192 + 128:hh * 192 + 192, tk:tk + 512], "QT_d")
        wukv_v = w_ukv_d.ap().rearrange("(c p) n -> p c n", p=128)
        for j in range(4):
            for part in range(4):
                st = w_st.next()
                h.dma("sp", st.ap[:, 0, :], wukv_v[:, j, part * 512:(part + 1) * 512], [], [st], chan=st.key)
                src = st.ap[:, 0, :].rearrange("p (h t d) -> p h t d", t=2, d=128)
                dk = upw.ap[:, j, part * 256:(part + 1) * 256].rearrange("p (h d) -> p h d", d=128)
                dv = upw.ap[:, j, 1024 + part * 256:1024 + (part + 1) * 256].rearrange("p (h d) -> p h d", d=128)
                h.cp(cast_eng(), dk, src[:, :, 0, :], [st], [upw])
                h.cp(cast_eng(), dv, src[:, :, 1, :], [st], [upw])
        wb = load_group(3584, 512)
        for tb in range(2):
            tk = t0 + tb * 512
            c = rms_block(wb, tb, kvng)
            for hh in range(8):
                pq = upps.next()
                for j in range(4):
                    h.mm(pq.ap, upw.ap[:, j, hh * 128:(hh + 1) * 128], c.ap[:, j, :], j == 0, j == 3, [upw, c], [pq])
                o = ostb.next()
                h.cp("act", o.ap, pq.ap, [pq], [o])
                h.dma("pool", KnT_d.ap()[hh * 128:(hh + 1) * 128, tk:tk + 512], o.ap, [o], ["KnT_d"], chan=o.key)
            for r in range(4):
                for grp in range(2):
                    pq = upps.next()
                    for j in range(4):
                        h.mm(pq.ap, c.ap[:, j, r * 128:(r + 1) * 128], upw.ap[:, j, 1024 + grp * 512:1024 + (grp + 1) * 512], j == 0, j == 3, [upw, c], [pq])
                    o = ostb.next()
                    h.cp("act", o.ap, pq.ap, [pq], [o])
                    h.dma("pool", Vv_d.ap()[tk + r * 128:tk + (r + 1) * 128, grp * 512:(grp + 1) * 512], o.ap, [o], ["Vv_d"], chan=o.key)
        wb = load_group(4096, 64, swap64=True)
        for tb in range(2):
            tk = t0 + tb * 512
            rope_tables(tk)
            pa = mmps.next(); pb = mmps.next()
            fm_mm(pa, wb, 0, tb, ncol=64, coff=0)
            fm_mm(pb, wb, 0, tb, ncol=64, coff=64)
            rope_apply(pa, pb, KrT_d.ap()[:, tk:tk + 512], "KrT_d")

    P.wait_all("sp", outs)
    P.wait_all("pool", outs)
    return P


L = 4
FT = 4096
NHALF = FT // 1024


def declare_dram(P):
    d = {}
    I = "ExternalInput"
    d["x"] = P.dram("x", [FT, D], F32, I)
    d["posr"] = P.dram("posr", [64, FT], I32, I)
    d["w_in"] = P.dram("w_in", [L, D, 4160], F32, I)
    d["w_uq"] = P.dram("w_uq", [L, 512, 1536], F32, I)
    d["w_ukv"] = P.dram("w_ukv", [L, 512, 2048], F32, I)
    d["lngb"] = P.dram("lngb", [L, 128, 2, 512], F32, I)
    d["wsT"] = P.dram("wsT", [L, 128, 4, 128], F32, I)
    d["tri"] = P.dram("tri", [128, 128], F32, I)
    d["sgub"] = P.dram("sgub", [L, 128, 4], F32, I)
    d["lbT"] = P.dram("lbT", [128, 4, 4], F32, I)
    d["lbB"] = P.dram("lbB", [128, 4, 512], F32, I)
    d["lmask"] = P.dram("lmask", [L, 128, 4], F32, I)
    d["qng"] = P.dram("qng", [L, 128, 4], F32, I)
    d["kvng"] = P.dram("kvng", [L, 128, 4], F32, I)
    d["ropec"] = P.dram("ropec", [64, 2], F32, I)
    d["ones"] = P.dram("ones", [128, 128], F32, I)
    d["b_ng"] = P.dram("b_ng", [L, 64, 4, 128], F32, I)
    d["b_ucat"] = P.dram("b_ucat", [64, 128], F32, I)
    d["b_lmat"] = P.dram("b_lmat", [64, 64], F32, I)
    d["b_cmask"] = P.dram("b_cmask", [128, 4, 512], BF16, I)
    d["b_onesb"] = P.dram("b_onesb", [128, 128], BF16, I)
    d["c_wout"] = P.dram("c_wout", [L, D, D], F32, I)
    d["c_ln1"] = P.dram("c_ln1", [L, 128, 2, D], F32, I)
    d["c_ln2"] = P.dram("c_ln2", [L, 128, 2, D], F32, I)
    d["c_wr"] = P.dram("c_wr", [L, D, 36], F32, I)
    d["c_br"] = P.dram("c_br", [L, 128, 36], F32, I)
    d["c_wg"] = P.dram("c_wg", [L, 32, D, 512], F32, I)
    d["c_wu"] = P.dram("c_wu", [L, 32, D, 512], F32, I)
    d["c_wd"] = P.dram("c_wd", [L, 32, 512, D], F32, I)
    d["c_ident"] = P.dram("c_ident", [128, 128], F32, I)
    d["c_identb"] = P.dram("c_identb", [128, 128], BF16, I)
    d["c_ustr"] = P.dram("c_ustr", [128, 128], BF16, I)
    d["c_blkst"] = P.dram("c_blkst", [128, 53], F32, I)
    d["c_pq"] = P.dram("c_pq", [128, 4], F32, I)
    d["xo"] = P.dram("xo", [FT, D], F32, "ExternalOutput")
    N = "Internal"
    d["xT_s"] = P.dram("xT_s", [FT // 512, 128, 16, 512], BF16, N)
    d["xres_s"] = P.dram("xres_s", [FT, D], F32, N)
    d["yaT_s"] = P.dram("yaT_s", [512, FT], BF16, N)
    for nm, shp, dt in (("hqT", [512, FT], F32), ("hkT", [512, FT], F32),
                        ("hlogf", [FT, 512], F32), ("hkk", [FT, 512], F32),
                        ("hv", [FT, 512], BF16), ("hgate", [FT, 512], F32),
                        ("QT", [1536, FT], BF16), ("KnT", [1024, FT], BF16),
                        ("KrT", [64, FT], BF16), ("Vv", [FT, 1024], BF16),
                        ("ybT", [512, FT], BF16), ("ycT", [1024, FT], BF16)):
        d["s_" + nm] = P.dram("s_" + nm, shp, dt, N)
    d["c_x1"] = P.dram("c_x1", [FT, D], F32, N)
    d["c_x1b"] = P.dram("c_x1b", [FT, D], BF16, N)
    d["c_xs"] = P.dram("c_xs", [159 * 128, D], BF16, N)
    d["c_ys"] = P.dram("c_ys", [159 * 128, D], F32, N)
    return d


def exchange(P, d, names, tag):
    for nm in names:
        src = d["s_" + nm]; dst = d["g_" + nm]
        def cc(e, src=src, dst=dst):
            return e.collective_compute("AllToAll", ALU.bypass, replica_groups=[[0, 1]], ins=[src.ap()], outs=[dst.ap()])
        P.dma("pool", None, None, reads=["s_" + nm], writes=["g_" + nm], chan="cc_" + nm, indirect=cc)


def emit_xT(P, h, d, o, tt, ident, xtb_rot, banks4):
    tb, r = divmod(tt, 4)
    xtb = xtb_rot.next()
    for c4 in range(4):
        pt = banks4[c4 % len(banks4)]
        for i in range(4):
            c = c4 * 4 + i
            h.tr(pt.ap[:, i * 128:(i + 1) * 128], o.ap[:, c * 128:(c + 1) * 128], ident.ap, [o, ident], [pt])
        h.cp(("act", "dve")[c4 % 2], xtb.ap[:, c4 * 4:(c4 + 1) * 4, :].rearrange("p c t -> p (c t)"), pt.ap, [pt], [xtb])
    for c4 in range(4):
        h.dma("pool", d["xT_s"].ap()[tb, :, c4 * 4:(c4 + 1) * 4, r * 128:(r + 1) * 128], xtb.ap[:, c4 * 4:(c4 + 1) * 4, :], [xtb], [("xT_s", tb)], chan=xtb.key)


def emit_X0(P, h, d):
    ident = Buf(P.sbuf("x0_ident", [128, 128], F32), "x0_ident")
    h.dma("sp", ident.ap, d["c_ident"].ap(), [], [ident], chan="const")
    xr = Rot(P, "x0_x", 2, [128, D], F32)
    xtb = Rot(P, "x0_xtb", 2, [128, 16, 128], BF16)
    banks = [Buf(P.psum(f"x0b{i}", [128, 512], F32), f"x0b{i}") for i in range(4)]
    for bk in banks:
        P.exclusive.add(bk.key)
    for tt in range(FT // 128):
        xt = xr.next()
        h.dma("sp", xt.ap, d["x"].ap()[tt * 128:(tt + 1) * 128, :], [], [xt], chan=xt.key)
        emit_xT(P, h, d, xt, tt, ident, xtb, banks)


def emit_A(P, h, d, l):
    HT = NTOK // 2
    xT_bf = [Buf(P.sbuf(f"xTbf{i}", [128, 16, 512], BF16), f"xTbf{i}") for i in range(2)]
    w_st = Rot(P, "wst", 2, [128, 4, 512], F32)
    w_bf = Rot(P, "wbf", 2, [128, 16, 512], BF16)
    upw = Buf(P.sbuf("upw", [128, 4, 2048], BF16), "upw")
    tmp = Rot(P, "tmp", 6, [128, 512], F32)
    gur = Rot(P, "gur", 2, [128, 512], F32)
    gvr = Rot(P, "gvr", 2, [128, 512], F32)
    ostf = Rot(P, "ostf", 4, [128, 512], F32)
    ostb = Rot(P, "ostb", 4, [128, 512], BF16)
    c_sb = Buf(P.sbuf("c_sb", [128, 4, 512], F32), "c_sb")
    sq = Buf(P.sbuf("sq", [128, 4, 512], F32), "sq")
    cn = Rot(P, "cn", 2, [128, 4, 512], BF16)
    rstd = Buf(P.sbuf("rstd", [128, 512], F32), "rstd")
    cos2 = Buf(P.sbuf("cos2", [64, 512], F32), "cos2")
    sinS = Buf(P.sbuf("sinS", [64, 512], F32), "sinS")
    rt = [Buf(P.sbuf(f"rt{i}", [64, 512], F32), f"rt{i}") for i in range(2)]
    rti = Buf(P.sbuf("rti", [64, 512], I32), "rti")
    posi = Buf(P.sbuf("posi", [64, FT], I32), "posi")
    lngb = Buf(P.sbuf("lngb_s", [128, 2, 512], F32), "lngb")
    wsT_f = Buf(P.sbuf("wsT_f", [128, 4, 128], F32), "wsT_f")
    wsT_b = Buf(P.sbuf("wsT_b", [128, 4, 128], BF16), "wsT_b")
    tri = Buf(P.sbuf("tri_s", [128, 128], F32), "tri")
    sgub = Buf(P.sbuf("sgub_s", [128, 4], F32), "sgub")
    lbT = Buf(P.sbuf("lbT_s", [128, 4, 4], F32), "lbT")
    lbB = Buf(P.sbuf("lbB_s", [128, 4, 512], F32), "lbB")
    lmask = Buf(P.sbuf("lmask_s", [128, 4], F32), "lmask")
    oml_fm = Buf(P.sbuf("oml_fm", [128, 4], F32), "oml_fm")
    oml_tm = Buf(P.sbuf("oml_tm", [128, 512], F32), "oml_tm")
    sm = [Buf(P.sbuf(f"sm{i}", [128, 4], F32), f"sm{i}") for i in range(2)]
    qng = Buf(P.sbuf("qng_s", [128, 4], F32), "qng")
    kvng = Buf(P.sbuf("kvng_s", [128, 4], F32), "kvng")
    ropec = Buf(P.sbuf("ropec_s", [64, 2], F32), "ropec")
    ones = Buf(P.sbuf("ones_s", [128, 128], F32), "ones")
    identb = Buf(P.sbuf("a_identb", [128, 128], BF16), "a_identb")
    yaT = Rot(P, "yaT", 2, [128, 4, 128], BF16)
    stat = Rot(P, "stat", 4, [128, 2], F32)
    mmps = Rot(P, "mmps", 4, [128, 512], F32, psum=True)
    upps = Rot(P, "upps", 3, [128, 512], F32, psum=True)
    ssqps = Buf(P.psum("ssqps", [128, 512], F32), "ssqps")
    for r_ in (mmps, upps):
        for b_ in r_.bufs:
            P.exclusive.add(b_.key)
    P.exclusive.add(ssqps.key)

    for b, ap_ in ((lngb, d["lngb"].ap()[l]), (wsT_f, d["wsT"].ap()[l]), (tri, d["tri"].ap()), (sgub, d["sgub"].ap()[l]),
                   (lbT, d["lbT"].ap()), (lbB, d["lbB"].ap()), (lmask, d["lmask"].ap()[l]), (qng, d["qng"].ap()[l]),
                   (kvng, d["kvng"].ap()[l]), (posi, d["posr"].ap()), (ropec, d["ropec"].ap()), (ones, d["ones"].ap()),
                   (identb, d["c_identb"].ap())):
        h.dma("sp", b.ap, ap_, [], [b], chan="const")
    for g in range(4):
        h.tt("dve", wsT_b.ap[:, g, :], wsT_f.ap[:, g, :], tri.ap, ALU.mult, [wsT_f, tri], [wsT_b])
    lb_compute(h, lambda l_: lbT.ap[:, l_, :], lambda l_: lmask.ap[:, l_:l_ + 1], sm[0], sm[1], oml_fm, lbT, 4)
    ta = tmp.next(); tb_ = tmp.next()
    lb_compute(h, lambda l_: lbB.ap[:, l_, :], lambda l_: lmask.ap[:, l_:l_ + 1], ta, tb_, oml_tm, lbB, 512)

    w_in_v = d["w_in"].ap()[l].rearrange("(c p) n -> p c n", p=128)
    cast_i = [0]

    def cast_eng():
        cast_i[0] += 1
        return ("dve", "act")[cast_i[0] % 2]

    def load_group(col0, ncols, swap64=False):
        wb = w_bf.next()
        for q in range(4):
            st = w_st.next()
            h.dma("sp", st.ap[:, :, 0:ncols], w_in_v[:, q * 4:(q + 1) * 4, col0:col0 + ncols], [], [st], chan=st.key)
            h.cp(cast_eng(), wb.ap[:, q * 4:(q + 1) * 4, 0:ncols], st.ap[:, :, 0:ncols], [st], [wb])
            if swap64:
                h.cp(cast_eng(), wb.ap[:, q * 4:(q + 1) * 4, 64:96], st.ap[:, :, 32:64], [st], [wb])
                h.cp(cast_eng(), wb.ap[:, q * 4:(q + 1) * 4, 96:128], st.ap[:, :, 0:32], [st], [wb])
        return wb

    def rope_tables(tok0):
        a, b = rt
        h.cp("dve", a.ap, posi.ap[:, tok0:tok0 + 512], [posi], [a])
        h.ts("dve", a.ap, a.ap, ropec.ap[:, 0:1], None, ALU.mult, None, [a, ropec], [a])
        for off, dst in ((0.0, sinS), (0.25, cos2)):
            h.ts("dve", b.ap, a.ap, off, None, ALU.add, None, [a], [b])
            h.cp("dve", rti.ap, b.ap, [b], [rti])
            h.cp("dve", dst.ap, rti.ap, [rti], [dst])
            h.tt("dve", b.ap, b.ap, dst.ap, ALU.subtract, [b, dst], [b])
            h.ts("dve", dst.ap, b.ap, 0.5, None, ALU.is_gt, None, [b], [dst])
            h.tt("dve", b.ap, b.ap, dst.ap, ALU.subtract, [b, dst], [b])
            h.ts("dve", dst.ap, b.ap, -0.5, None, ALU.is_lt, None, [b], [dst])
            h.tt("dve", b.ap, b.ap, dst.ap, ALU.add, [b, dst], [b])
            h.act(dst.ap, b.ap, AF.Sin, [b], [dst], scale=2.0 * math.pi)
        h.ts("dve", sinS.ap, sinS.ap, ropec.ap[:, 1:2], None, ALU.mult, None, [sinS, ropec], [sinS])

    def rope_apply(pa, pb, dsts, wkey):
        t1 = tmp.next(); t2 = tmp.next(); ob = ostb.next()
        h.tt("dve", t1.ap[0:64, :], pa.ap[0:64, :], cos2.ap, ALU.mult, [pa, cos2], [t1])
        h.tt("dve", t2.ap[0:64, :], pb.ap[0:64, :], sinS.ap, ALU.mult, [pb, sinS], [t2])
        h.tt("pool", ob.ap[0:64, :], t1.ap[0:64, :], t2.ap[0:64, :], ALU.add, [t1, t2], [ob])
        for dst in dsts:
            h.dma("pool", dst, ob.ap[0:64, :], [ob], [wkey], chan=ob.key)

    for half in range(NHALF):
        t0 = half * HT
        for tb in range(2):
            h.dma("sp", xT_bf[tb].ap, d["xT_s"].ap()[half * 2 + tb], [("xT_s", half * 2 + tb)], [xT_bf[tb]], chan=xT_bf[tb].key)

        def fm_mm(ps, wb, j, tb, ncol=128, coff=None):
            co = j * 128 if coff is None else coff
            for c in range(16):
                h.mm(ps.ap[0:ncol, :], wb.ap[:, c, co:co + ncol], xT_bf[tb].ap[:, c, :], c == 0, c == 15, [wb, xT_bf[tb]], [ps])

        def tm_mm(ps, wb, tt):
            tb, r = divmod(tt, 4)
            for c in range(16):
                h.mm(ps.ap, xT_bf[tb].ap[:, c, r * 128:(r + 1) * 128], wb.ap[:, c, :], c == 0, c == 15, [wb, xT_bf[tb]], [ps])

        wb_u = load_group(0, 512)
        wb_v = load_group(512, 512)
        def sgu_mm(tt_):
            pu_ = mmps.next(); tm_mm(pu_, wb_u, tt_)
            pv_ = mmps.next(); tm_mm(pv_, wb_v, tt_)
            return pu_, pv_
        nxt_uv = sgu_mm(0)
        for tt in range(8):
            tok = t0 + tt * 128
            pu, pv = nxt_uv
            if tt + 1 < 8:
                nxt_uv = sgu_mm(tt + 1)
            gu = gur.next()
            gelu_tanh(h, pu.ap, gu.ap, tmp, slice(None), [pu], [gu])
            gv = gvr.next()
            gelu_tanh(h, pv.ap, gv.ap, tmp, slice(None), [pv], [gv])
            st_ = stat.next()
            h.memset("pool", st_.ap, 0.0, [st_])
            h.red("dve", st_.ap[:, 0:1], gv.ap, ALU.add, [gv, st_], [st_])
            h.ts("dve", st_.ap[:, 0:1], st_.ap[:, 0:1], 1.0 / 512, None, ALU.mult, None, [st_], [st_])
            h.ts("dve", gv.ap, gv.ap, st_.ap[:, 0:1], None, ALU.subtract, None, [gv, st_], [gv])
            junk = tmp.next()
            h.act(junk.ap, gv.ap, AF.Square, [gv, st_], [junk, st_], accum=st_.ap[:, 1:2])
            h.ts("dve", st_.ap[:, 1:2], st_.ap[:, 1:2], 1.0 / 512, EPS, ALU.mult, ALU.add, [st_], [st_])
            h.act(st_.ap[:, 1:2], st_.ap[:, 1:2], AF.Sqrt, [st_], [st_])
            h.recip(st_.ap[:, 1:2], st_.ap[:, 1:2], [st_], [st_])
            h.stt("dve", junk.ap, gv.ap, st_.ap[:, 1:2], lngb.ap[:, 0, :], ALU.mult, ALU.mult, [gv, st_, lngb], [junk])
            vn = ostb.next()
            h.tt("pool", vn.ap, junk.ap, lngb.ap[:, 1, :], ALU.add, [junk, lngb], [vn])
            pm = upps.next()
            for g in range(4):
                h.mm(pm.ap[:, g * 128:(g + 1) * 128], wsT_b.ap[:, g, :], vn.ap[:, g * 128:(g + 1) * 128], True, True, [wsT_b, vn], [pm])
            mx = tmp.next()
            for g in range(4):
                h.ts("dve", mx.ap[:, g * 128:(g + 1) * 128], pm.ap[:, g * 128:(g + 1) * 128], sgub.ap[:, g:g + 1], None, ALU.add, None, [pm, sgub], [mx])
            yb = ostb.next()
            h.tt("pool", yb.ap, mx.ap, gu.ap, ALU.mult, [mx, gu], [yb])
            pt = upps.next()
            ptb = pt.ap.bitcast(BF16)
            for g in range(4):
                h.tr(ptb[:, g * 128:(g + 1) * 128], yb.ap[:, g * 128:(g + 1) * 128], identb.ap, [yb, identb], [pt])
            yt = yaT.next()
            h.cp("act", yt.ap.rearrange("p g t -> p (g t)"), ptb[:, 0:512], [pt], [yt])
            h.dma("pool", d["yaT_s"].ap().rearrange("(g p) t -> p g t", p=128)[:, :, tok:tok + 128], yt.ap, [yt], ["yaT_s"], chan=yt.key)

        wb = load_group(1024, 512)
        for j in range(4):
            for tb in range(2):
                ps = mmps.next(); fm_mm(ps, wb, j, tb)
                o = ostf.next()
                h.act(o.ap, ps.ap, AF.Silu, [ps], [o])
                h.dma("pool", d["s_hqT"].ap()[j * 128:(j + 1) * 128, t0 + tb * 512:t0 + (tb + 1) * 512], o.ap, [o], ["s_hqT"], chan=o.key)
        wb = load_group(1536, 512)
        for j in range(4):
            for tb in range(2):
                ps = mmps.next(); fm_mm(ps, wb, j, tb)
                o = ostf.next()
                h.act(o.ap, ps.ap, AF.Sigmoid, [ps], [o], scale=-1.0)
                h.ts("dve", o.ap, o.ap, oml_fm.ap[:, j:j + 1], None, ALU.mult, None, [o, oml_fm], [o])
                h.dma("pool", d["s_hkT"].ap()[j * 128:(j + 1) * 128, t0 + tb * 512:t0 + (tb + 1) * 512], o.ap, [o], ["s_hkT"], chan=o.key)

        def tm_dst(nm, tok, ncol):
            return d["s_" + nm].ap()[tok:tok + 128, :]

        for tt in range(8):
            tok = t0 + tt * 128
            ps = mmps.next(); tm_mm(ps, wb, tt)
            sg = tmp.next(); o1 = ostf.next(); o2 = ostf.next()
            h.act(sg.ap, ps.ap, AF.Sigmoid, [ps], [sg], scale=-1.0)
            h.tt("dve", o1.ap, sg.ap, oml_tm.ap, ALU.mult, [sg, oml_tm], [o1])
            h.ts("dve", sg.ap, o1.ap, -1.0, 1.0, ALU.mult, ALU.add, [o1], [sg])
            h.act(o2.ap, sg.ap, AF.Ln, [sg], [o2])
            h.dma("pool", tm_dst("hkk", tok, 256), o1.ap, [o1], ["s_hkk"], chan=o1.key)
            h.dma("pool", tm_dst("hlogf", tok, 256), o2.ap, [o2], ["s_hlogf"], chan=o2.key)
        wb = load_group(2048, 512)
        for tt in range(8):
            tok = t0 + tt * 128
            ps = mmps.next(); tm_mm(ps, wb, tt)
            o = ostb.next()
            h.cp("act", o.ap, ps.ap, [ps], [o])
            h.dma("pool", tm_dst("hv", tok, 256), o.ap, [o], ["s_hv"], chan=o.key)
        wb = load_group(2560, 512)
        for tt in range(8):
            tok = t0 + tt * 128
            ps = mmps.next(); tm_mm(ps, wb, tt)
            o = ostf.next()
            h.act(o.ap, ps.ap, AF.Silu, [ps], [o])
            h.dma("pool", tm_dst("hgate", tok, 256), o.ap, [o], ["s_hgate"], chan=o.key)

        def rms_block(wb, tb, gcol):
            pss = [mmps.next() for _ in range(4)]
            for j in range(4):
                fm_mm(pss[j], wb, j, tb)
            for j in range(4):
                h.cp("act", c_sb.ap[:, j, :], pss[j].ap, [pss[j]], [c_sb])
                h.act(sq.ap[:, j, :], pss[j].ap, AF.Square, [pss[j]], [sq])
            for j in range(4):
                h.mm(ssqps.ap, ones.ap, sq.ap[:, j, :], j == 0, j == 3, [ones, sq], [ssqps])
            h.ts("dve", rstd.ap, ssqps.ap, 1.0 / 512, EPS, ALU.mult, ALU.add, [ssqps], [rstd])
            h.act(rstd.ap, rstd.ap, AF.Sqrt, [rstd], [rstd])
            h.recip(rstd.ap, rstd.ap, [rstd], [rstd])
            c = cn.next()
            for j in range(4):
                h.stt("dve", c.ap[:, j, :], c_sb.ap[:, j, :], gcol.ap[:, j:j + 1], rstd.ap, ALU.mult, ALU.mult, [c_sb, gcol, rstd], [c])
            return c

        wuq_v = d["w_uq"].ap()[l].rearrange("(c p) n -> p c n", p=128)
        for j in range(4):
            for part in range(3):
                st = w_st.next()
                h.dma("sp", st.ap[:, 0, :], wuq_v[:, j, part * 512:(part + 1) * 512], [], [st], chan=st.key)
                h.cp(cast_eng(), upw.ap[:, j, part * 512:(part + 1) * 512], st.ap[:, 0, :], [st], [upw])
        for j in range(4):
            src = upw.ap[:, j, 0:1536].rearrange("p (h d) -> p h d", d=192)
            dst = upw.ap[:, j, 1536:2048].rearrange("p (h d) -> p h d", d=64)
            h.cp("dve", dst[:, :, 0:32], src[:, :, 160:192], [upw], [upw])
            h.cp("dve", dst[:, :, 32:64], src[:, :, 128:160], [upw], [upw])
        wb = load_group(3072, 512)
        for tb in range(2):
            tk = t0 + tb * 512
            c = rms_block(wb, tb, qng)
            rope_tables(tk)
            for hh in range(8):
                pq = upps.next()
                for j in range(4):
                    h.mm(pq.ap, upw.ap[:, j, hh * 192:hh * 192 + 128], c.ap[:, j, :], j == 0, j == 3, [upw, c], [pq])
                o = ostb.next()
                h.cp("act", o.ap, pq.ap, [pq], [o])
                h.dma("pool", d["s_QT"].ap()[hh * 192:hh * 192 + 128, tk:tk + 512], o.ap, [o], ["s_QT"], chan=o.key)
                pa = upps.next(); pb = upps.next()
                for j in range(4):
                    h.mm(pa.ap[0:64, :], upw.ap[:, j, hh * 192 + 128:hh * 192 + 192], c.ap[:, j, :], j == 0, j == 3, [upw, c], [pa])
                for j in range(4):
                    h.mm(pb.ap[0:64, :], upw.ap[:, j, 1536 + hh * 64:1536 + (hh + 1) * 64], c.ap[:, j, :], j == 0, j == 3, [upw, c], [pb])
                rope_apply(pa, pb, [d["s_QT"].ap()[hh * 192 + 128:hh * 192 + 192, tk:tk + 512]], "s_QT")
        wukv_v = d["w_ukv"].ap()[l].rearrange("(c p) n -> p c n", p=128)
        for j in range(4):
            for part in range(4):
                st = w_st.next()
                h.dma("sp", st.ap[:, 0, :], wukv_v[:, j, part * 512:(part + 1) * 512], [], [st], chan=st.key)
                src = st.ap[:, 0, :].rearrange("p (h t d) -> p h t d", t=2, d=128)
                dk = upw.ap[:, j, part * 256:(part + 1) * 256].rearrange("p (h d) -> p h d", d=128)
                dv = upw.ap[:, j, 1024 + part * 256:1024 + (part + 1) * 256].rearrange("p (h d) -> p h d", d=128)
                h.cp(cast_eng(), dk, src[:, :, 0, :], [st], [upw])
                h.cp(cast_eng(), dv, src[:, :, 1, :], [st], [upw])
        wb = load_group(3584, 512)
        for tb in range(2):
            tk = t0 + tb * 512
            c = rms_block(wb, tb, kvng)
            for hh in range(8):
                pq = upps.next()
                for j in range(4):
                    h.mm(pq.ap, upw.ap[:, j, hh * 128:(hh + 1) * 128], c.ap[:, j, :], j == 0, j == 3, [upw, c], [pq])
                o = ostb.next()
                h.cp("act", o.ap, pq.ap, [pq], [o])
                h.dma("pool", d["s_KnT"].ap()[hh * 128:(hh + 1) * 128, tk:tk + 512], o.ap, [o], ["s_KnT"], chan=o.key)
            for r in range(4):
                for grp in range(2):
                    pq = upps.next()
                    for j in range(4):
                        h.mm(pq.ap, c.ap[:, j, r * 128:(r + 1) * 128], upw.ap[:, j, 1024 + grp * 512:1024 + (grp + 1) * 512], j == 0, j == 3, [upw, c], [pq])
                    o = ostb.next()
                    h.cp("act", o.ap, pq.ap, [pq], [o])
                    h.dma("pool", d["s_Vv"].ap()[tk + r * 128:tk + (r + 1) * 128, grp * 512:(grp + 1) * 512], o.ap, [o], ["s_Vv"], chan=o.key)
        wb = load_group(4096, 64, swap64=True)
        for tb in range(2):
            tk = t0 + tb * 512
            rope_tables(tk)
            pa = mmps.next(); pb = mmps.next()
            fm_mm(pa, wb, 0, tb, ncol=64, coff=0)
            fm_mm(pb, wb, 0, tb, ncol=64, coff=64)
            rope_apply(pa, pb, [d["s_KrT"].ap()[0:64, tk:tk + 512]], "s_KrT")


S = 4096
CH = 64
NCH = S // CH
GRP = 8


def emit_B(P, h, d, l):
    ng = Buf(P.sbuf("ng", [64, 4, 128], F32), "ng")
    ucat = Buf(P.sbuf("ucat", [64, 128], F32), "ucat")
    lmat = Buf(P.sbuf("lmat", [64, 64], F32), "lmat")
    cmask = Buf(P.sbuf("cmask", [128, 4, 512], BF16), "cmask")
    onesb = Buf(P.sbuf("onesb", [128, 128], BF16), "onesb")
    identb = Buf(P.sbuf("b_identb", [128, 128], BF16), "b_identb")
    for b, ap_ in ((ng, d["b_ng"].ap()[l]), (ucat, d["b_ucat"].ap()), (lmat, d["b_lmat"].ap()), (cmask, d["b_cmask"].ap()),
                   (onesb, d["b_onesb"].ap()), (identb, d["c_identb"].ap())):
        h.dma("sp", b.ap, ap_, [], [b], chan="const")
    banks = [Buf(P.psum(f"bbank{i}", [128, 512], F32), f"bbank{i}") for i in range(8)]
    for bk in banks:
        P.exclusive.add(bk.key)

    gq = Rot(P, "gq", 2, [128, 512], F32)
    gk = Rot(P, "gk", 2, [128, 512], F32)
    glf = Rot(P, "glf", 2, [64, GRP, 128], F32)
    gkk = Rot(P, "gkk", 2, [64, GRP, 128], F32)
    ggt = Rot(P, "ggt", 2, [64, GRP, 128], F32)
    gv = Rot(P, "gv", 2, [64, GRP, 128], BF16)
    yst = Rot(P, "yst", 2, [128, 512], BF16)
    Sf = [Buf(P.sbuf(f"Sf{i}", [128, 128], F32), f"Sf{i}") for i in range(4)]
    Sb = [Rot(P, f"Sb{i}_", 2, [128, 128], BF16) for i in range(4)]
    eg = Rot(P, "eg", 2, [128, 128], F32)
    ek = Rot(P, "ek", 2, [128, 64], F32)
    er = Rot(P, "er", 2, [64, 128], F32)
    qt_ = Rot(P, "qt_", 2, [128, 64], BF16)
    kt_ = Rot(P, "kt_", 2, [128, 64], BF16)
    qh_ = Rot(P, "qh_", 2, [128, 64], BF16)
    kb_ = Rot(P, "kb_", 2, [64, 128], BF16)
    atb = Rot(P, "atb", 2, [64, 64], BF16)
    hst = Rot(P, "hst", 2, [64, 2], F32)
    htmp = Rot(P, "htmp", 2, [64, 128], F32)
    ych = Rot(P, "ych", 2, [64, 128], BF16)
    bkX, bkY = banks[6], banks[7]

    def tm_src(nm, t0, hd):
        return d["s_" + nm].ap()[t0:t0 + 512, hd * 128:(hd + 1) * 128].rearrange("(c s) k -> s c k", s=CH)

    def hgrn_head(hd):
        h.memset("pool", Sf[hd].ap, 0.0, [Sf[hd]])
        sb = Sb[hd].next()
        h.memset("pool", sb.ap, 0.0, [sb])
        for g in range(NCH // GRP):
            t0 = g * 512
            q_ = gq.next(); k_ = gk.next(); lf = glf.next(); kk = gkk.next(); gt = ggt.next(); v_ = gv.next()
            h.dma("sp", q_.ap, d["s_hqT"].ap()[hd * 128:(hd + 1) * 128, t0:t0 + 512], ["s_hqT"], [q_], chan=q_.key)
            h.dma("sp", k_.ap, d["s_hkT"].ap()[hd * 128:(hd + 1) * 128, t0:t0 + 512], ["s_hkT"], [k_], chan=k_.key)
            for buf, nm in ((lf, "hlogf"), (kk, "hkk"), (gt, "hgate"), (v_, "hv")):
                h.dma("sp", buf.ap, tm_src(nm, t0, hd), ["s_" + nm], [buf], chan=buf.key)
            yo = yst.next()
            for c in range(GRP):
                qc = q_.ap[:, c * CH:(c + 1) * CH]; kc = k_.ap[:, c * CH:(c + 1) * CH]
                h.mm(bkX.ap[:, 0:128], lf.ap[:, c, :], ucat.ap, True, True, [lf, ucat], [bkX])
                h.mm(bkX.ap[0:64, 128:256], lmat.ap, lf.ap[:, c, :], True, True, [lf, lmat], [bkX])
                e1 = eg.next(); e2 = ek.next(); e3 = er.next()
                h.act(e1.ap, bkX.ap[:, 0:128], AF.Exp, [bkX], [e1])
                h.act(e2.ap, bkX.ap[:, 64:128], AF.Exp, [bkX], [e2], scale=-1.0)
                h.act(e3.ap, bkX.ap[0:64, 128:256], AF.Exp, [bkX], [e3])
                qt = qt_.next(); kt = kt_.next(); qh = qh_.next(); kb = kb_.next()
                h.tt("dve", qt.ap, qc, e1.ap[:, 64:128], ALU.mult, [q_, e1], [qt])
                h.tt("pool", kt.ap, kc, e2.ap, ALU.mult, [k_, e2], [kt])
                h.tt("dve", qh.ap, qc, e1.ap[:, 0:64], ALU.mult, [q_, e1], [qh])
                h.tt("pool", kb.ap, kk.ap[:, c, :], e3.ap, ALU.mult, [kk, e3], [kb])
                yield
                h.mm(bkX.ap[0:64, 256:320], kt.ap, qt.ap, True, True, [kt, qt], [bkX])
                at = atb.next()
                h.tt("dve", at.ap, bkX.ap[0:64, 256:320], ucat.ap[:, 0:64], ALU.mult, [bkX, ucat], [at])
                yield
                h.mm(bkY.ap[0:64, 0:128], at.ap, v_.ap[:, c, :], True, False, [at, v_], [bkY])
                h.mm(bkY.ap[0:64, 0:128], qh.ap, sb.ap, False, True, [qh, sb], [bkY])
                h.mm(bkY.ap[:, 128:256], kb.ap, v_.ap[:, c, :], True, True, [kb, v_], [bkY])
                h.stt("dve", Sf[hd].ap, Sf[hd].ap, e1.ap[:, 63:64], bkY.ap[:, 128:256], ALU.mult, ALU.add, [Sf[hd], e1, bkY], [Sf[hd]])
                sb = Sb[hd].next()
                h.cp("pool", sb.ap, Sf[hd].ap, [Sf[hd]], [sb])
                yield
                st = hst.next(); tm = htmp.next(); tm2 = htmp.next()
                h.memset("pool", st.ap, 0.0, [st])
                h.act(tm.ap, bkY.ap[0:64, 0:128], AF.Square, [bkY, st], [tm, st], accum=st.ap[:, 0:1])
                h.ts("dve", st.ap[:, 0:1], st.ap[:, 0:1], 1.0 / 128, EPS, ALU.mult, ALU.add, [st], [st])
                h.act(st.ap[:, 0:1], st.ap[:, 0:1], AF.Sqrt, [st], [st])
                h.recip(st.ap[:, 0:1], st.ap[:, 0:1], [st], [st])
                h.stt("dve", tm2.ap, bkY.ap[0:64, 0:128], st.ap[:, 0:1], ng.ap[:, hd, :], ALU.mult, ALU.mult, [bkY, st, ng], [tm2])
                yc_ = ych.next()
                h.tt("pool", yc_.ap, tm2.ap, gt.ap[:, c, :], ALU.mult, [tm2, gt], [yc_])
                ptb = bkY.ap.bitcast(BF16)
                h.tr(ptb[:, 512:576], yc_.ap, identb.ap[0:64, 0:64], [yc_, identb], [bkY])
                h.cp("act", yo.ap[:, c * CH:(c + 1) * CH], ptb[:, 512:576], [bkY], [yo])
                yield
            h.dma("pool", d["s_ybT"].ap()[hd * 128:(hd + 1) * 128, t0:t0 + 512], yo.ap, [yo], ["s_ybT"], chan=yo.key)

    aq = Rot(P, "aq", 2, [128, S], BF16)
    aqr = Rot(P, "aqr", 2, [64, S], BF16)
    akn = Rot(P, "akn", 2, [128, S], BF16)
    akr = Buf(P.sbuf("akr", [64, S], BF16), "akr")
    av = Rot(P, "av", 2, [128, 32, 128], BF16)
    pt_ = Rot(P, "pt_", 3, [128, 512], BF16)
    rin = Rot(P, "rin", 2, [128, 512], F32)
    oo = Rot(P, "oo", 2, [128, 512], BF16)

    class BRot:
        def __init__(self, bufs):
            self.bufs = bufs; self.i = 0

        def next(self):
            b_ = self.bufs[self.i % len(self.bufs)]; self.i += 1; return b_
    stps = BRot(banks[0:2]); otps = BRot(banks[2:4]); rsps = BRot(banks[4:6])
    h.dma("sp", akr.ap, d["s_KrT"].ap(), ["s_KrT"], [akr], chan="akr")
    SCALE = 192 ** -0.5

    def attn_head(hd):
        q_ = aq.next(); qr = aqr.next(); kn = akn.next(); v_ = av.next()
        r0 = hd * 192
        h.dma("sp", q_.ap, d["s_QT"].ap()[r0:r0 + 128, :], ["s_QT"], [q_], chan=q_.key)
        h.dma("sp", qr.ap, d["s_QT"].ap()[r0 + 128:r0 + 192, :], ["s_QT"], [qr], chan=qr.key)
        h.dma("sp", kn.ap, d["s_KnT"].ap()[hd * 128:(hd + 1) * 128, :], ["s_KnT"], [kn], chan=kn.key)
        for part in range(4):
            rows = slice(part * 1024, (part + 1) * 1024)
            h.dma("sp", v_.ap[:, part * 8:(part + 1) * 8, :],
                  d["s_Vv"].ap()[rows, hd * 128:(hd + 1) * 128].rearrange("(j k) d -> k j d", k=128), ["s_Vv"], [v_], chan=v_.key)
        for i in range(8):
            qs = slice(i * 512, (i + 1) * 512)
            ot = otps.next(); rs = rsps.next()
            nk = 4 * i + 4

            def emit_st(j):
                ks = slice(j * 128, (j + 1) * 128)
                st = stps.next()
                h.mm(st.ap, kn.ap[:, ks], q_.ap[:, qs], True, False, [kn, q_], [st])
                h.mm(st.ap, akr.ap[:, ks], qr.ap[:, qs], False, True, [akr, qr], [st])
                return st
            st_next = emit_st(0)
            for j in range(nk):
                st = st_next
                if j + 1 < nk:
                    st_next = emit_st(j + 1)
                pt = pt_.next()
                h.act(pt.ap, st.ap, AF.Exp, [st], [pt], scale=SCALE)
                if j >= 4 * i:
                    m = j - 4 * i
                    h.tt("dve", pt.ap, pt.ap, cmask.ap[:, m, :], ALU.mult, [pt, cmask], [pt])
                h.mm(ot.ap, v_.ap[:, j, :], pt.ap, j == 0, j == nk - 1, [v_, pt], [ot])
                h.mm(rs.ap, onesb.ap, pt.ap, j == 0, j == nk - 1, [onesb, pt], [rs])
                yield
            ri = rin.next(); o = oo.next()
            h.recip(ri.ap, rs.ap, [rs], [ri])
            h.tt("dve", o.ap, ot.ap, ri.ap, ALU.mult, [ot, ri], [o])
            h.dma("pool", d["s_ycT"].ap()[hd * 128:(hd + 1) * 128, qs], o.ap, [o], ["s_ycT"], chan=o.key)

    def chain(gens):
        for g_ in gens:
            yield from g_
    Hs = chain([hgrn_head(hd) for hd in range(4)])
    Ts = chain([attn_head(hd) for hd in range(8)])
    alive_h = alive_t = True
    while alive_h or alive_t:
        if alive_t:
            try:
                next(Ts)
            except StopIteration:
                alive_t = False
        if alive_h:
            try:
                next(Hs)
            except StopIteration:
                alive_h = False


NT = FT // 128
ALPHA = (2 * 4) ** 0.25
SBK = 3
PADG = SBK * 128
NSB = (2 * FT) // PADG + 32
NBLK = NSB * SBK


def emit_C(P, h, d, l, last):
    x_src = d["x"] if l == 0 else d["xres_s"]
    x_dst = d["xo"] if last else d["xres_s"]
    wr = Buf(P.sbuf("wr", [128, 16, 36], F32), "wr")
    br = Buf(P.sbuf("br", [128, 36], F32), "br")
    ident = Buf(P.sbuf("identf", [128, 128], F32), "identf")
    identb = Buf(P.sbuf("identb", [128, 128], BF16), "identb")
    ustr = Buf(P.sbuf("ustr", [128, 128], BF16), "ustr")
    onesb = Buf(P.sbuf("conesb", [128, 128], BF16), "conesb")
    blkst = Buf(P.sbuf("blkst", [128, NSB], F32), "blkst")
    pq = Buf(P.sbuf("pq", [128, 4], F32), "pq")
    widxf = Buf(P.sbuf("widxf", [128, NSB], F32), "widxf")
    trl = Buf(P.sbuf("trl", [128, NSB], F32), "trl")
    widx = Buf(P.sbuf("widx", [128, 4, NSB], I32), "widx")
    lg = Buf(P.sbuf("lg", [128, 36], F32), "lg")
    rs = Buf(P.sbuf("rs", [128, 64], F32), "rs")
    OHall = Buf(P.sbuf("OHall", [128, NT, 2, 32], F32), "OHall")
    gates = Buf(P.sbuf("gates", [128, NT, 2], F32), "gates")
    cntb = Buf(P.sbuf("cntb", [128, NT, 32], BF16), "cntb")
    stat = Rot(P, "cstat", 2, [128, 2], F32)
    tot = Buf(P.sbuf("tot", [128, 32], F32), "tot")
    scA = Buf(P.sbuf("scA", [128, 32], F32), "scA")
    scB = Buf(P.sbuf("scB", [128, 32], F32), "scB")
    padded = Buf(P.sbuf("padded", [128, 32], F32), "padded")
    pstart = Buf(P.sbuf("pstart", [128, 32], F32), "pstart")
    sci = Buf(P.sbuf("sci", [128, 64], I32), "sci")
    bacc = Buf(P.sbuf("bacc", [128, NSB], F32), "bacc")
    posf = Buf(P.sbuf("posf", [128, NT * 2], F32), "posf")
    posi = Buf(P.sbuf("posi_c", [128, NT * 2], I32), "posi_c")
    x1br = Rot(P, "x1br", 2, [128, D], BF16)
    P.stage_reset(keep=True)
    banks = [Buf(P.psum(f"cbank{i}", [128, 512], F32), f"cbank{i}") for i in range(8)]
    for bk in banks:
        P.exclusive.add(bk.key)

    for b_, ap_ in ((wr, d["c_wr"].ap()[l].rearrange("(c p) n -> p c n", p=128)), (br, d["c_br"].ap()[l]), (ident, d["c_ident"].ap()),
                    (identb, d["c_identb"].ap()), (ustr, d["c_ustr"].ap()), (onesb, d["b_onesb"].ap()), (blkst, d["c_blkst"].ap()),
                    (pq, d["c_pq"].ap())):
        h.dma("sp", b_.ap, ap_, [], [b_], chan="const")

    cast_i = [0]

    def cast_eng():
        cast_i[0] += 1
        return ("dve", "pool")[cast_i[0] % 2]

    def layer_norm(z, junk, out_ap, out_writes, lngb):
        st = stat.next()
        h.memset("pool", st.ap, 0.0, [st])
        h.red("dve", st.ap[:, 0:1], z.ap, ALU.add, [z, st], [st])
        h.ts("dve", st.ap[:, 0:1], st.ap[:, 0:1], 1.0 / D, None, ALU.mult, None, [st], [st])
        h.ts("dve", z.ap, z.ap, st.ap[:, 0:1], None, ALU.subtract, None, [z, st], [z])
        h.act(junk.ap, z.ap, AF.Square, [z, st], [junk, st], accum=st.ap[:, 1:2])
        h.ts("dve", st.ap[:, 1:2], st.ap[:, 1:2], 1.0 / D, EPS, ALU.mult, ALU.add, [st], [st])
        h.act(st.ap[:, 1:2], st.ap[:, 1:2], AF.Sqrt, [st], [st])
        h.recip(st.ap[:, 1:2], st.ap[:, 1:2], [st], [st])
        h.stt("dve", junk.ap, z.ap, st.ap[:, 1:2], lngb.ap[:, 0, :], ALU.mult, ALU.mult, [z, st, lngb], [junk])
        h.tt("pool", out_ap, junk.ap, lngb.ap[:, 1, :], ALU.add, [junk, lngb], out_writes)

    lngb1 = Buf(P.sbuf("c_lngb1", [128, 2, D], F32), "c_lngb1")
    h.dma("sp", lngb1.ap, d["c_ln1"].ap()[l], [], [lngb1], chan="const")
    wst = Rot(P, "cwst1", 3, [128, 4, 512], F32)
    wout_bf = Buf(P.sbuf("wout_bf", [128, 16, D], BF16), "wout_bf")
    yTr = Rot(P, "yTr", 2, [128, 16, 128], BF16)
    x1T = Buf(P.sbuf("x1T", [128, 16, 128], F32), "x1T")
    xr = Rot(P, "xr", 3, [128, D], F32)
    zb = Buf(P.sbuf("zb", [128, D], F32), "zb")
    jk = Buf(P.sbuf("jk", [128, D], F32), "jk")
    wout_v = d["c_wout"].ap()[l].rearrange("(c p) n -> p c n", p=128)
    for q in range(4):
        for n in range(4):
            st = wst.next()
            h.dma("sp", st.ap, wout_v[:, q * 4:(q + 1) * 4, n * 512:(n + 1) * 512], [], [st], chan=st.key)
            h.cp(cast_eng(), wout_bf.ap[:, q * 4:(q + 1) * 4, n * 512:(n + 1) * 512], st.ap, [st], [wout_bf])
    fm = lambda t_, rows: t_.ap()[rows, :].rearrange("(c p) t -> p c t", p=128)
    def mix_mm(tt):
        tok = tt * 128
        yt = yTr.next(); xt = xr.next()
        tsl = slice(tok, tok + 128)
        h.dma("sp", yt.ap[:, 0:4, :], fm(d["yaT_s"], slice(0, 512))[:, :, tsl], ["yaT_s"], [yt], chan=yt.key)
        h.dma("sp", yt.ap[:, 4:8, :], fm(d["s_ybT"], slice(0, 512))[:, :, tsl], ["s_ybT"], [yt], chan=yt.key)
        h.dma("sp", yt.ap[:, 8:12, :], fm(d["s_ycT"], slice(0, 512))[:, :, tsl], ["s_ycT"], [yt], chan=yt.key)
        h.dma("sp", yt.ap[:, 12:16, :], fm(d["s_ycT"], slice(512, 1024))[:, :, tsl], ["s_ycT"], [yt], chan=yt.key)
        h.dma("sp", xt.ap, x_src.ap()[tok:tok + 128, :], [("xres", tt)], [xt], chan=xt.key)
        for n in range(4):
            for c in range(16):
                h.mm(banks[n].ap, yt.ap[:, c, :], wout_bf.ap[:, c, n * 512:(n + 1) * 512], c == 0, c == 15, [yt, wout_bf], [banks[n]])
        return xt
    nxt_xt = mix_mm(0)
    for tt in range(NT):
        tok = tt * 128
        xt = nxt_xt
        for n in range(4):
            h.stt("dve", zb.ap[:, n * 512:(n + 1) * 512], xt.ap[:, n * 512:(n + 1) * 512], ALPHA, banks[n].ap, ALU.mult, ALU.add, [xt, banks[n]], [zb])
        if tt + 1 < NT:
            nxt_xt = mix_mm(tt + 1)
        layer_norm(zb, jk, xt.ap, [xt], lngb1)
        h.dma("pool", d["c_x1"].ap()[tok:tok + 128, :], xt.ap, [xt], [("x1_d", tt)], chan="x1st" + xt.key)
        xb = x1br.next()
        h.cp("pool", xb.ap, xt.ap, [xt], [xb])
        h.dma("pool", d["c_x1b"].ap()[tok:tok + 128, :], xb.ap, [xb], [("x1b_d", tt)], chan=xb.key)
        for c4 in range(4):
            pt = banks[4 + c4 % 2]
            for i in range(4):
                c = c4 * 4 + i
                h.tr(pt.ap[:, i * 128:(i + 1) * 128], xt.ap[:, c * 128:(c + 1) * 128], ident.ap, [xt, ident], [pt])
            h.cp("act", x1T.ap[:, c4 * 4:(c4 + 1) * 4, :].rearrange("p c t -> p (c t)"), pt.ap, [pt], [x1T])
        lp = banks[6]
        for c in range(16):
            h.mm(lp.ap[:, 0:36], x1T.ap[:, c, :], wr.ap[:, c, :], c == 0, c == 15, [x1T, wr], [lp])
        h.tt("dve", lg.ap, lp.ap[:, 0:36], br.ap, ALU.add, [lp, br], [lg])
        R = [rs, lg]
        s = rs.ap
        h.memset("pool", s, 0.0, [rs])
        h.red("dve", s[:, 0:1], lg.ap[:, 0:4], ALU.max, R, [rs])
        h.ts("dve", s[:, 4:8], lg.ap[:, 0:4], s[:, 0:1], None, ALU.is_equal, None, R, [rs])
        h.ts("dve", s[:, 8:12], lg.ap[:, 0:4], s[:, 0:1], None, ALU.subtract, None, R, [rs])
        h.act(s[:, 8:12], s[:, 8:12], AF.Exp, [rs], [rs], accum=s[:, 1:2])
        h.recip(s[:, 2:3], s[:, 1:2], [rs], [rs])
        h.ts("dve", s[:, 12:20], lg.ap[:, 4:12], s[:, 4:5], None, ALU.mult, None, R, [rs])
        for g in range(1, 4):
            h.stt("dve", s[:, 12:20], lg.ap[:, 4 + 8 * g:12 + 8 * g], s[:, 4 + g:5 + g], s[:, 12:20], ALU.mult, ALU.add, R, [rs])
        h.red("dve", s[:, 20:21], s[:, 12:20], ALU.max, [rs], [rs])
        h.ts("dve", s[:, 24:32], s[:, 12:20], s[:, 20:21], None, ALU.is_equal, None, [rs], [rs])
        h.stt("dve", s[:, 32:40], s[:, 24:32], -1e30, s[:, 12:20], ALU.mult, ALU.add, [rs], [rs])
        h.red("dve", s[:, 21:22], s[:, 32:40], ALU.max, [rs], [rs])
        h.ts("dve", s[:, 40:48], s[:, 32:40], s[:, 21:22], None, ALU.is_equal, None, [rs], [rs])
        h.tt("dve", s[:, 22:23], s[:, 21:22], s[:, 20:21], ALU.subtract, [rs], [rs])
        h.act(s[:, 22:23], s[:, 22:23], AF.Exp, [rs], [rs])
        h.ts("dve", s[:, 23:24], s[:, 22:23], 1.0, None, ALU.add, None, [rs], [rs])
        h.recip(s[:, 23:24], s[:, 23:24], [rs], [rs])
        h.tt("dve", gates.ap[:, tt, 0:1], s[:, 2:3], s[:, 23:24], ALU.mult, [rs, gates], [gates])
        h.tt("dve", gates.ap[:, tt, 1:2], gates.ap[:, tt, 0:1], s[:, 22:23], ALU.mult, [rs, gates], [gates])
        for g in range(4):
            h.ts("dve", OHall.ap[:, tt, 0, g * 8:(g + 1) * 8], s[:, 24:32], s[:, 4 + g:5 + g], None, ALU.mult, None, [rs, OHall], [OHall])
            h.ts("dve", OHall.ap[:, tt, 1, g * 8:(g + 1) * 8], s[:, 40:48], s[:, 4 + g:5 + g], None, ALU.mult, None, [rs, OHall], [OHall])
        h.tt("dve", cntb.ap[:, tt, :], OHall.ap[:, tt, 0, :], OHall.ap[:, tt, 1, :], ALU.add, [OHall, cntb], [cntb])

    tp = banks[7]
    for t in range(NT):
        h.mm(tp.ap[:, 0:32], onesb.ap, cntb.ap[:, t, :], t == 0, t == NT - 1, [onesb, cntb], [tp])
    h.cp("dve", tot.ap, tp.ap[:, 0:32], [tp], [tot])
    h.ts("dve", scA.ap, tot.ap, float(PADG - 1), None, ALU.add, None, [tot], [scA])
    h.ts("dve", scB.ap, scA.ap, 1.0 / PADG, 0.001, ALU.mult, ALU.add, [scA], [scB])
    h.cp("dve", sci.ap[:, 0:32], scB.ap, [scB], [sci])
    h.cp("dve", scB.ap, sci.ap[:, 0:32], [sci], [scB])
    h.ts("dve", padded.ap, scB.ap, float(PADG), None, ALU.mult, None, [scB], [padded])
    h.tt("dve", padded.ap, padded.ap, scA.ap, ALU.is_gt, [padded, scA], [padded])
    h.tt("dve", scB.ap, scB.ap, padded.ap, ALU.subtract, [scB, padded], [scB])
    h.ts("dve", padded.ap, scB.ap, float(PADG), None, ALU.mult, None, [scB], [padded])
    h.cp("dve", scA.ap, padded.ap, [padded], [scA])
    A_, B_ = scA, scB
    for sh in (1, 2, 4, 8, 16):
        h.cp("dve", B_.ap[:, 0:sh], A_.ap[:, 0:sh], [A_], [B_])
        h.tt("dve", B_.ap[:, sh:32], A_.ap[:, sh:32], A_.ap[:, 0:32 - sh], ALU.add, [A_], [B_])
        A_, B_ = B_, A_
    pend = A_
    h.tt("dve", pstart.ap, pend.ap, padded.ap, ALU.subtract, [pend, padded], [pstart])
    h.memset("pool", bacc.ap, 0.0, [bacc])
    for e in range(32):
        h.stt("dve", bacc.ap, blkst.ap, pend.ap[:, e:e + 1], bacc.ap, ALU.is_ge, ALU.add, [blkst, pend, bacc], [bacc])
    h.ts("dve", bacc.ap, bacc.ap, 31.0, None, ALU.min, None, [bacc], [bacc])
    h.ts("dve", trl.ap, blkst.ap, pend.ap[:, 31:32], 1.0e6, ALU.is_ge, ALU.mult, [blkst, pend], [trl])
    for q in range(4):
        h.ts("dve", widxf.ap, bacc.ap, 512.0, pq.ap[:, q:q + 1], ALU.mult, ALU.add, [bacc, pq], [widxf])
        h.ts("dve", widxf.ap, widxf.ap, float(l * 16384), None, ALU.add, None, [widxf], [widxf])
        h.tt("dve", widxf.ap, widxf.ap, trl.ap, ALU.add, [widxf, trl], [widxf])
        h.cp("dve", widx.ap[:, q, :], widxf.ap, [widxf], [widx])
    for tt in range(NT):
        rp = banks[tt % 2]
        for t in range(tt):
            h.mm(rp.ap[:, 0:32], onesb.ap, cntb.ap[:, t, :], t == 0, False, [onesb, cntb], [rp])
        h.mm(rp.ap[:, 0:32], ustr.ap, cntb.ap[:, tt, :], tt == 0, True, [ustr, cntb], [rp])
        h.tt("dve", B_.ap, rp.ap[:, 0:32], pstart.ap, ALU.add, [rp, pstart], [B_])
        for k in range(2):
            h.tt("dve", padded.ap, B_.ap, OHall.ap[:, tt, k, :], ALU.mult, [B_, OHall], [padded])
            h.red("dve", posf.ap[:, tt * 2 + k:tt * 2 + k + 1], padded.ap, ALU.add, [padded, posf], [posf])
    h.cp("dve", posi.ap, posf.ap, [posf], [posi])
    xs_d = d["c_xs"]; ys_d = d["c_ys"]
    for tt in range(NT):
        xb = x1br.next()
        h.dma("sp", xb.ap, d["c_x1b"].ap()[tt * 128:(tt + 1) * 128, :], [("x1b_d", tt)], [xb], chan=xb.key)
        for k in range(2):
            col = tt * 2 + k
            def scat(e, xb=xb, col=col):
                return e.indirect_dma_start(out=xs_d.ap(), out_offset=bass.IndirectOffsetOnAxis(ap=_r(posi.ap)[:, col:col + 1], axis=0),
                                            in_=_r(xb.ap), in_offset=None)
            P.dma("pool", None, None, reads=keys([xb, posi]), writes=[("xs_d", col)], chan=f"scat{col % 4}", indirect=scat)

    P.stage_reset()
    wsets = []
    for i in range(2):
        wsets.append((Buf(P.sbuf(f"wg{i}", [128, 16, 512], BF16), f"wg{i}"),
                      Buf(P.sbuf(f"wu{i}", [128, 16, 512], BF16), f"wu{i}"),
                      Buf(P.sbuf(f"wd{i}", [128, 4, 2048], BF16), f"wd{i}")))
    xbl = [Buf(P.sbuf(f"xbl{i}", [128, D], BF16), f"xbl{i}") for i in range(2)]
    xsT = [Buf(P.sbuf(f"xsT{i}", [128, 16, 128], BF16), f"xsT{i}") for i in range(2)]
    hsg = Rot(P, "hsg", 2, [128, 512], F32)
    hTb = Rot(P, "hTb", 2, [128, 512], BF16)
    ysb = Rot(P, "ysb", 1, [128, D], F32)
    wst = Rot(P, "cwst3", 6, [128, 4, 512], F32)
    wg_v = d["c_wg"].ap().rearrange("l e (p q c) n -> (l e p q) (c n)", q=4, c=4)
    wu_v = d["c_wu"].ap().rearrange("l e (p q c) n -> (l e p q) (c n)", q=4, c=4)
    wd_v = d["c_wd"].ap().rearrange("l e (p q) n -> (l e p q) n", q=4)
    cast3 = [0]

    def cast3_eng():
        cast3[0] += 1
        return ("dve", "act")[cast3[0] % 2]

    ybk = [banks[0], banks[1]]
    ybi = [0]

    def load_weights(sb):
        wgb, wub, wdb = wsets[sb % 2]
        for wv, wbuf, isdown in ((wg_v, wgb, False), (wu_v, wub, False), (wd_v, wdb, True)):
            for q in range(4):
                st = wst.next()
                def ld(e, st=st, wv=wv, q=q, sb=sb):
                    if not hasattr(P, "_bcv"):
                        r_ = e.alloc_register("bcreg")
                        e.reg_mov(r_, 4 * 32 * 512 - 1)
                        P._bcv = e.snap(r_)
                    return e.indirect_dma_start(out=_r(st.ap).rearrange("p c n -> p (c n)"), out_offset=None, in_=wv,
                                                in_offset=bass.IndirectOffsetOnAxis(ap=_r(widx.ap)[:, q, sb:sb + 1], axis=0),
                                                bounds_check=P._bcv, oob_is_err=False)
                P.dma("pool", None, None, reads=keys([widx]), writes=keys([st]), chan=st.key, indirect=ld)
                if isdown:
                    h.cp(cast3_eng(), wbuf.ap[:, q, :], st.ap.rearrange("p c n -> p (c n)"), [st], [wbuf])
                else:
                    h.cp(cast3_eng(), wbuf.ap[:, q * 4:(q + 1) * 4, :], st.ap, [st], [wbuf])

    def emit_T(b):
        xb = xbl[b % 2]; xt_ = xsT[b % 2]
        h.dma("sp", xb.ap, xs_d.ap()[b * 128:(b + 1) * 128, :], [("xs_d", c_) for c_ in range(NT * 2)], [xb], chan=xb.key)
        for half in range(2):
            pt = banks[6 + half]
            ptb = pt.ap.bitcast(BF16)
            for i in range(8):
                c_ = half * 8 + i
                h.tr(ptb[:, i * 128:(i + 1) * 128], xb.ap.rearrange("s (p c) -> s c p", c=16)[:, c_, :], identb.ap, [xb, identb], [pt])
            h.cp(("act", "dve")[half], xt_.ap[:, half * 8:(half + 1) * 8, :].rearrange("p c t -> p (c t)"), ptb, [pt], [xt_])

    hT_of = {}
    hm_of = {}
    hmr = Rot(P, "hmr", 2, [128, 512], BF16)

    def emit_G(b):
        wgb, wub, wdb = wsets[(b // SBK) % 2]
        xt_ = xsT[b % 2]
        hg = banks[2 + (b % 2) * 2]; hu = banks[3 + (b % 2) * 2]
        for wbuf, bank in ((wgb, hg), (wub, hu)):
            for c_ in range(16):
                h.mm(bank.ap, xt_.ap[:, c_, :], wbuf.ap[:, c_, :], c_ == 0, c_ == 15, [wbuf, xt_], [bank])
        sg = hsg.next(); hm = hmr.next()
        h.act(sg.ap, hg.ap, AF.Silu, [hg], [sg])
        h.tt("dve", hm.ap, sg.ap, hu.ap, ALU.mult, [sg, hu], [hm])
        hm_of[b] = hm

    def emit_HT(b):
        hm = hm_of.pop(b)
        yk = ybk[ybi[0] % 2]; ybi[0] += 1
        ykb = yk.ap.bitcast(BF16)
        for cf in range(4):
            h.tr(ykb[:, cf * 128:(cf + 1) * 128], hm.ap.rearrange("s (f4 cf) -> s cf f4", cf=4)[:, cf, :], identb.ap, [hm, identb], [yk])
        hT = hTb.next()
        h.cp("act", hT.ap, ykb[:, 0:512], [yk], [hT])
        hT_of[b] = hT

    def emit_D(b):
        wgb, wub, wdb = wsets[(b // SBK) % 2]
        hT = hT_of.pop(b)
        yb_ = ysb.next()
        for n in range(4):
            yk = ybk[ybi[0] % 2]; ybi[0] += 1
            for f in range(4):
                h.mm(yk.ap, hT.ap[:, f * 128:(f + 1) * 128], wdb.ap[:, f, n * 512:(n + 1) * 512], f == 0, f == 3, [hT, wdb], [yk])
            h.cp(("act", "dve")[n % 2], yb_.ap[:, n * 512:(n + 1) * 512], yk.ap, [yk], [yb_])
        h.dma("sp", ys_d.ap()[b * 128:(b + 1) * 128, :], yb_.ap, [yb_], [("ys_d", b)], chan=yb_.key)

    load_weights(0)
    load_weights(1)
    emit_T(0); emit_G(0); emit_HT(0); emit_T(1)
    for b in range(NBLK):
        if b + 1 < NBLK:
            emit_G(b + 1)
        emit_D(b)
        if b + 1 < NBLK:
            emit_HT(b + 1)
        if b + 2 < NBLK:
            emit_T(b + 2)
        if b % SBK == SBK - 1:
            sb_next = b // SBK + 2
            if sb_next < NSB:
                load_weights(sb_next)

    P.stage_reset()
    lngb4 = Buf(P.sbuf("c_lngb4", [128, 2, D], F32), "c_lngb4")
    h.dma("sp", lngb4.ap, d["c_ln2"].ap()[l], [], [lngb4], chan="const2")
    y0 = Rot(P, "y0_", 2, [128, D], F32)
    y1 = Rot(P, "y1_", 2, [128, D], F32)
    x1r = Rot(P, "x1r", 2, [128, D], F32)
    jk2 = Buf(P.sbuf("jk2", [128, D], F32), "jk2")
    ob = Rot(P, "ob", 2, [128, D], F32)
    xtb_rot = Rot(P, "c_xtb", 2, [128, 16, 128], BF16)
    ys_reads = [("ys_d", b) for b in range(NBLK)]
    for tt in range(NT):
        tok = tt * 128
        a0 = y0.next(); a1 = y1.next(); xt = x1r.next(); o = ob.next()
        for k, dst in ((0, a0), (1, a1)):
            col = tt * 2 + k
            def gath(e, dst=dst, col=col):
                return e.indirect_dma_start(out=_r(dst.ap), out_offset=None, in_=ys_d.ap(),
                                            in_offset=bass.IndirectOffsetOnAxis(ap=_r(posi.ap)[:, col:col + 1], axis=0))
            P.dma("pool", None, None, reads=keys([posi]) + ys_reads, writes=keys([dst]), chan=dst.key, indirect=gath)
        h.dma("sp", xt.ap, d["c_x1"].ap()[tok:tok + 128, :], [("x1_d", tt)], [xt], chan=xt.key)
        h.act(a0.ap, a0.ap, AF.Copy, [a0, gates], [a0], scale=gates.ap[:, tt, 0:1])
        h.stt("dve", a0.ap, a1.ap, gates.ap[:, tt, 1:2], a0.ap, ALU.mult, ALU.add, [a1, gates, a0], [a0])
        h.stt("dve", a0.ap, xt.ap, ALPHA, a0.ap, ALU.mult, ALU.add, [xt, a0], [a0])
        layer_norm(a0, jk2, o.ap, [o], lngb4)
        h.dma("sp", x_dst.ap()[tok:tok + 128, :], o.ap, [o], [("xres", tt)] if not last else [("xo", tt)], chan=o.key)
        if not last:
            emit_xT(P, h, d, o, tt, ident, xtb_rot, banks[4:8])
    P.arena_base = 0
    P.stage_reset()


import ml_dtypes as _mld

AB_NAMES = ["hqT", "hkT", "hlogf", "hkk", "hv", "hgate", "QT", "KnT", "KrT", "Vv"]
BC_NAMES = ["ybT", "ycT"]


def build_fused(nlayers=4):
    P = Prog(); P.enable_arena(); h = H(P)
    d = declare_dram(P)
    emit_X0(P, h, d)
    P.stage_reset()
    for l in range(nlayers):
        emit_A(P, h, d, l)
        P.stage_reset()
        emit_B(P, h, d, l)
        P.stage_reset()
        emit_C(P, h, d, l, last=(l == nlayers - 1))
    outs = [("xo", tt) for tt in range(FT // 128)]
    P.wait_all("sp", outs)
    P.wait_all("pool", outs)
    return P


def prep_fused_inputs(inp):
    bf = _mld.bfloat16
    Lr = range(4)
    sh = {}
    sh["w_in"] = np.ascontiguousarray(inp["w_in"], dtype=np.float32)
    sh["w_uq"] = np.ascontiguousarray(inp["mla_w_uq"], dtype=np.float32)
    sh["w_ukv"] = np.ascontiguousarray(inp["mla_w_ukv"], dtype=np.float32)
    sh["lngb"] = np.ascontiguousarray(np.broadcast_to(np.stack([inp["sgu_ln_g"], inp["sgu_ln_b"]], 1)[:, None], (4, 128, 2, 512))).astype(np.float32)
    sh["wsT"] = np.ascontiguousarray(inp["sgu_ws"].transpose(0, 3, 1, 2)).astype(np.float32)
    s_ = np.arange(128)
    sh["tri"] = (s_[:, None] <= s_[None, :]).astype(np.float32)
    sh["sgub"] = np.ascontiguousarray(inp["sgu_b"].transpose(0, 2, 1)).astype(np.float32)
    lg = np.asarray(inp["hgrn_lb_logits"], dtype=np.float32)
    sh["lbT"] = np.ascontiguousarray(lg.reshape(4, 4, 128).transpose(2, 0, 1))
    sh["lbB"] = np.ascontiguousarray(np.broadcast_to(lg[None], (128, 4, 512)))
    lm = np.zeros((4, 128, 4), np.float32)
    for l in Lr:
        for j in range(4):
            if 1 <= j <= l:
                lm[l, :, j] = 1.0
    sh["lmask"] = lm
    sh["qng"] = np.ascontiguousarray(inp["mla_qn_g"].reshape(4, 4, 128).transpose(0, 2, 1)).astype(np.float32)
    sh["kvng"] = np.ascontiguousarray(inp["mla_kvn_g"].reshape(4, 4, 128).transpose(0, 2, 1)).astype(np.float32)
    inv = (1.0 / (np.float32(10000.0) ** (np.arange(0, 64, 2, dtype=np.float32) / np.float32(64)))).astype(np.float32)
    rc = np.zeros((64, 2), np.float32)
    rc[:, 0] = np.concatenate([inv, inv]) / np.float32(2 * np.pi)
    rc[:32, 1] = -1.0; rc[32:, 1] = 1.0
    sh["ropec"] = rc
    sh["ones"] = np.ones((128, 128), np.float32)
    s64 = np.arange(64)
    U = (s64[:, None] <= s64[None, :]).astype(np.float32)
    sh["b_ucat"] = np.ascontiguousarray(np.concatenate([U, U - U[:, 31:32]], 1))
    sh["b_lmat"] = (s64[:, None] > s64[None, :]).astype(np.float32)
    k = np.arange(128)[:, None]; q = np.arange(512)[None, :]
    sh["b_cmask"] = np.ascontiguousarray(np.stack([(128 * m + k <= q) for m in range(4)], 1).astype(np.float32).astype(bf))
    sh["b_onesb"] = np.ones((128, 128), bf)
    sh["c_wout"] = np.ascontiguousarray(inp["w_out"], dtype=np.float32)
    sh["c_ln1"] = np.ascontiguousarray(np.broadcast_to(np.stack([inp["ln1_g"], inp["ln1_b"]], 1)[:, None], (4, 128, 2, 2048))).astype(np.float32)
    sh["c_ln2"] = np.ascontiguousarray(np.broadcast_to(np.stack([inp["ln2_g"], inp["ln2_b"]], 1)[:, None], (4, 128, 2, 2048))).astype(np.float32)
    sh["c_wr"] = np.ascontiguousarray(np.concatenate([inp["router_group_w"], inp["router_expert_w"]], 2)).astype(np.float32)
    sh["c_br"] = np.ascontiguousarray(np.broadcast_to(np.concatenate([inp["router_group_b"], inp["router_expert_b"]], 1)[:, None], (4, 128, 36))).astype(np.float32)
    sh["c_wg"] = np.ascontiguousarray(inp["expert_w_gate"], dtype=np.float32)
    sh["c_wu"] = np.ascontiguousarray(inp["expert_w_up"], dtype=np.float32)
    sh["c_wd"] = np.ascontiguousarray(inp["expert_w_down"], dtype=np.float32)
    sh["c_ident"] = np.eye(128, dtype=np.float32)
    sh["c_identb"] = np.eye(128, dtype=np.float32).astype(bf)
    n_ = np.arange(128)
    sh["c_ustr"] = (n_[:, None] < n_[None, :]).astype(np.float32).astype(bf)
    sh["c_blkst"] = np.ascontiguousarray(np.broadcast_to((np.arange(53, dtype=np.float32) * 384)[None], (128, 53)))
    sh["c_pq"] = (np.arange(128, dtype=np.float32)[:, None] * 4 + np.arange(4, dtype=np.float32)[None, :])
    xflat = np.ascontiguousarray(inp["x"], dtype=np.float32).reshape(-1, 2048)
    posflat = np.ascontiguousarray(inp["positions"]).astype(np.int32).reshape(-1)
    ng = np.asarray(inp["hgrn_norm_g"], dtype=np.float32)
    per = []
    sh["b_ng"] = np.ascontiguousarray(np.broadcast_to(ng.reshape(4, 1, 4, 128), (4, 64, 4, 128)))
    for c in range(8):
        b = c // 2
        m = {}
        m["x"] = np.ascontiguousarray(xflat[b * 4096:(b + 1) * 4096])
        m["posr"] = np.ascontiguousarray(np.broadcast_to(posflat[b * 4096:(b + 1) * 4096][None], (64, 4096)))
        per.append(m)
    return sh, per


_FCACHE = {}


def kernel(**inputs):
    inp = {k: np.asarray(v) for k, v in inputs.items()}
    if "nc" not in _FCACHE:
        _FCACHE["nc"] = build_fused(4).build()
    nc = _FCACHE["nc"]
    sh, per = prep_fused_inputs(inp)
    maps = []
    for c in range(8):
        m = dict(sh); m.update(per[c]); maps.append(m)
    res = run_bass_kernel_spmd(nc, maps, core_ids=list(range(8))).results
    out = np.concatenate([np.asarray(res[c]["xo"], dtype=np.float32)[(c % 2) * 2048:(c % 2 + 1) * 2048] for c in range(8)], axis=0)
    return out.reshape(4, 4096, 2048).astype(np.float32)
```
